# Optimizing a Trainium2 kernel written in Bass

```python
import jax, jax.numpy as jnp
from jax import lax
import numpy as np

D_MODEL = 2048
BATCH = 4
SEQ = 4096
DEPTH = 2

N_EVEN = (DEPTH + 1) // 2
N_ODD = DEPTH // 2
N_MEM = 256
ROPE_THETA = 500000.0
ROT_FRAC = 4
EPS = 1e-6
Q_BLOCK = 128

MLA_HEADS = 8
MLA_Q_RANK = 512
MLA_KV_RANK = 256
MLA_NOPE = 128
MLA_ROPE = 64
MLA_V = 128
MOBA_HEADS = 8
MOBA_DH = 128
MOBA_BLOCK = 256
MOBA_TOPK = 3
MOBA_QCHUNK = 32
SWA_HEADS = 16
SWA_KV_HEADS = 2
SWA_DH = 64
SWA_WINDOW = 128
DIL_HEADS = 6
DIL_DH = 128
DIL_PATTERNS = ((128, 1), (512, 4), (2048, 16))
MEM_HEADS = 4
MEM_DH = 128

MEMQ_W = MEM_HEADS * MEM_DH
EVEN_VW = MLA_HEADS * MLA_V + MOBA_HEADS * MOBA_DH + MEMQ_W
EVEN_SPLITS = (MLA_Q_RANK, MLA_KV_RANK, MLA_ROPE, 3 * MOBA_HEADS * MOBA_DH, MEMQ_W, EVEN_VW)
EVEN_IN = sum(EVEN_SPLITS)
ODD_VW = SWA_HEADS * SWA_DH + DIL_HEADS * DIL_DH + MEMQ_W
ODD_SPLITS = ((SWA_HEADS + 2 * SWA_KV_HEADS) * SWA_DH, 3 * DIL_HEADS * DIL_DH, MEMQ_W, ODD_VW)
ODD_IN = sum(ODD_SPLITS)

kernel_name = "hybrid_mla_moba_swa_dilated_memory"


def split_cols(h, sizes):
    return jnp.split(h, [int(i) for i in np.cumsum(sizes)[:-1]], axis=-1)


def rms_norm(x, g):
    xf = x.astype(jnp.float32)
    y = xf * lax.rsqrt(jnp.mean(xf * xf, axis=-1, keepdims=True) + EPS)
    return (y * g.astype(jnp.float32)).astype(x.dtype)


def apply_rope(x, rot_dim):
    S = x.shape[1]
    inv = 1.0 / (ROPE_THETA ** (jnp.arange(0, rot_dim, 2, dtype=jnp.float32) / rot_dim))
    ang = jnp.arange(S, dtype=jnp.float32)[:, None] * inv[None, :]
    c, s = jnp.cos(ang)[None, :, None, :], jnp.sin(ang)[None, :, None, :]
    xr = x[..., :rot_dim].astype(jnp.float32)
    x1, x2 = xr[..., : rot_dim // 2], xr[..., rot_dim // 2:]
    rot = jnp.concatenate([x1 * c - x2 * s, x1 * s + x2 * c], axis=-1).astype(x.dtype)
    return jnp.concatenate([rot, x[..., rot_dim:]], axis=-1)


def causal_attention_qblocks(q, k, v, scale):
    B, H, S, dq = q.shape
    nb = S // Q_BLOCK
    qb = q.reshape(B, H, nb, Q_BLOCK, dq).transpose(2, 0, 1, 3, 4)
    kpos = jnp.arange(S)

    def one(args):
        qi, n = args
        s = jnp.einsum('bhqd,bhkd->bhqk', qi, k).astype(jnp.float32) * scale
        qpos = n * Q_BLOCK + jnp.arange(Q_BLOCK)
        s = jnp.where(kpos[None, :] <= qpos[:, None], s, -jnp.inf)
        p = jax.nn.softmax(s, axis=-1).astype(v.dtype)
        return jnp.einsum('bhqk,bhkd->bhqd', p, v)

    o = lax.map(one, (qb, jnp.arange(nb)))
    return o.transpose(1, 2, 0, 3, 4).reshape(B, H, S, v.shape[-1])


def moba_attention(q, k, v):
    B, H, S, dh = q.shape
    L = MOBA_BLOCK
    nblk = -(-S // L)
    Sp = nblk * L
    pad = ((0, 0), (0, 0), (0, Sp - S), (0, 0))
    kb = jnp.pad(k, pad).reshape(B, H, nblk, L, dh)
    vb = jnp.pad(v, pad).reshape(B, H, nblk, L, dh)
    kmean = jnp.mean(kb.astype(jnp.float32), axis=3)
    topk = min(MOBA_TOPK, nblk - 1)
    scale = dh ** -0.5
    QC = MOBA_QCHUNK
    nc = S // QC
    qc = q.reshape(B, H, nc, QC, dh).transpose(2, 0, 1, 3, 4)
    offs = jnp.arange(L)
    gather = jax.vmap(jax.vmap(lambda t, i: t[i]))

    def one(args):
        qi, c = args
        blk = (c * QC) // L
        qpos = c * QC + jnp.arange(QC)
        k_own = lax.dynamic_index_in_dim(kb, blk, axis=2, keepdims=False)
        v_own = lax.dynamic_index_in_dim(vb, blk, axis=2, keepdims=False)
        s_own = jnp.einsum('bhqd,bhld->bhql', qi, k_own).astype(jnp.float32) * scale
        s_own = jnp.where((blk * L + offs)[None, :] <= qpos[:, None], s_own, -jnp.inf)
        if topk == 0:
            p = jax.nn.softmax(s_own, axis=-1).astype(v.dtype)
            return jnp.einsum('bhql,bhld->bhqd', p, v_own)
        gate = jnp.einsum('bhqd,bhnd->bhqn', qi.astype(jnp.float32), kmean)
        gate = jnp.where(jnp.arange(nblk) < blk, gate, -jnp.inf)
        _, idx = lax.top_k(gate, topk)
        valid = jnp.arange(topk) < blk
        kg = gather(kb, idx)
        vg = gather(vb, idx)
        s_sel = jnp.einsum('bhqd,bhqkld->bhqkl', qi, kg).astype(jnp.float32) * scale
        s_sel = jnp.where(valid[:, None], s_sel, -jnp.inf)
        s_all = jnp.concatenate([s_sel.reshape(B, H, QC, topk * L), s_own], axis=-1)
        p = jax.nn.softmax(s_all, axis=-1).astype(v.dtype)
        p_sel = p[..., : topk * L].reshape(B, H, QC, topk, L)
        return (jnp.einsum('bhqkl,bhqkld->bhqd', p_sel, vg)
                + jnp.einsum('bhql,bhld->bhqd', p[..., topk * L:], v_own))

    o = lax.map(one, (qc, jnp.arange(nc)))
    return o.transpose(1, 2, 0, 3, 4).reshape(B, H, S, dh)


def banded_attention(q, k, v, max_dist, scale):
    B, Hk, G, L, dh = q.shape
    QB = Q_BLOCK
    nb = -(-L // QB)
    Lp = nb * QB
    qp = jnp.pad(q, ((0, 0), (0, 0), (0, 0), (0, Lp - L), (0, 0))).reshape(B, Hk, G, nb, QB, dh)
    kp = jnp.pad(k, ((0, 0), (0, 0), (QB, Lp - L), (0, 0)))
    vp = jnp.pad(v, ((0, 0), (0, 0), (QB, Lp - L), (0, 0)))
    kband = jnp.concatenate([kp[:, :, :Lp].reshape(B, Hk, nb, QB, dh),
                             kp[:, :, QB:].reshape(B, Hk, nb, QB, dh)], axis=3)
    vband = jnp.concatenate([vp[:, :, :Lp].reshape(B, Hk, nb, QB, dh),
                             vp[:, :, QB:].reshape(B, Hk, nb, QB, dh)], axis=3)
    s = jnp.einsum('bkgnqd,bknsd->bkgnqs', qp, kband).astype(jnp.float32) * scale
    dist = QB + jnp.arange(QB)[:, None] - jnp.arange(2 * QB)[None, :]
    kpos = jnp.arange(nb)[:, None] * QB - QB + jnp.arange(2 * QB)[None, :]
    mask = ((dist >= 0) & (dist <= max_dist))[None] & (kpos >= 0)[:, None, :]
    s = jnp.where(mask, s, -jnp.inf)
    m = jnp.max(s, axis=-1, keepdims=True)
    p = jnp.exp(s - m)
    l = jnp.sum(p, axis=-1, keepdims=True)
    o = jnp.einsum('bkgnqs,bknsd->bkgnqd', (p / l).astype(v.dtype), vband)
    lse = (m + jnp.log(l))[..., 0]
    return (o.reshape(B, Hk, G, Lp, dh)[:, :, :, :L], lse.reshape(B, Hk, G, Lp)[:, :, :, :L])


def dilated_attention(q, k, v):
    B, S, H, dh = q.shape
    outs, lses = [], []
    for window, dil in DIL_PATTERNS:
        L = S // dil

        def to_sub(t):
            return t.reshape(B, L, dil, H, dh).transpose(0, 2, 3, 1, 4).reshape(B * dil, H, L, dh)

        o, lse = banded_attention(to_sub(q)[:, :, None], to_sub(k), to_sub(v), window // dil, dh ** -0.5)
        outs.append(o[:, :, 0].reshape(B, dil, H, L, dh).transpose(0, 3, 1, 2, 4).reshape(B, S, H, dh))
        lses.append(lse[:, :, 0].reshape(B, dil, H, L).transpose(0, 3, 1, 2).reshape(B, S, H))
    w = jax.nn.softmax(jnp.stack(lses, axis=0), axis=0)
    return jnp.sum(w[..., None] * jnp.stack(outs, axis=0).astype(jnp.float32), axis=0).astype(q.dtype)


def memory_attention(q_mem, mem, mem_norm_g, w_mem_kv):
    B, S, _ = q_mem.shape
    kv = (rms_norm(mem, mem_norm_g) @ w_mem_kv).reshape(B, -1, 2, MEM_HEADS, MEM_DH)
    q = q_mem.reshape(B, S, MEM_HEADS, MEM_DH)
    s = jnp.einsum('bshd,bmhd->bhsm', q, kv[:, :, 0]).astype(jnp.float32) * MEM_DH ** -0.5
    p = jax.nn.softmax(s, axis=-1).astype(q.dtype)
    return jnp.einsum('bhsm,bmhd->bshd', p, kv[:, :, 1]).reshape(B, S, MEMQ_W)


def even_layer(x, mem, norm_g, w_in, q_norm_g, w_uq, kv_norm_g, w_ukv, mem_norm_g, w_mem_kv, w_out):
    B, S, _ = x.shape
    h = rms_norm(x, norm_g) @ w_in
    c_q, c_kv, k_pe, qkv_b, q_mem, gate = split_cols(h, EVEN_SPLITS)
    q_a = (rms_norm(c_q, q_norm_g) @ w_uq).reshape(B, S, MLA_HEADS, MLA_NOPE + MLA_ROPE)
    q_a = jnp.concatenate([q_a[..., :MLA_NOPE], apply_rope(q_a[..., MLA_NOPE:], MLA_ROPE)], axis=-1)
    kv_a = (rms_norm(c_kv, kv_norm_g) @ w_ukv).reshape(B, S, MLA_HEADS, MLA_NOPE + MLA_V)
    k_pe = apply_rope(k_pe[:, :, None, :], MLA_ROPE)
    k_a = jnp.concatenate([kv_a[..., :MLA_NOPE],
                           jnp.broadcast_to(k_pe, (B, S, MLA_HEADS, MLA_ROPE))], axis=-1)
    v_a = kv_a[..., MLA_NOPE:]
    o_a = causal_attention_qblocks(q_a.transpose(0, 2, 1, 3), k_a.transpose(0, 2, 1, 3),
                                   v_a.transpose(0, 2, 1, 3), (MLA_NOPE + MLA_ROPE) ** -0.5)
    o_a = o_a.transpose(0, 2, 1, 3).reshape(B, S, MLA_HEADS * MLA_V)
    qkv_b = qkv_b.reshape(B, S, 3, MOBA_HEADS, MOBA_DH)
    rot = MOBA_DH // ROT_FRAC
    q_b = apply_rope(qkv_b[:, :, 0], rot).transpose(0, 2, 1, 3)
    k_b = apply_rope(qkv_b[:, :, 1], rot).transpose(0, 2, 1, 3)
    v_b = qkv_b[:, :, 2].transpose(0, 2, 1, 3)
    o_b = moba_attention(q_b, k_b, v_b).transpose(0, 2, 1, 3).reshape(B, S, MOBA_HEADS * MOBA_DH)
    o_m = memory_attention(q_mem, mem, mem_norm_g, w_mem_kv)
    y = jnp.concatenate([o_a, o_b.astype(o_a.dtype), o_m.astype(o_a.dtype)], axis=-1) * jax.nn.silu(gate)
    return x + (y @ w_out).astype(x.dtype)


def odd_layer(x, mem, norm_g, w_in, sinks, mem_norm_g, w_mem_kv, w_out):
    B, S, _ = x.shape
    h = rms_norm(x, norm_g) @ w_in
    qkv_c, qkv_d, q_mem, gate = split_cols(h, ODD_SPLITS)
    q_c, k_c, v_c = split_cols(qkv_c, (SWA_HEADS * SWA_DH, SWA_KV_HEADS * SWA_DH, SWA_KV_HEADS * SWA_DH))
    rot_c = SWA_DH // ROT_FRAC
    G = SWA_HEADS // SWA_KV_HEADS
    q_c = apply_rope(q_c.reshape(B, S, SWA_HEADS, SWA_DH), rot_c)
    k_c = apply_rope(k_c.reshape(B, S, SWA_KV_HEADS, SWA_DH), rot_c)
    v_c = v_c.reshape(B, S, SWA_KV_HEADS, SWA_DH)
    qg = q_c.reshape(B, S, SWA_KV_HEADS, G, SWA_DH).transpose(0, 2, 3, 1, 4)
    o_c, lse_c = banded_attention(qg, k_c.transpose(0, 2, 1, 3), v_c.transpose(0, 2, 1, 3),
                                  SWA_WINDOW - 1, SWA_DH ** -0.5)
    o_c = o_c.transpose(0, 3, 1, 2, 4).reshape(B, S, SWA_HEADS, SWA_DH)
    lse_c = lse_c.transpose(0, 3, 1, 2).reshape(B, S, SWA_HEADS)
    sink_w = jax.nn.sigmoid(lse_c - sinks.astype(jnp.float32))
    o_c = (o_c.astype(jnp.float32) * sink_w[..., None]).astype(x.dtype).reshape(B, S, SWA_HEADS * SWA_DH)
    qkv_d = qkv_d.reshape(B, S, 3, DIL_HEADS, DIL_DH)
    rot_d = DIL_DH // ROT_FRAC
    o_d = dilated_attention(apply_rope(qkv_d[:, :, 0], rot_d), apply_rope(qkv_d[:, :, 1], rot_d),
                            qkv_d[:, :, 2]).reshape(B, S, DIL_HEADS * DIL_DH)
    o_m = memory_attention(q_mem, mem, mem_norm_g, w_mem_kv)
    y = jnp.concatenate([o_c, o_d.astype(o_c.dtype), o_m.astype(o_c.dtype)], axis=-1) * jax.nn.silu(gate).astype(o_c.dtype)
    return x + (y @ w_out).astype(x.dtype)


def setup_inputs(seed: int = 0) -> dict:
    key = jax.random.key(seed)
    ks = jax.random.split(key, 20)

    def dense(k, shape):
        return jax.random.normal(k, shape, jnp.float32) * shape[-2] ** -0.5

    def gain(k, shape):
        return 1.0 + 0.02 * jax.random.normal(k, shape, jnp.float32)

    return {
        "x": jax.random.normal(ks[0], (BATCH, SEQ, D_MODEL), jnp.float32),
        "mem": jax.random.normal(ks[1], (BATCH, N_MEM, D_MODEL), jnp.float32),
        "ev_norm_g": gain(ks[2], (N_EVEN, D_MODEL)),
        "ev_w_in": dense(ks[3], (N_EVEN, D_MODEL, EVEN_IN)),
        "ev_q_norm_g": gain(ks[4], (N_EVEN, MLA_Q_RANK)),
        "ev_w_uq": dense(ks[5], (N_EVEN, MLA_Q_RANK, MLA_HEADS * (MLA_NOPE + MLA_ROPE))),
        "ev_kv_norm_g": gain(ks[6], (N_EVEN, MLA_KV_RANK)),
        "ev_w_ukv": dense(ks[7], (N_EVEN, MLA_KV_RANK, MLA_HEADS * (MLA_NOPE + MLA_V))),
        "ev_mem_norm_g": gain(ks[8], (N_EVEN, D_MODEL)),
        "ev_w_mem_kv": dense(ks[9], (N_EVEN, D_MODEL, 2 * MEMQ_W)),
        "ev_w_out": dense(ks[10], (N_EVEN, EVEN_VW, D_MODEL)),
        "od_norm_g": gain(ks[11], (N_ODD, D_MODEL)),
        "od_w_in": dense(ks[12], (N_ODD, D_MODEL, ODD_IN)),
        "od_sinks": 0.5 * jax.random.normal(ks[13], (N_ODD, SWA_HEADS), jnp.float32),
        "od_mem_norm_g": gain(ks[14], (N_ODD, D_MODEL)),
        "od_w_mem_kv": dense(ks[15], (N_ODD, D_MODEL, 2 * MEMQ_W)),
        "od_w_out": dense(ks[16], (N_ODD, ODD_VW, D_MODEL)),
        "final_norm_g": gain(ks[17], (D_MODEL,)),
    }


def reference(x, mem, ev_norm_g, ev_w_in, ev_q_norm_g, ev_w_uq, ev_kv_norm_g, ev_w_ukv,
              ev_mem_norm_g, ev_w_mem_kv, ev_w_out, od_norm_g, od_w_in, od_sinks,
              od_mem_norm_g, od_w_mem_kv, od_w_out, final_norm_g):
    for layer in range(DEPTH):
        i = layer // 2
        if layer % 2 == 0:
            x = even_layer(x, mem, ev_norm_g[i], ev_w_in[i], ev_q_norm_g[i], ev_w_uq[i],
                           ev_kv_norm_g[i], ev_w_ukv[i], ev_mem_norm_g[i], ev_w_mem_kv[i], ev_w_out[i])
        else:
            x = odd_layer(x, mem, od_norm_g[i], od_w_in[i], od_sinks[i], od_mem_norm_g[i],
                          od_w_mem_kv[i], od_w_out[i])
    return rms_norm(x, final_norm_g)
```

```python
from contextlib import ExitStack

import ml_dtypes
import numpy as np

import concourse.bass as bass
import concourse.mybir as mybir
from concourse.bass_utils import run_bass_kernel_spmd

F32 = mybir.dt.float32
BF16 = mybir.dt.bfloat16
AF = mybir.ActivationFunctionType
ALU = mybir.AluOpType

D = 2048
KC = 16
NMEM = 256
EPS = 1e-6
THETA = 500000.0
NEG = -30000.0

L0_ROWS = {}
_r = 0
for _n, _k in (("kpe", 128), ("bq", 1024), ("bk", 1024), ("bv", 1024), ("mq", 512), ("gate", 2560),
               ("qn", 1024), ("qr", 512), ("kn", 1024), ("va", 1024)):
    L0_ROWS[_n] = _r
    _r += _k
L0_NROWS = _r
L1_ROWS = {}
_r = 0
for _n, _k in (("cq", 1024), ("ck", 128), ("cv", 128), ("dq", 768), ("dk", 768), ("dv", 768), ("mq", 512),
               ("gate", 2304)):
    L1_ROWS[_n] = _r
    _r += _k
L1_NROWS = _r

M_CAUSAL = 0
M_SWA = 4
M_DIL = 9
M_SWA2 = 29
NMASK = 31
T_MLA, T_MOBA, T_SWA = 0, 1, 2


def _rope_tables(S):
    t = np.arange(S, dtype=np.float32)
    tabs = np.zeros((3, 2, 128, S), np.float32)
    tabs[:, 0] = 1.0

    def fill(ti, rot, rows0):
        half = rot // 2
        inv = (1.0 / (np.float32(THETA) ** (np.arange(0, rot, 2, dtype=np.float32) / np.float32(rot)))).astype(np.float32)
        ang = t[None, :] * inv[:, None]
        c, s = np.cos(ang).astype(np.float32), np.sin(ang).astype(np.float32)
        for r0 in rows0:
            tabs[ti, 0, r0:r0 + half] = c
            tabs[ti, 0, r0 + half:r0 + rot] = c
            tabs[ti, 1, r0:r0 + half] = -s
            tabs[ti, 1, r0 + half:r0 + rot] = s

    fill(T_MLA, 64, (0, 64))
    fill(T_MOBA, 32, (0,))
    fill(T_SWA, 16, (0, 64))
    return tabs


def _perms():
    p = np.zeros((3, 128, 128), np.float32)

    def fill(ti, rot, rows0):
        half = rot // 2
        for r0 in rows0:
            for i in range(half):
                p[ti, r0 + i + half, r0 + i] = 1.0
                p[ti, r0 + i, r0 + i + half] = 1.0

    fill(T_MLA, 64, (0, 64))
    fill(T_MOBA, 32, (0,))
    fill(T_SWA, 16, (0, 64))
    return p


def _masks():
    k = np.arange(128)[:, None]
    q = np.arange(512)[None, :]
    m = np.zeros((NMASK, 128, 512), np.float32)
    for j in range(4):
        m[M_CAUSAL + j] = (k + 128 * j <= q)
    for j in range(5):
        d = q - k - 128 * (j - 1)
        m[M_SWA + j] = (d >= 0) & (d <= 127)
    for j in range(20):
        d = q - k - 128 * (j - 16)
        m[M_DIL + j] = (((d >= 0) & (d <= 128)).astype(np.float32)
                        + ((d >= 0) & (d <= 512) & (d % 4 == 0)).astype(np.float32)
                        + ((d >= 0) & (d <= 2048) & (d % 16 == 0)).astype(np.float32))
    q1 = q % 128
    m[M_SWA2 + 0] = (q1 < k)
    m[M_SWA2 + 1] = (k <= q1)
    return m.astype(ml_dtypes.bfloat16)


def _moba_consts(S):
    TT = S // 128
    negm = np.zeros((128, TT, 16), np.float32)
    keepm = np.zeros((128, TT, 16), np.float32)
    for t in range(TT):
        blk = t // 2
        negm[:, t, blk:] = -1e30
        if blk >= 4:
            keepm[:, t, :blk] = 1.0
    return negm, keepm


class Buf:
    __slots__ = ("name", "w", "rs")

    def __init__(self, name):
        self.name = name
        self.w = None
        self.rs = {}


class Prog:
    NDS = 8

    def __init__(self, nc):
        self.nc = nc
        self.es = ExitStack()
        self.eng = {"pe": nc.tensor, "act": nc.scalar, "dve": nc.vector, "pool": nc.gpsimd, "sp": nc.sync}
        self.sem = {k: self.es.enter_context(nc.semaphore("sem_" + k)) for k in self.eng}
        self.cnt = {k: 0 for k in self.eng}
        self.seen = {k: {} for k in self.eng}
        self.dsem = {k: [self.es.enter_context(nc.semaphore(f"dsem_{k}_{i}")) for i in range(self.NDS)]
                     for k in ("sp", "pool")}
        self.dval = {k: [0] * self.NDS for k in self.dsem}
        self.dnext = {k: 0 for k in self.dsem}
        self.nb = 0

    def buf(self, name=None):
        self.nb += 1
        return Buf(name or f"b{self.nb}")

    def _wait(self, e, tok):
        sem, val, key = tok
        if self.seen[e].get(key, 0) >= val:
            return
        self.eng[e].wait_ge(sem, val)
        self.seen[e][key] = val

    def _deps(self, e, r, w):
        toks = []
        for b in r:
            if b.w is not None:
                toks.append(b.w)
        for b in w:
            if b.w is not None:
                toks.append(b.w)
            toks.extend(b.rs.values())
        own = "sem_" + e
        for t in toks:
            if e == "pe" and t[2] == own:
                continue
            self._wait(e, t)

    def _commit(self, tok, r, w):
        for b in w:
            b.w = tok
            b.rs = {}
        for b in r:
            if b in w:
                continue
            b.rs[tok[2]] = tok

    def op(self, e, fn, r=(), w=()):
        self._deps(e, r, w)
        ins = fn(self.eng[e])
        self.cnt[e] += 1
        ins.then_inc(self.sem[e], 1)
        tok = (self.sem[e], self.cnt[e], "sem_" + e)
        self._commit(tok, r, w)

    def dma(self, e, out, in_, r=(), w=()):
        self._deps(e, r, w)
        i = self.dnext[e]
        self.dnext[e] = (i + 1) % self.NDS
        key = f"dsem_{e}_{i}"
        sem = self.dsem[e][i]
        self._wait(e, (sem, self.dval[e][i], key))
        ins = self.eng[e].dma_start(out=out, in_=in_)
        self.dval[e][i] += 16
        ins.then_inc(sem, 16)
        tok = (sem, self.dval[e][i], key)
        self._commit(tok, r, w)

    def barrier(self):
        toks = [(self.sem[k], self.cnt[k], "sem_" + k) for k in self.eng if self.cnt[k] > 0]
        for k in self.dsem:
            for i in range(self.NDS):
                if self.dval[k][i] > 0:
                    toks.append((self.dsem[k][i], self.dval[k][i], f"dsem_{k}_{i}"))
        for e in self.eng:
            for t in toks:
                self._wait(e, t)


class Phase:
    def __init__(self, P, name):
        self.P = P
        self.name = name
        self.es = ExitStack()
        self.n = 0

    def __enter__(self):
        self.P.barrier()
        return self

    def __exit__(self, *a):
        self.P.barrier()
        self.es.close()
        return False

    def sb(self, shape, dt):
        self.n += 1
        t = self.es.enter_context(self.P.nc.sbuf_tensor(f"{self.name}_s{self.n}", list(shape), dt))
        return t, self.P.buf()

    def ps(self, shape, dt):
        self.n += 1
        t = self.es.enter_context(self.P.nc.psum_tensor(f"{self.name}_p{self.n}", list(shape), dt))
        return t, self.P.buf()


def phase_norm(P, C, src_d, gT_name, dst_d, ntok, tag):
    nc = P.nc
    NT = ntok // 128
    GW = min(4, NT)
    with Phase(P, "nrm" + tag) as ph:
        gT, b_g = ph.sb([128, KC], F32)
        ident, b_id = ph.sb([128, 128], BF16)
        P.dma("sp", gT[:], C[gT_name][:, :], w=[b_g])
        P.dma("sp", ident[:], C["ident"][:, :], w=[b_id])
        xt = [ph.sb([128, D], F32) for _ in range(2)]
        junk = ph.sb([128, D], BF16)
        xs = [ph.sb([128, D], BF16) for _ in range(2)]
        ss = [ph.sb([128, 1], F32) for _ in range(2)]
        rs = [ph.sb([128, 1], F32) for _ in range(2)]
        st = [ph.sb([128, KC, 128 * GW], BF16) for _ in range(2)]
        pst = [ph.ps([128, 8, 128], BF16) for _ in range(2)]
        for t in range(NT):
            s = t % 2
            x_t, b_x = xt[s]
            P.dma("sp", x_t[:], src_d[t * 128:(t + 1) * 128, :], w=[b_x])
            P.op("act", lambda e, x_t=x_t, s=s: e.activation(out=junk[0][:], in_=x_t[:], func=AF.Square,
                                                             accum_out=ss[s][0][:]),
                 r=[b_x], w=[junk[1], ss[s][1]])
            P.op("dve", lambda e, s=s: e.tensor_scalar(rs[s][0][:], ss[s][0][:], 1.0 / D, EPS, ALU.mult, ALU.add),
                 r=[ss[s][1]], w=[rs[s][1]])
            P.op("act", lambda e, s=s: e.activation(out=rs[s][0][:], in_=rs[s][0][:], func=AF.Sqrt),
                 r=[rs[s][1]], w=[rs[s][1]])
            P.op("dve", lambda e, s=s: e.reciprocal(rs[s][0][:], rs[s][0][:]), r=[rs[s][1]], w=[rs[s][1]])
            P.op("act", lambda e, x_t=x_t, s=s: e.activation(out=xs[s][0][:], in_=x_t[:], func=AF.Copy,
                                                             scale=rs[s][0][:]),
                 r=[b_x, rs[s][1]], w=[xs[s][1]])
            gi = (t // GW) % 2
            stg, b_st = st[gi]
            off = (t % GW) * 128
            for hh in range(2):
                pt, b_pt = pst[hh]

                def tr(e, pt=pt, s=s, hh=hh):
                    ins = None
                    for c in range(8):
                        ins = e.transpose(pt[:, c, :], xs[s][0][:, (hh * 8 + c) * 128:(hh * 8 + c + 1) * 128],
                                          ident[:])
                    return ins
                P.op("pe", tr, r=[xs[s][1], b_id], w=[b_pt])
                P.op("dve", lambda e, pt=pt, stg=stg, hh=hh, off=off: e.tensor_tensor(
                    stg[:, hh * 8:(hh + 1) * 8, off:off + 128], pt[:],
                    gT[:, hh * 8:(hh + 1) * 8].unsqueeze(2).to_broadcast([128, 8, 128]), ALU.mult),
                    r=[b_pt, b_g], w=[b_st])
            if t % GW == GW - 1:
                t0 = (t // GW) * GW * 128
                P.dma("pool", dst_d.rearrange("(c p) s -> p c s", p=128)[:, :, t0:t0 + 128 * GW], stg[:],
                      r=[b_st], w=[C["_b_" + dst_d.tensor.name]])


def phase_inproj(P, C, xnT_d, w_d, jobs, ntok, tag):
    nc = P.nc
    HALF = min(ntok, 2048)
    NH = ntok // HALF
    GW = min(512, HALF)
    TG = HALF // GW
    b_xsrc = C["_b_" + xnT_d.tensor.name]
    tabs_needed = sorted({j["tab"] for j in jobs if j["kind"] == "rope"})
    with Phase(P, "inp" + tag) as ph:
        xn, b_xn = ph.sb([128, KC, HALF], BF16)
        wf = [ph.sb([128, 4, 512], F32) for _ in range(3)]
        wb = [(ph.sb([128, KC, 512], BF16)[0], [P.buf() for _ in range(4)]) for _ in range(2)]
        units = []
        for job in jobs:
            if units and units[-1][-1]["c0"] + units[-1][-1]["n"] == job["c0"] and \
                    job["c0"] + job["n"] - units[-1][0]["c0"] <= 512:
                units[-1].append(job)
            else:
                units.append([job])
        nwf = 0
        stg = [ph.sb([128, HALF], BF16) for _ in range(2)]
        has_lat = any(j["kind"] == "lat" for j in jobs)
        stg32 = [ph.sb([128, HALF], F32) for _ in range(2)] if has_lat else None
        a32 = [ph.sb([128, GW], BF16) for _ in range(2)]
        t1 = [ph.sb([128, GW], F32) for _ in range(2)]
        t2 = [ph.sb([128, GW], F32) for _ in range(2)]
        tabs = {ti: (ph.sb([128, HALF], F32), ph.sb([128, HALF], F32)) for ti in tabs_needed}
        perm = {ti: ph.sb([128, 128], BF16) for ti in tabs_needed}
        for ti in tabs_needed:
            P.dma("sp", perm[ti][0][:], C["perm"][ti, :, :], w=[perm[ti][1]])
        psA = [ph.ps([128, GW], F32) for _ in range(3)]
        psB = [ph.ps([128, GW], F32) for _ in range(2)]
        na = 0
        nb = 0
        for hf in range(NH):
            h0 = hf * HALF
            P.dma("sp", xn[:], xnT_d.rearrange("(c p) s -> p c s", p=128)[:, :, h0:h0 + HALF], r=[b_xsrc], w=[b_xn])
            for ti in tabs_needed:
                for cs in range(2):
                    P.dma("sp", tabs[ti][cs][0][:], C["rope"][ti, cs, :, h0:h0 + HALF], w=[tabs[ti][cs][1]])
            ji = -1

            def load_unit(ui):
                nonlocal nwf
                unit = units[ui]
                u0 = unit[0]["c0"]
                un = unit[-1]["c0"] + unit[-1]["n"] - u0
                wb_t, b_wbs = wb[ui % 2]
                for q4 in range(4):
                    wf_t, b_wf = wf[nwf % 3]
                    nwf += 1
                    P.dma("sp", wf_t[:, :, :un],
                          w_d.rearrange("(kc p) c -> p kc c", p=128)[:, 4 * q4:4 * q4 + 4, u0:u0 + un], w=[b_wf])
                    if q4 % 2 == 0:
                        P.op("act", lambda e, wf_t=wf_t, wb_t=wb_t, un=un, q4=q4: e.activation(
                            out=wb_t[:, 4 * q4:4 * q4 + 4, :un], in_=wf_t[:, :, :un], func=AF.Copy),
                            r=[b_wf], w=[b_wbs[q4]])
                    else:
                        P.op("dve", lambda e, wf_t=wf_t, wb_t=wb_t, un=un, q4=q4: e.tensor_copy(
                            wb_t[:, 4 * q4:4 * q4 + 4, :un], wf_t[:, :, :un]), r=[b_wf], w=[b_wbs[q4]])
            load_unit(0)
            for ui, unit in enumerate(units):
              u0 = unit[0]["c0"]
              wb_t, b_wbs = wb[ui % 2]
              if ui + 1 < len(units):
                  load_unit(ui + 1)
              for job in unit:
                ji += 1
                s = ji % 2
                n = job["n"]
                c0 = job["c0"]
                wo_ = c0 - u0
                kind = job["kind"]
                if kind == "lat":
                    sg, b_sg = stg32[s]
                else:
                    sg, b_sg = stg[s]
                for tg in range(TG):
                    ps, b_ps = psA[na % 3]
                    na += 1
                    g0 = tg * GW

                    def mm(e, ps=ps, wb_t=wb_t, n=n, g0=g0, wo_=wo_):
                        ins = None
                        for kc in range(KC):
                            ins = e.matmul(ps[:n, :], wb_t[:, kc, wo_:wo_ + n], xn[:, kc, g0:g0 + GW],
                                           start=(kc == 0), stop=(kc == KC - 1))
                        return ins
                    P.op("pe", mm, r=b_wbs + [b_xn], w=[b_ps])
                    dst = sg[:n, g0:g0 + GW]
                    if kind in ("plain", "lat"):
                        P.op("act", lambda e, dst=dst, ps=ps, n=n: e.activation(out=dst, in_=ps[:n, :], func=AF.Copy),
                             r=[b_ps], w=[b_sg])
                    elif kind == "silu":
                        P.op("act", lambda e, dst=dst, ps=ps, n=n: e.activation(out=dst, in_=ps[:n, :], func=AF.Silu),
                             r=[b_ps], w=[b_sg])
                    else:
                        ti = job["tab"]
                        a_t, b_a = a32[nb % 2]
                        t1_t, b_t1 = t1[nb % 2]
                        t2_t, b_t2 = t2[nb % 2]
                        p2, b_p2 = psB[nb % 2]
                        nb += 1
                        ct, b_ct = tabs[ti][0]
                        sn, b_sn = tabs[ti][1]
                        pm, b_pm = perm[ti]
                        P.op("act", lambda e, a_t=a_t, ps=ps, n=n: e.activation(out=a_t[:n, :], in_=ps[:n, :],
                                                                                func=AF.Copy),
                             r=[b_ps], w=[b_a])
                        P.op("pe", lambda e, p2=p2, pm=pm, a_t=a_t, n=n: e.matmul(p2[:n, :], pm[:n, :n], a_t[:n, :],
                                                                                  start=True, stop=True),
                             r=[b_pm, b_a], w=[b_p2])
                        P.op("dve", lambda e, t1_t=t1_t, a_t=a_t, ct=ct, n=n, g0=g0: e.tensor_tensor(
                            t1_t[:n, :], ps[:n, :], ct[:n, g0:g0 + GW], ALU.mult), r=[b_ps, b_a, b_ct], w=[b_t1])
                        P.op("dve", lambda e, t2_t=t2_t, p2=p2, sn=sn, n=n, g0=g0: e.tensor_tensor(
                            t2_t[:n, :], p2[:n, :], sn[:n, g0:g0 + GW], ALU.mult), r=[b_p2, b_sn], w=[b_t2])
                        P.op("dve", lambda e, dst=dst, t1_t=t1_t, t2_t=t2_t, n=n: e.tensor_tensor(
                            dst, t1_t[:n, :], t2_t[:n, :], ALU.add), r=[b_t1, b_t2], w=[b_sg])
                dd = job["dst"]
                P.dma("pool", dd[0:n, h0:h0 + HALF], sg[:n, :], r=[b_sg], w=[C["_b_" + dd.tensor.name]])


def phase_mla2(P, C, lat_d, hT, R, wuq_d, wukv_d, S):
    nc = P.nc
    GW = min(512, S)
    TG = S // GW
    b_lat = C["_b_" + lat_d.tensor.name]
    b_hT = C["_b_" + hT.tensor.name]
    with Phase(P, "mla2") as ph:
        wuq, b_wuq = ph.sb([128, 4, 1536], BF16)
        wukv, b_wukv = ph.sb([128, 2, 2048], BF16)
        wst = [ph.sb([128, 2048], F32) for _ in range(2)]
        gq, b_gq = ph.sb([128, 4], F32)
        gkv, b_gkv = ph.sb([128, 2], F32)
        ones, b_ones = ph.sb([128, 128], BF16)
        pm, b_pm = ph.sb([128, 128], BF16)
        P.dma("sp", gq[:], C["gq"][:, :], w=[b_gq])
        P.dma("sp", gkv[:], C["gkv"][:, :], w=[b_gkv])
        P.dma("sp", ones[:], C["ones"][:, :], w=[b_ones])
        P.dma("sp", pm[:], C["perm"][T_MLA, :, :], w=[b_pm])
        k = 0
        for kc in range(4):
            w_t, b_w = wst[k % 2]
            k += 1
            P.dma("sp", w_t[:, :1536], wuq_d[kc * 128:(kc + 1) * 128, :], w=[b_w])
            P.op("pool", lambda e, w_t=w_t, kc=kc: e.tensor_copy(wuq[:, kc, :], w_t[:, :1536]), r=[b_w], w=[b_wuq])
        for kc in range(2):
            w_t, b_w = wst[k % 2]
            k += 1
            P.dma("sp", w_t[:, :], wukv_d[kc * 128:(kc + 1) * 128, :], w=[b_w])
            P.op("pool", lambda e, w_t=w_t, kc=kc: e.tensor_copy(wukv[:, kc, :], w_t[:, :]), r=[b_w], w=[b_wukv])
        latt = [ph.sb([128, 6, GW], F32) for _ in range(2)]
        sq = [ph.sb([128, 6, GW], BF16) for _ in range(2)]
        rq = [ph.sb([128, GW], F32) for _ in range(2)]
        rkv = [ph.sb([128, GW], F32) for _ in range(2)]
        cn = [ph.sb([128, 6, GW], BF16) for _ in range(2)]
        cs = [(ph.sb([128, GW], F32), ph.sb([128, GW], F32)) for _ in range(2)]
        ost = [ph.sb([128, 28, GW], BF16) for _ in range(2)]
        a32 = [ph.sb([128, GW], BF16) for _ in range(2)]
        t1 = [ph.sb([128, GW], F32) for _ in range(2)]
        t2 = [ph.sb([128, GW], F32) for _ in range(2)]
        pss = [ph.ps([128, GW], F32) for _ in range(2)]
        psA = [ph.ps([128, GW], F32) for _ in range(3)]
        psB = [ph.ps([128, GW], F32) for _ in range(2)]
        na = 0
        nb = 0
        for tg in range(TG):
            s = tg % 2
            g0 = tg * GW
            lt, b_lt = latt[s]
            sq_t, b_sq = sq[s]
            cn_t, b_cn = cn[s]
            os_t, b_os = ost[s]
            P.dma("sp", lt[:], lat_d.rearrange("(c p) s -> p c s", p=128)[:, :, g0:g0 + GW], r=[b_lat], w=[b_lt])
            for cc in range(2):
                P.dma("sp", cs[s][cc][0][:], C["rope"][T_MLA, cc, :, g0:g0 + GW], w=[cs[s][cc][1]])
            P.op("act", lambda e, sq_t=sq_t, lt=lt: e.activation(out=sq_t[:], in_=lt[:], func=AF.Square),
                 r=[b_lt], w=[b_sq])
            for (pi, c_lo, c_hi, rr, nfeat, gg, b_gg) in ((0, 0, 4, rq[s], 512, gq, b_gq),
                                                          (1, 4, 6, rkv[s], 256, gkv, b_gkv)):
                p_s, b_p = pss[pi]

                def mm(e, p_s=p_s, c_lo=c_lo, c_hi=c_hi, sq_t=sq_t):
                    ins = None
                    for c in range(c_lo, c_hi):
                        ins = e.matmul(p_s[:], ones[:], sq_t[:, c, :], start=(c == c_lo), stop=(c == c_hi - 1))
                    return ins
                P.op("pe", mm, r=[b_ones, b_sq], w=[b_p])
                r_t, b_r = rr
                P.op("dve", lambda e, r_t=r_t, p_s=p_s, nfeat=nfeat: e.tensor_scalar(
                    r_t[:], p_s[:], 1.0 / nfeat, EPS, ALU.mult, ALU.add), r=[b_p], w=[b_r])
                P.op("act", lambda e, r_t=r_t: e.activation(out=r_t[:], in_=r_t[:], func=AF.Sqrt),
                     r=[b_r], w=[b_r])
                P.op("dve", lambda e, r_t=r_t: e.reciprocal(r_t[:], r_t[:]), r=[b_r], w=[b_r])
                for c in range(c_lo, c_hi):
                    P.op("dve", lambda e, c=c, c_lo=c_lo, lt=lt, cn_t=cn_t, gg=gg, r_t=r_t: e.scalar_tensor_tensor(
                        cn_t[:, c, :], lt[:, c, :], gg[:, c - c_lo:c - c_lo + 1], r_t[:], ALU.mult, ALU.mult),
                        r=[b_lt, b_gg, b_r], w=[b_cn])
            for oc in range(28):
                ps, b_ps = psA[na % 3]
                na += 1
                if oc < 8:
                    groups = [(0, 128, wuq, b_wuq, 0, 4, oc * 192)]
                elif oc < 12:
                    j = oc - 8
                    groups = [(0, 64, wuq, b_wuq, 0, 4, (2 * j) * 192 + 128),
                              (64, 64, wuq, b_wuq, 0, 4, (2 * j + 1) * 192 + 128)]
                elif oc < 20:
                    groups = [(0, 128, wukv, b_wukv, 4, 2, (oc - 12) * 256)]
                else:
                    groups = [(0, 128, wukv, b_wukv, 4, 2, (oc - 20) * 256 + 128)]

                def mm2(e, ps=ps, groups=groups, cn_t=cn_t):
                    ins = None
                    for (p0, m, wt, _b, cbase, nk, col) in groups:
                        for kc in range(nk):
                            ins = e.matmul(ps[p0:p0 + m, :], wt[:, kc, col:col + m], cn_t[:, cbase + kc, :],
                                           start=(kc == 0), stop=(kc == nk - 1))
                    return ins
                P.op("pe", mm2, r=[groups[0][3], b_cn], w=[b_ps])
                dst = os_t[:, oc, :]
                if not (8 <= oc < 12):
                    P.op("act", lambda e, dst=dst, ps=ps: e.activation(out=dst, in_=ps[:], func=AF.Copy),
                         r=[b_ps], w=[b_os])
                else:
                    a_t, b_a = a32[nb % 2]
                    t1_t, b_t1 = t1[nb % 2]
                    t2_t, b_t2 = t2[nb % 2]
                    p2, b_p2 = psB[nb % 2]
                    nb += 1
                    (ct, b_ct), (sn, b_sn) = cs[s]
                    P.op("act", lambda e, a_t=a_t, ps=ps: e.activation(out=a_t[:], in_=ps[:], func=AF.Copy),
                         r=[b_ps], w=[b_a])
                    P.op("pe", lambda e, p2=p2, a_t=a_t: e.matmul(p2[:], pm[:], a_t[:], start=True, stop=True),
                         r=[b_pm, b_a], w=[b_p2])
                    P.op("dve", lambda e, t1_t=t1_t, ps=ps, ct=ct: e.tensor_tensor(t1_t[:], ps[:], ct[:], ALU.mult),
                         r=[b_ps, b_a, b_ct], w=[b_t1])
                    P.op("dve", lambda e, t2_t=t2_t, p2=p2, sn=sn: e.tensor_tensor(t2_t[:], p2[:], sn[:], ALU.mult),
                         r=[b_p2, b_sn], w=[b_t2])
                    P.op("dve", lambda e, dst=dst, t1_t=t1_t, t2_t=t2_t: e.tensor_tensor(dst, t1_t[:], t2_t[:], ALU.add),
                         r=[b_t1, b_t2], w=[b_os])
            for (name, o0, nch) in (("qn", 0, 8), ("qr", 8, 4), ("kn", 12, 8), ("va", 20, 8)):
                r0 = R[name]
                P.dma("pool", hT[r0:r0 + nch * 128, :].rearrange("(c p) s -> p c s", p=128)[:, :, g0:g0 + GW],
                      os_t[:, o0:o0 + nch, :], r=[b_os], w=[b_hT])


def phase_attn(P, C, layer, hT, R, memkv_d, yT_d, S, sinks_d=None):
    nc = P.nc
    TT = S // 128
    NG = S // 512
    b_hT = C["_b_" + hT.tensor.name]
    b_mem = C["_b_" + memkv_d.tensor.name]
    b_y = C["_b_" + yT_d.tensor.name]
    with Phase(P, f"att{layer}") as ph:
        ones, b_ones = ph.sb([128, 128], BF16)
        ident, b_id = ph.sb([128, 128], BF16)
        P.dma("sp", ones[:], C["ones"][:, :], w=[b_ones])
        P.dma("sp", ident[:], C["ident"][:, :], w=[b_id])
        if layer == 0:
            mlist = list(range(M_CAUSAL, M_CAUSAL + 4))
        else:
            mlist = list(range(M_DIL, M_DIL + 20))
        masks, b_masks = ph.sb([128, len(mlist), 512], BF16)
        P.dma("sp", masks[:], C["masks"][mlist[0]:mlist[0] + len(mlist), :, :].rearrange("m p q -> p m q"),
              w=[b_masks])

        def mask_ap(mi):
            return masks[:, mi - mlist[0], :]
        NBIN = 12
        if layer == 1:
            bmd, b_bmd = ph.sb([128, NBIN, 512], BF16)
            P.op("dve", lambda e: e.tensor_scalar(bmd[:], masks[:, 0:NBIN, :], -1.0, -NEG, ALU.add, ALU.mult),
                 r=[b_masks], w=[b_bmd])
        qA = [ph.sb([128, S], BF16) for _ in range(2)]
        qB = [ph.sb([64, S], BF16) for _ in range(2)]
        kA = [ph.sb([128, S], BF16) for _ in range(2)]
        kB = [ph.sb([64, S], BF16) for _ in range(2)]
        vT = [ph.sb([128, S], BF16) for _ in range(2)]
        V = [ph.sb([128, TT, 128], BF16) for _ in range(2)]
        gt = [ph.sb([128, S], BF16) for _ in range(2)]
        yst = [ph.sb([128, S], BF16) for _ in range(2)]
        NPT = 6
        pT = [ph.sb([128, 512], BF16) for _ in range(NPT)]
        rden = [ph.sb([128, 512], F32) for _ in range(2)]
        o32 = [ph.sb([128, 512], F32) for _ in range(2)]
        st = [ph.ps([128, 512], F32) for _ in range(3)]
        oT = [ph.ps([128, 512], F32) for _ in range(2)]
        den = [ph.ps([128, 512], F32) for _ in range(2)]
        misc = [ph.ps([128, 8, 128], BF16) for _ in range(1)]
        if layer == 0:
            esel, b_esel = ph.sb([16, 16, 128], BF16)
            P.dma("sp", esel[:], C["esel"][:, :, :], w=[b_esel])
            negm, b_negm = ph.sb([128, TT, 16], F32)
            keepm, b_keepm = ph.sb([128, TT, 16], F32)
            P.dma("sp", negm[:], C["negm"][:, :, :], w=[b_negm])
            P.dma("sp", keepm[:], C["keepm"][:, :, :], w=[b_keepm])
            kmean, b_kmean = ph.sb([128, 16], F32)
            kmeanb, b_kmeanb = ph.sb([128, 16], BF16)
            P.op("dve", lambda e: e.memset(kmeanb[:], 0.0), r=[], w=[b_kmeanb])
            NBLK = S // 256
            gpad, b_gpad = ph.sb([128, TT, 16], F32)
            m8, b_m8 = ph.sb([128, TT, 8], F32)
            bias, b_bias = ph.sb([128, TT, 16], BF16)
            biasT, b_biasT = ph.sb([16, S], BF16)
        else:
            esink, b_esink = ph.sb([128, 16], F32)
            P.dma("sp", esink[:], sinks_d.partition_broadcast(128), w=[b_esink])
            P.op("act", lambda e: e.activation(out=esink[:], in_=esink[:], func=AF.Exp), r=[], w=[b_esink])
        cnt = {"h": 0, "kv": 0, "pt": 0, "st": 0, "og": 0, "misc": 0}

        def load_kv(kspecs, vspec, dv):
            s = cnt["kv"] % 2
            cnt["kv"] += 1
            ktiles = []
            for i, (kap, K, bsrc) in enumerate(kspecs):
                t, b = (kA, kB)[i][s]
                nk = kap.shape[1]
                P.dma("sp", t[:K, :nk], kap, r=[bsrc], w=[b])
                ktiles.append((t, b, K))
            vt, b_vt = vT[s]
            vap, bsrc = vspec
            nk = vap.shape[1]
            P.dma("sp", vt[:dv, :nk], vap, r=[bsrc], w=[b_vt])
            v_t, b_v = V[s]
            nkt = nk // 128
            for t0 in range(0, nkt, 8):
                m_t, b_m = misc[0]
                cnt["misc"] += 1
                nn = min(8, nkt - t0)

                def tr(e, m_t=m_t, vt=vt, t0=t0, nn=nn):
                    ins = None
                    for c in range(nn):
                        ins = e.transpose(m_t[:, c, :dv], vt[:dv, (t0 + c) * 128:(t0 + c + 1) * 128], ident[:dv, :dv])
                    return ins
                P.op("pe", tr, r=[b_vt, b_id], w=[b_m])
                P.op("dve", lambda e, m_t=m_t, v_t=v_t, t0=t0, nn=nn: e.tensor_copy(v_t[:, t0:t0 + nn, :dv],
                                                                                    m_t[:, :nn, :dv]),
                     r=[b_m], w=[b_v])
            return ktiles, (v_t, b_v)

        def attn_q(qspecs, ktiles, vtile, dv, scale, plan, gate_ap, y_ap, moba=False, sink_h=None):
            s = cnt["h"] % 2
            cnt["h"] += 1
            qtiles = []
            for i, (qap, K, bsrc) in enumerate(qspecs):
                t, b = (qA, qB)[i][s]
                P.dma("sp", t[:K, :], qap, r=[bsrc], w=[b])
                qtiles.append((t, b, K))
            g_t, b_g = gt[s]
            P.dma("sp", g_t[:dv, :], gate_ap, r=[b_hT], w=[b_g])
            y_t, b_yt = yst[s]
            v_t, b_v = vtile
            if moba:
                k_t, b_k, _ = ktiles[0]
                q_t, b_q, _ = qtiles[0]
                P.op("dve", lambda e: e.tensor_reduce(kmean[:, :NBLK], k_t[:, :].rearrange("p (n l) -> p n l", l=256),
                                                      mybir.AxisListType.X, ALU.add), r=[b_k], w=[b_kmean])
                P.op("dve", lambda e: e.tensor_scalar(kmeanb[:, :NBLK], kmean[:, :NBLK], 1.0 / 256.0, None, ALU.mult),
                     r=[b_kmean], w=[b_kmeanb])
                gp, b_gp = st[cnt["st"] % 3]
                cnt["st"] += 1

                def gm(e):
                    ins = None
                    for t in range(TT):
                        ins = e.matmul(gp[:, t * 16:(t + 1) * 16], q_t[:, t * 128:(t + 1) * 128], kmeanb[:],
                                       start=True, stop=True)
                    return ins
                P.op("pe", gm, r=[b_q, b_kmeanb], w=[b_gp])
                P.op("dve", lambda e: e.tensor_tensor(gpad[:], gp[:, :TT * 16].rearrange("p (t n) -> p t n", n=16),
                                                      negm[:], ALU.add), r=[b_gp, b_negm], w=[b_gpad])
                for t in range(TT):
                    P.op("dve", lambda e, t=t: e.max(out=m8[:, t, :], in_=gpad[:, t, :]), r=[b_gpad],
                         w=[b_m8] if t in (0, TT - 1) else [])
                P.op("dve", lambda e: e.tensor_tensor(bias[:], gpad[:], m8[:, :, 2:3].to_broadcast([128, TT, 16]),
                                                      ALU.is_ge), r=[b_gpad, b_m8], w=[b_bias])
                P.op("dve", lambda e: e.tensor_scalar(bias[:], bias[:], -NEG, NEG, ALU.mult, ALU.add),
                     r=[b_bias], w=[b_bias])
                P.op("dve", lambda e: e.tensor_tensor(bias[:], bias[:], keepm[:], ALU.mult),
                     r=[b_bias, b_keepm], w=[b_bias])
                for t0 in range(0, TT, 8):
                    m_t, b_m = misc[0]
                    cnt["misc"] += 1
                    nn = min(8, TT - t0)

                    def tr(e, m_t=m_t, t0=t0, nn=nn):
                        ins = None
                        for c in range(nn):
                            ins = e.transpose(m_t[:16, c, :], bias[:, t0 + c, :], ident[:])
                        return ins
                    P.op("pe", tr, r=[b_bias, b_id], w=[b_m])
                    P.op("dve", lambda e, m_t=m_t, t0=t0, nn=nn: e.tensor_copy(
                        biasT[:, t0 * 128:(t0 + nn) * 128].rearrange("p (c q) -> p c q", q=128), m_t[:16, :nn, :]),
                        r=[b_m], w=[b_biasT])
            LAG = 3
            items = []
            for g in range(NG):
                tiles = plan(g)
                og = cnt["og"] % 2
                cnt["og"] += 1
                for i, (kt, mi, brow) in enumerate(tiles):
                    items.append((g, og, kt, mi, brow, i == 0, i == len(tiles) - 1))

            def finalize(g, og):
                o_t, b_o = oT[og]
                d_t, b_d = den[og]
                q0 = g * 512
                r_t, b_r = rden[og]
                o3, b_o3 = o32[og]
                if sink_h is not None:
                    P.op("dve", lambda e: e.tensor_scalar(r_t[:dv, :], d_t[:dv, :], esink[:dv, sink_h:sink_h + 1],
                                                          None, ALU.add), r=[b_d, b_esink], w=[b_r])
                    P.op("dve", lambda e: e.reciprocal(r_t[:dv, :], r_t[:dv, :]), r=[b_r], w=[b_r])
                    P.op("dve", lambda e: e.tensor_tensor(o3[:dv, :], o_t[:dv, :], r_t[:dv, :], ALU.mult),
                         r=[b_o, b_r], w=[b_o3])
                else:
                    P.op("act", lambda e: e.activation(out=r_t[:dv, :], in_=d_t[:dv, :], func=AF.Ln),
                         r=[b_d], w=[b_r])
                    P.op("act", lambda e: e.activation(out=r_t[:dv, :], in_=r_t[:dv, :], func=AF.Exp, scale=-1.0),
                         r=[b_r], w=[b_r])
                    P.op("dve", lambda e: e.tensor_tensor(o3[:dv, :], o_t[:dv, :], r_t[:dv, :], ALU.mult),
                         r=[b_o, b_r], w=[b_o3])
                P.op("pool", lambda e: e.tensor_tensor(y_t[:dv, q0:q0 + 512], o3[:dv, :], g_t[:dv, q0:q0 + 512],
                                                       ALU.mult), r=[b_o3, b_g], w=[b_yt])

            def pv(item, p_t, b_p):
                g, og, kt, mi, brow, first, last = item
                o_t, b_o = oT[og]
                d_t, b_d = den[og]

                def f(e):
                    e.matmul(o_t[:dv, :], v_t[:, kt, :dv], p_t[:], start=first, stop=last)
                    return e.matmul(d_t[:dv, :], ones[:, :dv], p_t[:], start=first, stop=last)
                P.op("pe", f, r=[b_v, b_p, b_ones], w=[b_o, b_d])
                if last:
                    finalize(g, og)

            inflight = []
            for item in items:
                g, og, kt, mi, brow, first, last = item
                q0 = g * 512
                s_t, b_s = st[cnt["st"] % 3]
                cnt["st"] += 1

                pe_mask = (layer == 1 and mi is not None and M_DIL <= mi < M_DIL + NBIN)

                def qk(e):
                    ins = None
                    nparts = len(ktiles)
                    for pi in range(nparts):
                        k_t, _, K = ktiles[pi]
                        q_t, _, _ = qtiles[pi]
                        ins = e.matmul(s_t[:], k_t[:K, kt * 128:(kt + 1) * 128], q_t[:K, q0:q0 + 512],
                                       start=(pi == 0), stop=(pi == nparts - 1 and brow is None and not pe_mask))
                    if brow is not None:
                        ins = e.matmul(s_t[:], esel[:, brow, :], biasT[:, q0:q0 + 512], start=False, stop=True)
                    if pe_mask:
                        ins = e.matmul(s_t[:], ident[:, :], bmd[:, mi - M_DIL, :], start=False, stop=True)
                    return ins
                rr = [b for (_, b, _) in ktiles] + [b for (_, b, _) in qtiles]
                if brow is not None:
                    rr += [b_esel, b_biasT]
                if pe_mask:
                    rr += [b_id, b_bmd]
                P.op("pe", qk, r=rr, w=[b_s])
                p_t, b_p = pT[cnt["pt"] % NPT]
                cnt["pt"] += 1
                P.op("act", lambda e: e.activation(out=p_t[:], in_=s_t[:], func=AF.Exp, scale=float(scale)),
                     r=[b_s], w=[b_p])
                if mi is not None and not pe_mask:
                    P.op("dve", lambda e: e.tensor_tensor(p_t[:], p_t[:], mask_ap(mi), ALU.mult),
                         r=[b_p, b_masks], w=[b_p])
                inflight.append((item, p_t, b_p))
                if len(inflight) > LAG:
                    pv(*inflight.pop(0))
            while inflight:
                pv(*inflight.pop(0))
            P.dma("pool", y_ap, y_t[:dv, :], r=[b_yt], w=[b_y])

        def rows(name, r0, n):
            return hT[R[name] + r0:R[name] + r0 + n, :]

        def plan_causal(g):
            return [(kt, (M_CAUSAL + kt - 4 * g) if kt >= 4 * g else None, None) for kt in range(4 * g + 4)]

        def plan_moba(g):
            return [(kt, (M_CAUSAL + kt - 4 * g) if kt >= 4 * g else None, (kt // 2) if kt < 4 * g + 2 else None)
                    for kt in range(4 * g + 4)]

        def plan_mem(g):
            return [(0, None, None), (1, None, None)]

        def plan_swa(g):
            return [(4 * g - 1 + j, M_SWA + j, None) for j in range(5) if 4 * g - 1 + j >= 0]

        def plan_dil(g):
            return [(4 * g - 16 + j, M_DIL + j, None) for j in range(20) if 4 * g - 16 + j >= 0]

        gate0 = R["gate"]
        if layer == 0:
            for h in range(8):
                kt_, vt_ = load_kv([(rows("kn", h * 128, 128), 128, b_hT), (rows("kpe", 0, 64), 64, b_hT)],
                                   (rows("va", h * 128, 128), b_hT), 128)
                attn_q([(rows("qn", h * 128, 128), 128, b_hT), (rows("qr", h * 64, 64), 64, b_hT)], kt_, vt_, 128,
                       192 ** -0.5, plan_causal, hT[gate0 + h * 128:gate0 + (h + 1) * 128, :],
                       yT_d[h * 128:(h + 1) * 128, :])
            for h in range(8):
                kt_, vt_ = load_kv([(rows("bk", h * 128, 128), 128, b_hT)], (rows("bv", h * 128, 128), b_hT), 128)
                c = 8 + h
                attn_q([(rows("bq", h * 128, 128), 128, b_hT)], kt_, vt_, 128, 128 ** -0.5, plan_moba,
                       hT[gate0 + c * 128:gate0 + (c + 1) * 128, :], yT_d[c * 128:(c + 1) * 128, :], moba=True)
            ybase = 16
        else:
            for h in range(6):
                kt_, vt_ = load_kv([(rows("dk", h * 128, 128), 128, b_hT)], (rows("dv", h * 128, 128), b_hT), 128)
                c = 8 + h
                attn_q([(rows("dq", h * 128, 128), 128, b_hT)], kt_, vt_, 128, 128 ** -0.5, plan_dil,
                       hT[gate0 + c * 128:gate0 + (c + 1) * 128, :], yT_d[c * 128:(c + 1) * 128, :])
            ybase = 14
        for h in range(4):
            kt_, vt_ = load_kv([(memkv_d[h * 128:(h + 1) * 128, :], 128, b_mem)],
                               (memkv_d[512 + h * 128:512 + (h + 1) * 128, :], b_mem), 128)
            c = ybase + h
            attn_q([(rows("mq", h * 128, 128), 128, b_hT)], kt_, vt_, 128, 128 ** -0.5, plan_mem,
                   hT[gate0 + c * 128:gate0 + (c + 1) * 128, :], yT_d[c * 128:(c + 1) * 128, :])


def phase_swa(P, C, hT, R, yT_d, S, sinks_d):
    TT = S // 128
    b_hT = C["_b_" + hT.tensor.name]
    b_y = C["_b_" + yT_d.tensor.name]
    gate0 = R["gate"]
    scale = 64 ** -0.5
    with Phase(P, "swa") as ph:
        ones, b_ones = ph.sb([128, 128], BF16)
        ident, b_id = ph.sb([128, 128], BF16)
        P.dma("sp", ones[:], C["ones"][:, :], w=[b_ones])
        P.dma("sp", ident[:], C["ident"][:, :], w=[b_id])
        masks, b_masks = ph.sb([128, 2, 512], BF16)
        P.dma("sp", masks[:], C["masks"][M_SWA2:M_SWA2 + 2, :, :].rearrange("m p q -> p m q"), w=[b_masks])
        esink, b_esink = ph.sb([128, 16], F32)
        P.dma("sp", esink[:], sinks_d.partition_broadcast(128), w=[b_esink])
        P.op("act", lambda e: e.activation(out=esink[:], in_=esink[:], func=AF.Exp), r=[], w=[b_esink])
        bm, b_bm = ph.sb([128, 2, 512], BF16)
        P.op("dve", lambda e: e.tensor_scalar(bm[:], masks[:], -1.0, -NEG, ALU.add, ALU.mult),
             r=[b_masks], w=[b_bm])
        sel1, b_sel1 = ph.sb([16, 64], BF16)
        P.dma("sp", sel1[:], C["esel"][:, 0, 0:64], w=[b_sel1])
        srow, b_srow = ph.sb([16, 4, 128], BF16)
        P.op("dve", lambda e: e.memset(srow[:], 0.0), r=[], w=[b_srow])
        q4 = [ph.sb([64, 4, S], BF16) for _ in range(2)]
        g4 = [ph.sb([64, 4, S], BF16) for _ in range(1)]
        y4 = [ph.sb([64, 4, S], BF16) for _ in range(1)]
        kk = [ph.sb([64, S], BF16) for _ in range(2)]
        vT = [ph.sb([64, S], BF16) for _ in range(2)]
        V = [ph.sb([128, TT, 64], BF16) for _ in range(2)]
        pT = [ph.sb([128, 512], BF16) for _ in range(6)]
        rden = [ph.sb([64, 512], F32) for _ in range(2)]
        o32 = [ph.sb([64, 512], F32) for _ in range(2)]
        st = [ph.ps([128, 512], F32) for _ in range(3)]
        oT = [ph.ps([128, 512], F32) for _ in range(2)]
        den = [ph.ps([128, 512], F32) for _ in range(2)]
        misc = ph.ps([128, 8, 128], BF16)
        n_st = 0
        n_pt = 0
        n_og = 0
        it = 0
        for kvh in range(2):
            k_t, b_k = kk[kvh % 2]
            vt, b_vt = vT[kvh % 2]
            v_t, b_v = V[kvh % 2]
            P.dma("sp", k_t[:], hT[R["ck"] + kvh * 64:R["ck"] + (kvh + 1) * 64, :], r=[b_hT], w=[b_k])
            P.dma("sp", vt[:], hT[R["cv"] + kvh * 64:R["cv"] + (kvh + 1) * 64, :], r=[b_hT], w=[b_vt])
            for t0 in range(0, TT, 8):
                m_t, b_m = misc

                def tr(e, m_t=m_t, vt=vt, t0=t0):
                    ins = None
                    for c in range(8):
                        ins = e.transpose(m_t[:, c, :64], vt[:, (t0 + c) * 128:(t0 + c + 1) * 128], ident[:64, :64])
                    return ins
                P.op("pe", tr, r=[b_vt, b_id], w=[b_m])
                P.op("dve", lambda e, m_t=m_t, v_t=v_t, t0=t0: e.tensor_copy(v_t[:, t0:t0 + 8, :], m_t[:, :8, :64]),
                     r=[b_m], w=[b_v])
            for hg in range(2):
                h0 = kvh * 8 + hg * 4
                q_t, b_q = q4[it % 2]
                g_t, b_g = g4[0]
                y_t, b_yt = y4[0]
                it += 1
                P.dma("sp", q_t[:], hT[R["cq"] + h0 * 64:R["cq"] + (h0 + 4) * 64, :].rearrange("(h p) s -> p h s", p=64),
                      r=[b_hT], w=[b_q])
                P.dma("sp", g_t[:], hT[gate0 + h0 * 64:gate0 + (h0 + 4) * 64, :].rearrange("(h p) s -> p h s", p=64),
                      r=[b_hT], w=[b_g])
                P.op("dve", lambda e: e.tensor_copy(srow[0:1, :, :],
                                                    esink[0:1, h0:h0 + 4].unsqueeze(2).to_broadcast([1, 4, 128])),
                     r=[b_esink], w=[b_srow])
                items = []
                for qt in range(TT):
                    og = n_og % 2
                    n_og += 1
                    tiles = [(qt - 1, 0), (qt, 1)] if qt > 0 else [(qt, 1)]
                    for i, (kt, mi) in enumerate(tiles):
                        items.append((qt, og, kt, mi, i == 0, i == len(tiles) - 1))

                def finalize(qt, og):
                    o_t, b_o = oT[og]
                    d_t, b_d = den[og]
                    r_t, b_r = rden[og]
                    o3, b_o3 = o32[og]
                    P.op("act", lambda e: e.activation(out=r_t[:], in_=d_t[:64, :], func=AF.Ln), r=[b_d], w=[b_r])
                    P.op("act", lambda e: e.activation(out=r_t[:], in_=r_t[:], func=AF.Exp, scale=-1.0),
                         r=[b_r], w=[b_r])
                    P.op("dve", lambda e: e.tensor_tensor(o3[:], o_t[:64, :], r_t[:], ALU.mult), r=[b_o, b_r], w=[b_o3])
                    P.op("pool", lambda e: e.tensor_tensor(
                        y_t[:, :, qt * 128:(qt + 1) * 128], o3[:].rearrange("p (h q) -> p h q", h=4),
                        g_t[:, :, qt * 128:(qt + 1) * 128], ALU.mult), r=[b_o3, b_g], w=[b_yt])

                def pv(item, p_t, b_p):
                    qt, og, kt, mi, first, last = item
                    o_t, b_o = oT[og]
                    d_t, b_d = den[og]

                    def f(e):
                        e.matmul(o_t[:64, :], v_t[:, kt, :], p_t[:], start=first, stop=last)
                        ins = e.matmul(d_t[:64, :], ones[:, :64], p_t[:], start=first, stop=False)
                        if last:
                            ins = e.matmul(d_t[:64, :], sel1[:, :], srow[:].rearrange("p h q -> p (h q)"),
                                           start=False, stop=True)
                        return ins
                    P.op("pe", f, r=[b_v, b_p, b_ones, b_sel1, b_srow], w=[b_o, b_d])
                    if last:
                        finalize(qt, og)

                inflight = []
                for item in items:
                    qt, og, kt, mi, first, last = item
                    s_t, b_s = st[n_st % 3]
                    n_st += 1
                    def qk(e):
                        e.matmul(s_t[:], k_t[:, kt * 128:(kt + 1) * 128], q_t[:, :, qt * 128:(qt + 1) * 128],
                                 start=True, stop=False)
                        return e.matmul(s_t[:], ident[:, :], bm[:, mi, :], start=False, stop=True)
                    P.op("pe", qk, r=[b_k, b_q, b_id, b_bm], w=[b_s])
                    p_t, b_p = pT[n_pt % 6]
                    n_pt += 1
                    P.op("act", lambda e: e.activation(out=p_t[:], in_=s_t[:], func=AF.Exp, scale=float(scale)),
                         r=[b_s], w=[b_p])
                    inflight.append((item, p_t, b_p))
                    if len(inflight) > 3:
                        pv(*inflight.pop(0))
                while inflight:
                    pv(*inflight.pop(0))
                P.dma("pool", yT_d[h0 * 64:(h0 + 4) * 64, :].rearrange("(h p) s -> p h s", p=64), y_t[:],
                      r=[b_yt], w=[b_y])


def phase_out(P, C, layer, yT_d, wout_d, NCH, xin_d, xout_d, gname, xnT_d, S, final):
    nc = P.nc
    NT = S // 128
    GT = 4
    b_y = C["_b_" + yT_d.tensor.name]
    b_xin = C["_b_" + xin_d.tensor.name] if ("_b_" + xin_d.tensor.name) in C else None
    with Phase(P, f"out{layer}") as ph:
        wo, b_wo = ph.sb([128, NCH, D], BF16)
        b_wos = [P.buf() for _ in range(NCH)]
        wst = [ph.sb([128, D], F32) for _ in range(2)]
        for c in range(NCH):
            w_t, b_w = wst[c % 2]
            P.dma("sp", w_t[:], wout_d[c * 128:(c + 1) * 128, :], w=[b_w])
            if c % 2:
                P.op("act", lambda e, w_t=w_t, c=c: e.activation(out=wo[:, c, :], in_=w_t[:], func=AF.Copy),
                     r=[b_w], w=[b_wos[c]])
            else:
                P.op("dve", lambda e, w_t=w_t, c=c: e.tensor_copy(wo[:, c, :], w_t[:]), r=[b_w], w=[b_wos[c]])
        ident, b_id = ph.sb([128, 128], BF16)
        P.dma("sp", ident[:], C["ident"][:, :], w=[b_id])
        if final:
            gfin, b_gf = ph.sb([128, D], F32)
            P.dma("sp", gfin[:], C["gfin"].partition_broadcast(128), w=[b_gf])
        else:
            gT, b_g = ph.sb([128, KC], F32)
            P.dma("sp", gT[:], C[gname][:, :], w=[b_g])
        yt = [ph.sb([128, NCH, 128 * GT], BF16) for _ in range(2)]
        xt = [ph.sb([128, D], F32) for _ in range(2)]
        junk = ph.sb([128, D], BF16)
        ss = [ph.sb([128, 1], F32) for _ in range(2)]
        rs = [ph.sb([128, 1], F32) for _ in range(2)]
        acc = [ph.ps([128, 512], F32) for _ in range(4)]
        if not final:
            xs = [ph.sb([128, D], BF16) for _ in range(2)]
            stg = [ph.sb([128, KC, 128 * GT], BF16) for _ in range(2)]
            pst = [ph.ps([128, 8, 128], BF16) for _ in range(2)]
        else:
            ot = [ph.sb([128, D], F32) for _ in range(2)]
        na = 0
        for t in range(NT):
            s = t % 2
            gi = (t // GT) % 2
            y_t, b_yt = yt[gi]
            if t % GT == 0:
                P.dma("sp", y_t[:], yT_d.rearrange("(c p) s -> p c s", p=128)[:, :, t * 128:(t + GT) * 128],
                      r=[b_y], w=[b_yt])
            off = (t % GT) * 128
            x_t, b_x = xt[s]
            P.dma("sp", x_t[:], xin_d[t * 128:(t + 1) * 128, :], r=[b_xin] if b_xin else [], w=[b_x])
            x1_t, b_x1 = x_t, b_x
            for nq in range(4):
                a_t, b_a = acc[na % 4]
                na += 1

                def mm(e, a_t=a_t, y_t=y_t, off=off, nq=nq):
                    ins = None
                    for c in range(NCH):
                        ins = e.matmul(a_t[:], y_t[:, c, off:off + 128], wo[:, c, nq * 512:(nq + 1) * 512],
                                       start=(c == 0), stop=(c == NCH - 1))
                    return ins
                P.op("pe", mm, r=[b_yt] + b_wos, w=[b_a])
                P.op("dve", lambda e, a_t=a_t, x_t=x_t, x1_t=x1_t, nq=nq: e.tensor_tensor(
                    x1_t[:, nq * 512:(nq + 1) * 512], a_t[:], x_t[:, nq * 512:(nq + 1) * 512], ALU.add),
                    r=[b_a, b_x], w=[b_x1])
            P.op("act", lambda e, x1_t=x1_t, s=s: e.activation(out=junk[0][:], in_=x1_t[:], func=AF.Square,
                                                               accum_out=ss[s][0][:]),
                 r=[b_x1], w=[junk[1], ss[s][1]])
            P.op("dve", lambda e, s=s: e.tensor_scalar(rs[s][0][:], ss[s][0][:], 1.0 / D, EPS, ALU.mult, ALU.add),
                 r=[ss[s][1]], w=[rs[s][1]])
            P.op("act", lambda e, s=s: e.activation(out=rs[s][0][:], in_=rs[s][0][:], func=AF.Sqrt),
                 r=[rs[s][1]], w=[rs[s][1]])
            P.op("dve", lambda e, s=s: e.reciprocal(rs[s][0][:], rs[s][0][:]), r=[rs[s][1]], w=[rs[s][1]])
            if final:
                o_t, b_o = ot[s]
                P.op("dve", lambda e, o_t=o_t, x1_t=x1_t, s=s: e.scalar_tensor_tensor(
                    o_t[:], x1_t[:], rs[s][0][:, 0:1], gfin[:], ALU.mult, ALU.mult),
                    r=[b_x1, rs[s][1], b_gf], w=[b_o])
                P.dma("pool", xout_d[t * 128:(t + 1) * 128, :], o_t[:], r=[b_o], w=[C["_b_out"]])
            else:
                P.dma("pool", xout_d[t * 128:(t + 1) * 128, :], x1_t[:], r=[b_x1], w=[C["_b_" + xout_d.tensor.name]])
                P.op("act", lambda e, x1_t=x1_t, s=s: e.activation(out=xs[s][0][:], in_=x1_t[:], func=AF.Copy,
                                                                  scale=rs[s][0][:]),
                     r=[b_x1, rs[s][1]], w=[xs[s][1]])
                sg, b_sg = stg[gi]
                for hh in range(2):
                    pt, b_pt = pst[hh]

                    def tr(e, pt=pt, s=s, hh=hh):
                        ins = None
                        for c in range(8):
                            ins = e.transpose(pt[:, c, :], xs[s][0][:, (hh * 8 + c) * 128:(hh * 8 + c + 1) * 128],
                                              ident[:])
                        return ins
                    P.op("pe", tr, r=[xs[s][1], b_id], w=[b_pt])
                    P.op("dve", lambda e, pt=pt, sg=sg, hh=hh, off=off: e.tensor_tensor(
                        sg[:, hh * 8:(hh + 1) * 8, off:off + 128], pt[:],
                        gT[:, hh * 8:(hh + 1) * 8].unsqueeze(2).to_broadcast([128, 8, 128]), ALU.mult),
                        r=[b_pt, b_g], w=[b_sg])
                if t % GT == GT - 1:
                    t0 = (t // GT) * GT * 128
                    P.dma("pool", xnT_d.rearrange("(c p) s -> p c s", p=128)[:, :, t0:t0 + 128 * GT], sg[:],
                          r=[b_sg], w=[C["_b_" + xnT_d.tensor.name]])


IN_SPECS = None


def build_program(S, stop_after=None, debug_outs=()):
    nc = bass.Bass("TRN2", target_bir_lowering=False)
    P = Prog(nc)
    C = {}

    def din(name, shape, dt=F32):
        C[name] = nc.dram_tensor(name, list(shape), dt, kind="ExternalInput").ap()
        return C[name]

    def scratch(name, shape, dt):
        kind = "ExternalOutput" if name in debug_outs else "Internal"
        t = nc.dram_tensor(name, list(shape), dt, kind=kind).ap()
        C["_b_" + name] = P.buf(name)
        return t
    x = din("x", [S, D])
    mem = din("mem", [NMEM, D])
    din("g0", [128, KC]); din("g1", [128, KC]); din("gm0", [128, KC]); din("gm1", [128, KC])
    din("gq", [128, 4]); din("gkv", [128, 2]); din("gfin", [D])
    w_in0 = din("w_in0", [D, 6976]); w_in1 = din("w_in1", [D, 6400])
    wuq = din("wuq", [512, 1536]); wukv = din("wukv", [256, 2048])
    wm0 = din("wm0", [D, 1024]); wm1 = din("wm1", [D, 1024])
    wo0 = din("wo0", [2560, D]); wo1 = din("wo1", [2304, D])
    sinks = din("sinks", [16])
    din("ident", [128, 128], BF16); din("ones", [128, 128], BF16); din("esel", [16, 16, 128], BF16)
    din("perm", [3, 128, 128], BF16); din("rope", [3, 2, 128, S]); din("masks", [NMASK, 128, 512], BF16)
    din("negm", [128, S // 128, 16]); din("keepm", [128, S // 128, 16])
    out = nc.dram_tensor("out", [S, D], F32, kind="ExternalOutput").ap()
    C["_b_out"] = P.buf("out")

    xnT = scratch("xnT", [D, S], BF16)
    memnT = scratch("memnT", [D, NMEM], BF16)
    memkv0 = scratch("memkv0", [1024, NMEM], BF16)
    memkv1 = scratch("memkv1", [1024, NMEM], BF16)
    lat = scratch("lat", [768, S], F32)
    hT0 = scratch("hT0", [L0_NROWS, S], BF16)
    hT1 = scratch("hT1", [L1_NROWS, S], BF16)
    yT0 = scratch("yT0", [2560, S], BF16)
    yT1 = scratch("yT1", [2304, S], BF16)
    x1 = scratch("x1", [S, D], F32)

    def jobs_for(hT, R, spec):
        jobs = []
        for (name, c0, ncols, kind, tab, dst) in spec:
            for j in range(0, ncols, 128):
                n = min(128, ncols - j)
                if dst is None:
                    d = hT[R[name] + j:R[name] + j + 128, :] if n == 128 else hT[R[name] + j:R[name] + j + n, :]
                else:
                    d = dst[j:j + n, :]
                jobs.append(dict(c0=c0 + j, n=n, kind=kind, tab=tab, dst=d))
        return jobs

    def done():
        P.barrier()
        return nc

    memjobs = lambda dst: [dict(c0=j, n=128, kind="plain", tab=None, dst=dst[j:j + 128, :]) for j in range(0, 1024, 128)]
    phase_norm(P, C, mem, "gm0", memnT, NMEM, "m0")
    phase_inproj(P, C, memnT, wm0, memjobs(memkv0), NMEM, "m0")
    phase_norm(P, C, mem, "gm1", memnT, NMEM, "m1")
    phase_inproj(P, C, memnT, wm1, memjobs(memkv1), NMEM, "m1")
    phase_norm(P, C, x, "g0", xnT, S, "x0")
    if stop_after == "norm0":
        return done()
    spec0 = [("lat", 0, 768, "lat", None, lat), ("kpe", 768, 64, "rope", T_MLA, None),
             ("bq", 832, 1024, "rope", T_MOBA, None), ("bk", 1856, 1024, "rope", T_MOBA, None),
             ("bv", 2880, 1024, "plain", None, None), ("mq", 3904, 512, "plain", None, None),
             ("gate", 4416, 2560, "silu", None, None)]
    phase_inproj(P, C, xnT, w_in0, jobs_for(hT0, L0_ROWS, spec0), S, "l0")
    phase_mla2(P, C, lat, hT0, L0_ROWS, wuq, wukv, S)
    if stop_after == "inproj0":
        return done()
    phase_attn(P, C, 0, hT0, L0_ROWS, memkv0, yT0, S)
    if stop_after == "attn0":
        return done()
    phase_out(P, C, 0, yT0, wo0, 20, x, x1, "g1", xnT, S, final=False)
    if stop_after == "out0":
        return done()
    spec1 = [("cq", 0, 1024, "rope", T_SWA, None), ("ck", 1024, 128, "rope", T_SWA, None),
             ("cv", 1152, 128, "plain", None, None), ("dq", 1280, 768, "rope", T_MOBA, None),
             ("dk", 2048, 768, "rope", T_MOBA, None), ("dv", 2816, 768, "plain", None, None),
             ("mq", 3584, 512, "plain", None, None), ("gate", 4096, 2304, "silu", None, None)]
    phase_inproj(P, C, xnT, w_in1, jobs_for(hT1, L1_ROWS, spec1), S, "l1")
    if stop_after == "inproj1":
        return done()
    phase_swa(P, C, hT1, L1_ROWS, yT1, S, sinks)
    phase_attn(P, C, 1, hT1, L1_ROWS, memkv1, yT1, S, sinks_d=sinks)
    if stop_after == "attn1":
        return done()
    phase_out(P, C, 1, yT1, wo1, 18, x1, out, None, None, S, final=True)
    return done()


def make_in_maps(inputs, S, nb):
    f = lambda a: np.ascontiguousarray(np.asarray(a, dtype=np.float32))
    gl = lambda g: np.ascontiguousarray(f(g).reshape(-1, 128).T)
    negm, keepm = _moba_consts(S)
    esel = np.zeros((16, 16, 128), np.float32)
    for n in range(16):
        esel[n, n, :] = 1.0
    common = {
        "g0": gl(inputs["ev_norm_g"][0]), "g1": gl(inputs["od_norm_g"][0]),
        "gm0": gl(inputs["ev_mem_norm_g"][0]), "gm1": gl(inputs["od_mem_norm_g"][0]),
        "gq": gl(inputs["ev_q_norm_g"][0]), "gkv": gl(inputs["ev_kv_norm_g"][0]),
        "gfin": f(inputs["final_norm_g"]),
        "w_in0": f(inputs["ev_w_in"][0]), "w_in1": f(inputs["od_w_in"][0]),
        "wuq": f(inputs["ev_w_uq"][0]), "wukv": f(inputs["ev_w_ukv"][0]),
        "wm0": f(inputs["ev_w_mem_kv"][0]), "wm1": f(inputs["od_w_mem_kv"][0]),
        "wo0": f(inputs["ev_w_out"][0]), "wo1": f(inputs["od_w_out"][0]),
        "sinks": f(inputs["od_sinks"][0]),
        "ident": np.eye(128, dtype=np.float32).astype(ml_dtypes.bfloat16),
        "ones": np.ones((128, 128), np.float32).astype(ml_dtypes.bfloat16),
        "esel": esel.astype(ml_dtypes.bfloat16),
        "perm": _perms().astype(ml_dtypes.bfloat16), "rope": _rope_tables(S), "masks": _masks(), "negm": negm, "keepm": keepm,
    }
    maps = []
    for b in range(nb):
        m = dict(common)
        m["x"] = f(inputs["x"][b][:S])
        m["mem"] = f(inputs["mem"][b])
        maps.append(m)
    return maps


def kernel(**inputs):
    B, S = inputs["x"].shape[0], inputs["x"].shape[1]
    nc = build_program(S)
    maps = make_in_maps(inputs, S, B)
    res = run_bass_kernel_spmd(nc, maps, core_ids=list(range(B)))
    return np.stack([np.asarray(res.results[b]["out"], dtype=np.float32) for b in range(B)], axis=0)
```

```python
from contextlib import ExitStack

import ml_dtypes
import numpy as np

import concourse.bass as bass
import concourse.mybir as mybir
from concourse.bass_utils import run_bass_kernel_spmd

F32 = mybir.dt.float32
BF16 = mybir.dt.bfloat16
AF = mybir.ActivationFunctionType
ALU = mybir.AluOpType

D = 2048
KC = 16
NMEM = 256
EPS = 1e-6
THETA = 500000.0
NEG = -30000.0

L0_ROWS = {}
_r = 0
for _n, _k in (("kpe", 128), ("bq", 1024), ("bk", 1024), ("bv", 1024), ("mq", 512), ("gate", 2560),
               ("qn", 1024), ("qr", 512), ("kn", 1024), ("va", 1024)):
    L0_ROWS[_n] = _r
    _r += _k
L0_NROWS = _r
L1_ROWS = {}
_r = 0
for _n, _k in (("cq", 1024), ("ck", 128), ("cv", 128), ("dq", 768), ("dk", 768), ("dv", 768), ("mq", 512),
               ("gate", 2304)):
    L1_ROWS[_n] = _r
    _r += _k
L1_NROWS = _r

M_CAUSAL = 0
M_SWA = 4
M_DIL = 9
M_SWA2 = 29
NMASK = 31
T_MLA, T_MOBA, T_SWA = 0, 1, 2


def _rope_tables(S):
    t = np.arange(S, dtype=np.float32)
    tabs = np.zeros((3, 2, 128, S), np.float32)
    tabs[:, 0] = 1.0

    def fill(ti, rot, rows0):
        half = rot // 2
        inv = (1.0 / (np.float32(THETA) ** (np.arange(0, rot, 2, dtype=np.float32) / np.float32(rot)))).astype(np.float32)
        ang = t[None, :] * inv[:, None]
        c, s = np.cos(ang).astype(np.float32), np.sin(ang).astype(np.float32)
        for r0 in rows0:
            tabs[ti, 0, r0:r0 + half] = c
            tabs[ti, 0, r0 + half:r0 + rot] = c
            tabs[ti, 1, r0:r0 + half] = -s
            tabs[ti, 1, r0 + half:r0 + rot] = s

    fill(T_MLA, 64, (0, 64))
    fill(T_MOBA, 32, (0,))
    fill(T_SWA, 16, (0, 64))
    return tabs


def _perms():
    p = np.zeros((3, 128, 128), np.float32)

    def fill(ti, rot, rows0):
        half = rot // 2
        for r0 in rows0:
            for i in range(half):
                p[ti, r0 + i + half, r0 + i] = 1.0
                p[ti, r0 + i, r0 + i + half] = 1.0

    fill(T_MLA, 64, (0, 64))
    fill(T_MOBA, 32, (0,))
    fill(T_SWA, 16, (0, 64))
    return p


def _masks():
    k = np.arange(128)[:, None]
    q = np.arange(512)[None, :]
    m = np.zeros((NMASK, 128, 512), np.float32)
    for j in range(4):
        m[M_CAUSAL + j] = (k + 128 * j <= q)
    for j in range(5):
        d = q - k - 128 * (j - 1)
        m[M_SWA + j] = (d >= 0) & (d <= 127)
    for j in range(20):
        d = q - k - 128 * (j - 16)
        m[M_DIL + j] = (((d >= 0) & (d <= 128)).astype(np.float32)
                        + ((d >= 0) & (d <= 512) & (d % 4 == 0)).astype(np.float32)
                        + ((d >= 0) & (d <= 2048) & (d % 16 == 0)).astype(np.float32))
    q1 = q % 128
    m[M_SWA2 + 0] = (q1 < k)
    m[M_SWA2 + 1] = (k <= q1)
    return m.astype(ml_dtypes.bfloat16)


def _moba_consts(S):
    TT = S // 128
    negm = np.zeros((128, TT, 16), np.float32)
    keepm = np.zeros((128, TT, 16), np.float32)
    for t in range(TT):
        blk = t // 2
        negm[:, t, blk:] = -1e30
        if blk >= 4:
            keepm[:, t, :blk] = 1.0
    return negm, keepm


class Buf:
    __slots__ = ("name", "w", "rs")

    def __init__(self, name):
        self.name = name
        self.w = None
        self.rs = {}


class Prog:
    NDS = 8

    def __init__(self, nc):
        self.nc = nc
        self.es = ExitStack()
        self.eng = {"pe": nc.tensor, "act": nc.scalar, "dve": nc.vector, "pool": nc.gpsimd, "sp": nc.sync}
        self.sem = {k: self.es.enter_context(nc.semaphore("sem_" + k)) for k in self.eng}
        self.cnt = {k: 0 for k in self.eng}
        self.seen = {k: {} for k in self.eng}
        self.dsem = {k: [self.es.enter_context(nc.semaphore(f"dsem_{k}_{i}")) for i in range(self.NDS)]
                     for k in ("sp", "pool")}
        self.dval = {k: [0] * self.NDS for k in self.dsem}
        self.dnext = {k: 0 for k in self.dsem}
        self.nb = 0

    def buf(self, name=None):
        self.nb += 1
        return Buf(name or f"b{self.nb}")

    def _wait(self, e, tok):
        sem, val, key = tok
        if self.seen[e].get(key, 0) >= val:
            return
        self.eng[e].wait_ge(sem, val)
        self.seen[e][key] = val

    def _deps(self, e, r, w):
        toks = []
        for b in r:
            if b.w is not None:
                toks.append(b.w)
        for b in w:
            if b.w is not None:
                toks.append(b.w)
            toks.extend(b.rs.values())
        own = "sem_" + e
        for t in toks:
            if e == "pe" and t[2] == own:
                continue
            self._wait(e, t)

    def _commit(self, tok, r, w):
        for b in w:
            b.w = tok
            b.rs = {}
        for b in r:
            if b in w:
                continue
            b.rs[tok[2]] = tok

    def op(self, e, fn, r=(), w=()):
        self._deps(e, r, w)
        ins = fn(self.eng[e])
        self.cnt[e] += 1
        ins.then_inc(self.sem[e], 1)
        tok = (self.sem[e], self.cnt[e], "sem_" + e)
        self._commit(tok, r, w)

    def dma(self, e, out, in_, r=(), w=()):
        self._deps(e, r, w)
        i = self.dnext[e]
        self.dnext[e] = (i + 1) % self.NDS
        key = f"dsem_{e}_{i}"
        sem = self.dsem[e][i]
        self._wait(e, (sem, self.dval[e][i], key))
        ins = self.eng[e].dma_start(out=out, in_=in_)
        self.dval[e][i] += 16
        ins.then_inc(sem, 16)
        tok = (sem, self.dval[e][i], key)
        self._commit(tok, r, w)

    def barrier(self):
        toks = [(self.sem[k], self.cnt[k], "sem_" + k) for k in self.eng if self.cnt[k] > 0]
        for k in self.dsem:
            for i in range(self.NDS):
                if self.dval[k][i] > 0:
                    toks.append((self.dsem[k][i], self.dval[k][i], f"dsem_{k}_{i}"))
        for e in self.eng:
            for t in toks:
                self._wait(e, t)


class Phase:
    def __init__(self, P, name):
        self.P = P
        self.name = name
        self.es = ExitStack()
        self.n = 0

    def __enter__(self):
        self.P.barrier()
        return self

    def __exit__(self, *a):
        self.P.barrier()
        self.es.close()
        return False

    def sb(self, shape, dt):
        self.n += 1
        t = self.es.enter_context(self.P.nc.sbuf_tensor(f"{self.name}_s{self.n}", list(shape), dt))
        return t, self.P.buf()

    def ps(self, shape, dt):
        self.n += 1
        t = self.es.enter_context(self.P.nc.psum_tensor(f"{self.name}_p{self.n}", list(shape), dt))
        return t, self.P.buf()


def phase_norm(P, C, src_d, gT_name, dst_d, ntok, tag):
    nc = P.nc
    NT = ntok // 128
    GW = min(4, NT)
    with Phase(P, "nrm" + tag) as ph:
        gT, b_g = ph.sb([128, KC], F32)
        ident, b_id = ph.sb([128, 128], BF16)
        P.dma("sp", gT[:], C[gT_name][:, :], w=[b_g])
        P.dma("sp", ident[:], C["ident"][:, :], w=[b_id])
        xt = [ph.sb([128, D], F32) for _ in range(2)]
        junk = ph.sb([128, D], BF16)
        xs = [ph.sb([128, D], BF16) for _ in range(2)]
        ss = [ph.sb([128, 1], F32) for _ in range(2)]
        rs = [ph.sb([128, 1], F32) for _ in range(2)]
        st = [ph.sb([128, KC, 128 * GW], BF16) for _ in range(2)]
        pst = [ph.ps([128, 8, 128], BF16) for _ in range(2)]
        for t in range(NT):
            s = t % 2
            x_t, b_x = xt[s]
            P.dma("sp", x_t[:], src_d[t * 128:(t + 1) * 128, :], w=[b_x])
            P.op("act", lambda e, x_t=x_t, s=s: e.activation(out=junk[0][:], in_=x_t[:], func=AF.Square,
                                                             accum_out=ss[s][0][:]),
                 r=[b_x], w=[junk[1], ss[s][1]])
            P.op("dve", lambda e, s=s: e.tensor_scalar(rs[s][0][:], ss[s][0][:], 1.0 / D, EPS, ALU.mult, ALU.add),
                 r=[ss[s][1]], w=[rs[s][1]])
            P.op("act", lambda e, s=s: e.activation(out=rs[s][0][:], in_=rs[s][0][:], func=AF.Sqrt),
                 r=[rs[s][1]], w=[rs[s][1]])
            P.op("dve", lambda e, s=s: e.reciprocal(rs[s][0][:], rs[s][0][:]), r=[rs[s][1]], w=[rs[s][1]])
            P.op("act", lambda e, x_t=x_t, s=s: e.activation(out=xs[s][0][:], in_=x_t[:], func=AF.Copy,
                                                             scale=rs[s][0][:]),
                 r=[b_x, rs[s][1]], w=[xs[s][1]])
            gi = (t // GW) % 2
            stg, b_st = st[gi]
            off = (t % GW) * 128
            for hh in range(2):
                pt, b_pt = pst[hh]

                def tr(e, pt=pt, s=s, hh=hh):
                    ins = None
                    for c in range(8):
                        ins = e.transpose(pt[:, c, :], xs[s][0][:, (hh * 8 + c) * 128:(hh * 8 + c + 1) * 128],
                                          ident[:])
                    return ins
                P.op("pe", tr, r=[xs[s][1], b_id], w=[b_pt])
                P.op("dve", lambda e, pt=pt, stg=stg, hh=hh, off=off: e.tensor_tensor(
                    stg[:, hh * 8:(hh + 1) * 8, off:off + 128], pt[:],
                    gT[:, hh * 8:(hh + 1) * 8].unsqueeze(2).to_broadcast([128, 8, 128]), ALU.mult),
                    r=[b_pt, b_g], w=[b_st])
            if t % GW == GW - 1:
                t0 = (t // GW) * GW * 128
                P.dma("pool", dst_d.rearrange("(c p) s -> p c s", p=128)[:, :, t0:t0 + 128 * GW], stg[:],
                      r=[b_st], w=[C["_b_" + dst_d.tensor.name]])


def phase_inproj(P, C, xnT_d, w_d, jobs, ntok, tag):
    nc = P.nc
    HALF = min(ntok, 2048)
    NH = ntok // HALF
    GW = min(512, HALF)
    TG = HALF // GW
    b_xsrc = C["_b_" + xnT_d.tensor.name]
    tabs_needed = sorted({j["tab"] for j in jobs if j["kind"] == "rope"})
    with Phase(P, "inp" + tag) as ph:
        xn, b_xn = ph.sb([128, KC, HALF], BF16)
        wf = [ph.sb([128, 4, 512], F32) for _ in range(3)]
        wb = [(ph.sb([128, KC, 512], BF16)[0], [P.buf() for _ in range(4)]) for _ in range(2)]
        units = []
        for job in jobs:
            if units and units[-1][-1]["c0"] + units[-1][-1]["n"] == job["c0"] and \
                    job["c0"] + job["n"] - units[-1][0]["c0"] <= 512:
                units[-1].append(job)
            else:
                units.append([job])
        nwf = 0
        stg = [ph.sb([128, HALF], BF16) for _ in range(2)]
        has_lat = any(j["kind"] == "lat" for j in jobs)
        stg32 = [ph.sb([128, HALF], F32) for _ in range(2)] if has_lat else None
        a32 = [ph.sb([128, GW], BF16) for _ in range(2)]
        t1 = [ph.sb([128, GW], F32) for _ in range(2)]
        t2 = [ph.sb([128, GW], F32) for _ in range(2)]
        tabs = {ti: (ph.sb([128, HALF], F32), ph.sb([128, HALF], F32)) for ti in tabs_needed}
        perm = {ti: ph.sb([128, 128], BF16) for ti in tabs_needed}
        for ti in tabs_needed:
            P.dma("sp", perm[ti][0][:], C["perm"][ti, :, :], w=[perm[ti][1]])
        psA = [ph.ps([128, GW], F32) for _ in range(5)]
        psB = [ph.ps([128, GW], F32) for _ in range(3)]
        na = 0
        nb = 0
        for hf in range(NH):
            h0 = hf * HALF
            P.dma("sp", xn[:], xnT_d.rearrange("(c p) s -> p c s", p=128)[:, :, h0:h0 + HALF], r=[b_xsrc], w=[b_xn])
            for ti in tabs_needed:
                for cs in range(2):
                    P.dma("sp", tabs[ti][cs][0][:], C["rope"][ti, cs, :, h0:h0 + HALF], w=[tabs[ti][cs][1]])
            ji = -1

            def load_unit(ui):
                nonlocal nwf
                unit = units[ui]
                u0 = unit[0]["c0"]
                un = unit[-1]["c0"] + unit[-1]["n"] - u0
                wb_t, b_wbs = wb[ui % 2]
                for q4 in range(4):
                    wf_t, b_wf = wf[nwf % 3]
                    nwf += 1
                    P.dma("sp", wf_t[:, :, :un],
                          w_d.rearrange("(kc p) c -> p kc c", p=128)[:, 4 * q4:4 * q4 + 4, u0:u0 + un], w=[b_wf])
                    if q4 % 2 == 0:
                        P.op("act", lambda e, wf_t=wf_t, wb_t=wb_t, un=un, q4=q4: e.activation(
                            out=wb_t[:, 4 * q4:4 * q4 + 4, :un], in_=wf_t[:, :, :un], func=AF.Copy),
                            r=[b_wf], w=[b_wbs[q4]])
                    else:
                        P.op("dve", lambda e, wf_t=wf_t, wb_t=wb_t, un=un, q4=q4: e.tensor_copy(
                            wb_t[:, 4 * q4:4 * q4 + 4, :un], wf_t[:, :, :un]), r=[b_wf], w=[b_wbs[q4]])
            load_unit(0)
            for ui, unit in enumerate(units):
              u0 = unit[0]["c0"]
              wb_t, b_wbs = wb[ui % 2]
              if ui + 1 < len(units):
                  load_unit(ui + 1)
              for job in unit:
                ji += 1
                s = ji % 2
                n = job["n"]
                c0 = job["c0"]
                wo_ = c0 - u0
                kind = job["kind"]
                if kind == "lat":
                    sg, b_sg = stg32[s]
                else:
                    sg, b_sg = stg[s]
                for tg in range(TG):
                    ps, b_ps = psA[na % 5]
                    na += 1
                    g0 = tg * GW

                    def mm(e, ps=ps, wb_t=wb_t, n=n, g0=g0, wo_=wo_):
                        ins = None
                        for kc in range(KC):
                            ins = e.matmul(ps[:n, :], wb_t[:, kc, wo_:wo_ + n], xn[:, kc, g0:g0 + GW],
                                           start=(kc == 0), stop=(kc == KC - 1))
                        return ins
                    P.op("pe", mm, r=b_wbs + [b_xn], w=[b_ps])
                    dst = sg[:n, g0:g0 + GW]
                    if kind in ("plain", "lat"):
                        P.op("act", lambda e, dst=dst, ps=ps, n=n: e.activation(out=dst, in_=ps[:n, :], func=AF.Copy),
                             r=[b_ps], w=[b_sg])
                    elif kind == "silu":
                        P.op("act", lambda e, dst=dst, ps=ps, n=n: e.activation(out=dst, in_=ps[:n, :], func=AF.Silu),
                             r=[b_ps], w=[b_sg])
                    else:
                        ti = job["tab"]
                        a_t, b_a = a32[nb % 2]
                        t1_t, b_t1 = t1[nb % 2]
                        t2_t, b_t2 = t2[nb % 2]
                        p2, b_p2 = psB[nb % 3]
                        nb += 1
                        ct, b_ct = tabs[ti][0]
                        sn, b_sn = tabs[ti][1]
                        pm, b_pm = perm[ti]
                        P.op("act", lambda e, a_t=a_t, ps=ps, n=n: e.activation(out=a_t[:n, :], in_=ps[:n, :],
                                                                                func=AF.Copy),
                             r=[b_ps], w=[b_a])
                        P.op("pe", lambda e, p2=p2, pm=pm, a_t=a_t, n=n: e.matmul(p2[:n, :], pm[:n, :n], a_t[:n, :],
                                                                                  start=True, stop=True),
                             r=[b_pm, b_a], w=[b_p2])
                        P.op("dve", lambda e, t1_t=t1_t, a_t=a_t, ct=ct, n=n, g0=g0: e.tensor_tensor(
                            t1_t[:n, :], ps[:n, :], ct[:n, g0:g0 + GW], ALU.mult), r=[b_ps, b_a, b_ct], w=[b_t1])
                        P.op("dve", lambda e, t2_t=t2_t, p2=p2, sn=sn, n=n, g0=g0: e.tensor_tensor(
                            t2_t[:n, :], p2[:n, :], sn[:n, g0:g0 + GW], ALU.mult), r=[b_p2, b_sn], w=[b_t2])
                        P.op("dve", lambda e, dst=dst, t1_t=t1_t, t2_t=t2_t, n=n: e.tensor_tensor(
                            dst, t1_t[:n, :], t2_t[:n, :], ALU.add), r=[b_t1, b_t2], w=[b_sg])
                dd = job["dst"]
                P.dma("pool", dd[0:n, h0:h0 + HALF], sg[:n, :], r=[b_sg], w=[C["_b_" + dd.tensor.name]])


def phase_mla2(P, C, lat_d, hT, R, wuq_d, wukv_d, S):
    nc = P.nc
    GW = min(512, S)
    TG = S // GW
    b_lat = C["_b_" + lat_d.tensor.name]
    b_hT = C["_b_" + hT.tensor.name]
    with Phase(P, "mla2") as ph:
        wuq, b_wuq = ph.sb([128, 4, 1536], BF16)
        wukv, b_wukv = ph.sb([128, 2, 2048], BF16)
        wst = [ph.sb([128, 2048], F32) for _ in range(2)]
        gq, b_gq = ph.sb([128, 4], F32)
        gkv, b_gkv = ph.sb([128, 2], F32)
        ones, b_ones = ph.sb([128, 128], BF16)
        pm, b_pm = ph.sb([128, 128], BF16)
        P.dma("sp", gq[:], C["gq"][:, :], w=[b_gq])
        P.dma("sp", gkv[:], C["gkv"][:, :], w=[b_gkv])
        P.dma("sp", ones[:], C["ones"][:, :], w=[b_ones])
        P.dma("sp", pm[:], C["perm"][T_MLA, :, :], w=[b_pm])
        k = 0
        for kc in range(4):
            w_t, b_w = wst[k % 2]
            k += 1
            P.dma("sp", w_t[:, :1536], wuq_d[kc * 128:(kc + 1) * 128, :], w=[b_w])
            P.op("pool", lambda e, w_t=w_t, kc=kc: e.tensor_copy(wuq[:, kc, :], w_t[:, :1536]), r=[b_w], w=[b_wuq])
        for kc in range(2):
            w_t, b_w = wst[k % 2]
            k += 1
            P.dma("sp", w_t[:, :], wukv_d[kc * 128:(kc + 1) * 128, :], w=[b_w])
            P.op("pool", lambda e, w_t=w_t, kc=kc: e.tensor_copy(wukv[:, kc, :], w_t[:, :]), r=[b_w], w=[b_wukv])
        latt = [ph.sb([128, 6, GW], F32) for _ in range(2)]
        sq = [ph.sb([128, 6, GW], BF16) for _ in range(2)]
        rq = [ph.sb([128, GW], F32) for _ in range(2)]
        rkv = [ph.sb([128, GW], F32) for _ in range(2)]
        cn = [ph.sb([128, 6, GW], BF16) for _ in range(2)]
        cs = [(ph.sb([128, GW], F32), ph.sb([128, GW], F32)) for _ in range(2)]
        ost = [ph.sb([128, 28, GW], BF16) for _ in range(2)]
        a32 = [ph.sb([128, GW], BF16) for _ in range(2)]
        t1 = [ph.sb([128, GW], F32) for _ in range(2)]
        t2 = [ph.sb([128, GW], F32) for _ in range(2)]
        pss = [ph.ps([128, GW], F32) for _ in range(2)]
        psA = [ph.ps([128, GW], F32) for _ in range(3)]
        psB = [ph.ps([128, GW], F32) for _ in range(2)]
        na = 0
        nb = 0
        for tg in range(TG):
            s = tg % 2
            g0 = tg * GW
            lt, b_lt = latt[s]
            sq_t, b_sq = sq[s]
            cn_t, b_cn = cn[s]
            os_t, b_os = ost[s]
            P.dma("sp", lt[:], lat_d.rearrange("(c p) s -> p c s", p=128)[:, :, g0:g0 + GW], r=[b_lat], w=[b_lt])
            for cc in range(2):
                P.dma("sp", cs[s][cc][0][:], C["rope"][T_MLA, cc, :, g0:g0 + GW], w=[cs[s][cc][1]])
            P.op("act", lambda e, sq_t=sq_t, lt=lt: e.activation(out=sq_t[:], in_=lt[:], func=AF.Square),
                 r=[b_lt], w=[b_sq])
            for (pi, c_lo, c_hi, rr, nfeat, gg, b_gg) in ((0, 0, 4, rq[s], 512, gq, b_gq),
                                                          (1, 4, 6, rkv[s], 256, gkv, b_gkv)):
                p_s, b_p = pss[pi]

                def mm(e, p_s=p_s, c_lo=c_lo, c_hi=c_hi, sq_t=sq_t):
                    ins = None
                    for c in range(c_lo, c_hi):
                        ins = e.matmul(p_s[:], ones[:], sq_t[:, c, :], start=(c == c_lo), stop=(c == c_hi - 1))
                    return ins
                P.op("pe", mm, r=[b_ones, b_sq], w=[b_p])
                r_t, b_r = rr
                P.op("dve", lambda e, r_t=r_t, p_s=p_s, nfeat=nfeat: e.tensor_scalar(
                    r_t[:], p_s[:], 1.0 / nfeat, EPS, ALU.mult, ALU.add), r=[b_p], w=[b_r])
                P.op("act", lambda e, r_t=r_t: e.activation(out=r_t[:], in_=r_t[:], func=AF.Sqrt),
                     r=[b_r], w=[b_r])
                P.op("dve", lambda e, r_t=r_t: e.reciprocal(r_t[:], r_t[:]), r=[b_r], w=[b_r])
                for c in range(c_lo, c_hi):
                    P.op("dve", lambda e, c=c, c_lo=c_lo, lt=lt, cn_t=cn_t, gg=gg, r_t=r_t: e.scalar_tensor_tensor(
                        cn_t[:, c, :], lt[:, c, :], gg[:, c - c_lo:c - c_lo + 1], r_t[:], ALU.mult, ALU.mult),
                        r=[b_lt, b_gg, b_r], w=[b_cn])
            for oc in range(28):
                ps, b_ps = psA[na % 3]
                na += 1
                if oc < 8:
                    groups = [(0, 128, wuq, b_wuq, 0, 4, oc * 192)]
                elif oc < 12:
                    j = oc - 8
                    groups = [(0, 64, wuq, b_wuq, 0, 4, (2 * j) * 192 + 128),
                              (64, 64, wuq, b_wuq, 0, 4, (2 * j + 1) * 192 + 128)]
                elif oc < 20:
                    groups = [(0, 128, wukv, b_wukv, 4, 2, (oc - 12) * 256)]
                else:
                    groups = [(0, 128, wukv, b_wukv, 4, 2, (oc - 20) * 256 + 128)]

                def mm2(e, ps=ps, groups=groups, cn_t=cn_t):
                    ins = None
                    for (p0, m, wt, _b, cbase, nk, col) in groups:
                        for kc in range(nk):
                            ins = e.matmul(ps[p0:p0 + m, :], wt[:, kc, col:col + m], cn_t[:, cbase + kc, :],
                                           start=(kc == 0), stop=(kc == nk - 1))
                    return ins
                P.op("pe", mm2, r=[groups[0][3], b_cn], w=[b_ps])
                dst = os_t[:, oc, :]
                if not (8 <= oc < 12):
                    P.op("act", lambda e, dst=dst, ps=ps: e.activation(out=dst, in_=ps[:], func=AF.Copy),
                         r=[b_ps], w=[b_os])
                else:
                    a_t, b_a = a32[nb % 2]
                    t1_t, b_t1 = t1[nb % 2]
                    t2_t, b_t2 = t2[nb % 2]
                    p2, b_p2 = psB[nb % 2]
                    nb += 1
                    (ct, b_ct), (sn, b_sn) = cs[s]
                    P.op("act", lambda e, a_t=a_t, ps=ps: e.activation(out=a_t[:], in_=ps[:], func=AF.Copy),
                         r=[b_ps], w=[b_a])
                    P.op("pe", lambda e, p2=p2, a_t=a_t: e.matmul(p2[:], pm[:], a_t[:], start=True, stop=True),
                         r=[b_pm, b_a], w=[b_p2])
                    P.op("dve", lambda e, t1_t=t1_t, ps=ps, ct=ct: e.tensor_tensor(t1_t[:], ps[:], ct[:], ALU.mult),
                         r=[b_ps, b_a, b_ct], w=[b_t1])
                    P.op("dve", lambda e, t2_t=t2_t, p2=p2, sn=sn: e.tensor_tensor(t2_t[:], p2[:], sn[:], ALU.mult),
                         r=[b_p2, b_sn], w=[b_t2])
                    P.op("dve", lambda e, dst=dst, t1_t=t1_t, t2_t=t2_t: e.tensor_tensor(dst, t1_t[:], t2_t[:], ALU.add),
                         r=[b_t1, b_t2], w=[b_os])
            for (name, o0, nch) in (("qn", 0, 8), ("qr", 8, 4), ("kn", 12, 8), ("va", 20, 8)):
                r0 = R[name]
                P.dma("pool", hT[r0:r0 + nch * 128, :].rearrange("(c p) s -> p c s", p=128)[:, :, g0:g0 + GW],
                      os_t[:, o0:o0 + nch, :], r=[b_os], w=[b_hT])


def phase_attn(P, C, layer, hT, R, memkv_d, yT_d, S, sinks_d=None):
    nc = P.nc
    TT = S // 128
    NG = S // 512
    b_hT = C["_b_" + hT.tensor.name]
    b_mem = C["_b_" + memkv_d.tensor.name]
    b_y = C["_b_" + yT_d.tensor.name]
    with Phase(P, f"att{layer}") as ph:
        ones, b_ones = ph.sb([128, 128], BF16)
        ident, b_id = ph.sb([128, 128], BF16)
        P.dma("sp", ones[:], C["ones"][:, :], w=[b_ones])
        P.dma("sp", ident[:], C["ident"][:, :], w=[b_id])
        if layer == 0:
            mlist = list(range(M_CAUSAL, M_CAUSAL + 4))
        else:
            mlist = list(range(M_DIL, M_DIL + 20))
        masks, b_masks = ph.sb([128, len(mlist), 512], BF16)
        P.dma("sp", masks[:], C["masks"][mlist[0]:mlist[0] + len(mlist), :, :].rearrange("m p q -> p m q"),
              w=[b_masks])

        def mask_ap(mi):
            return masks[:, mi - mlist[0], :]
        NBIN = 12
        if layer == 1:
            bmd, b_bmd = ph.sb([128, NBIN, 512], BF16)
            P.op("dve", lambda e: e.tensor_scalar(bmd[:], masks[:, 0:NBIN, :], -1.0, -NEG, ALU.add, ALU.mult),
                 r=[b_masks], w=[b_bmd])
        qA = [ph.sb([128, S], BF16) for _ in range(2)]
        qB = [ph.sb([64, S], BF16) for _ in range(2)]
        kA = [ph.sb([128, S], BF16) for _ in range(2)]
        kB = [ph.sb([64, S], BF16) for _ in range(2)]
        vT = [ph.sb([128, S], BF16) for _ in range(2)]
        V = [ph.sb([128, TT, 128], BF16) for _ in range(2)]
        gt = [ph.sb([128, S], BF16) for _ in range(2)]
        yst = [ph.sb([128, S], BF16) for _ in range(2)]
        NPT = 5
        pT = [ph.sb([128, 512], BF16) for _ in range(NPT)]
        rden = [ph.sb([128, 512], F32) for _ in range(2)]
        o32 = [ph.sb([128, 512], F32) for _ in range(2)]
        st = [ph.ps([128, 512], F32) for _ in range(3)]
        oT = [ph.ps([128, 512], F32) for _ in range(2)]
        den = [ph.ps([128, 512], F32) for _ in range(2)]
        misc = [ph.ps([128, 8, 128], BF16) for _ in range(1)]
        if layer == 0:
            esel, b_esel = ph.sb([16, 16, 128], BF16)
            P.dma("sp", esel[:], C["esel"][:, :, :], w=[b_esel])
            negm, b_negm = ph.sb([128, TT, 16], F32)
            keepm, b_keepm = ph.sb([128, TT, 16], F32)
            P.dma("sp", negm[:], C["negm"][:, :, :], w=[b_negm])
            P.dma("sp", keepm[:], C["keepm"][:, :, :], w=[b_keepm])
            kmean, b_kmean = ph.sb([128, 16], F32)
            kmeanb, b_kmeanb = ph.sb([128, 16], BF16)
            P.op("dve", lambda e: e.memset(kmeanb[:], 0.0), r=[], w=[b_kmeanb])
            NBLK = S // 256
            gpad, b_gpad = ph.sb([128, TT, 16], F32)
            m8, b_m8 = ph.sb([128, TT, 8], F32)
            bias, b_bias = ph.sb([128, TT, 16], BF16)
            biasT, b_biasT = ph.sb([16, S], BF16)
        else:
            esink, b_esink = ph.sb([128, 16], F32)
            P.dma("sp", esink[:], sinks_d.partition_broadcast(128), w=[b_esink])
            P.op("act", lambda e: e.activation(out=esink[:], in_=esink[:], func=AF.Exp), r=[], w=[b_esink])
        cnt = {"h": 0, "kv": 0, "pt": 0, "st": 0, "og": 0, "misc": 0}

        def load_kv(kspecs, vspec, dv):
            s = cnt["kv"] % 2
            cnt["kv"] += 1
            ktiles = []
            for i, (kap, K, bsrc) in enumerate(kspecs):
                t, b = (kA, kB)[i][s]
                nk = kap.shape[1]
                P.dma("sp", t[:K, :nk], kap, r=[bsrc], w=[b])
                ktiles.append((t, b, K))
            vt, b_vt = vT[s]
            vap, bsrc = vspec
            nk = vap.shape[1]
            P.dma("sp", vt[:dv, :nk], vap, r=[bsrc], w=[b_vt])
            v_t, b_v = V[s]
            nkt = nk // 128
            for t0 in range(0, nkt, 8):
                m_t, b_m = misc[0]
                cnt["misc"] += 1
                nn = min(8, nkt - t0)

                def tr(e, m_t=m_t, vt=vt, t0=t0, nn=nn):
                    ins = None
                    for c in range(nn):
                        ins = e.transpose(m_t[:, c, :dv], vt[:dv, (t0 + c) * 128:(t0 + c + 1) * 128], ident[:dv, :dv])
                    return ins
                P.op("pe", tr, r=[b_vt, b_id], w=[b_m])
                P.op("dve", lambda e, m_t=m_t, v_t=v_t, t0=t0, nn=nn: e.tensor_copy(v_t[:, t0:t0 + nn, :dv],
                                                                                    m_t[:, :nn, :dv]),
                     r=[b_m], w=[b_v])
            return ktiles, (v_t, b_v)

        def attn_q(qspecs, ktiles, vtile, dv, scale, plan, gate_ap, y_ap, moba=False, sink_h=None):
            s = cnt["h"] % 2
            cnt["h"] += 1
            qtiles = []
            for i, (qap, K, bsrc) in enumerate(qspecs):
                t, b = (qA, qB)[i][s]
                P.dma("sp", t[:K, :], qap, r=[bsrc], w=[b])
                qtiles.append((t, b, K))
            g_t, b_g = gt[s]
            P.dma("sp", g_t[:dv, :], gate_ap, r=[b_hT], w=[b_g])
            y_t, b_yt = yst[s]
            v_t, b_v = vtile
            if moba:
                k_t, b_k, _ = ktiles[0]
                q_t, b_q, _ = qtiles[0]
                P.op("dve", lambda e: e.tensor_reduce(kmean[:, :NBLK], k_t[:, :].rearrange("p (n l) -> p n l", l=256),
                                                      mybir.AxisListType.X, ALU.add), r=[b_k], w=[b_kmean])
                P.op("dve", lambda e: e.tensor_scalar(kmeanb[:, :NBLK], kmean[:, :NBLK], 1.0 / 256.0, None, ALU.mult),
                     r=[b_kmean], w=[b_kmeanb])
                gp, b_gp = st[cnt["st"] % 3]
                cnt["st"] += 1

                def gm(e):
                    ins = None
                    for t in range(TT):
                        ins = e.matmul(gp[:, t * 16:(t + 1) * 16], q_t[:, t * 128:(t + 1) * 128], kmeanb[:],
                                       start=True, stop=True)
                    return ins
                P.op("pe", gm, r=[b_q, b_kmeanb], w=[b_gp])
                P.op("dve", lambda e: e.tensor_tensor(gpad[:], gp[:, :TT * 16].rearrange("p (t n) -> p t n", n=16),
                                                      negm[:], ALU.add), r=[b_gp, b_negm], w=[b_gpad])
                for t in range(TT):
                    P.op("dve", lambda e, t=t: e.max(out=m8[:, t, :], in_=gpad[:, t, :]), r=[b_gpad],
                         w=[b_m8] if t in (0, TT - 1) else [])
                P.op("dve", lambda e: e.tensor_tensor(bias[:], gpad[:], m8[:, :, 2:3].to_broadcast([128, TT, 16]),
                                                      ALU.is_ge), r=[b_gpad, b_m8], w=[b_bias])
                P.op("dve", lambda e: e.tensor_scalar(bias[:], bias[:], -NEG, NEG, ALU.mult, ALU.add),
                     r=[b_bias], w=[b_bias])
                P.op("dve", lambda e: e.tensor_tensor(bias[:], bias[:], keepm[:], ALU.mult),
                     r=[b_bias, b_keepm], w=[b_bias])
                for t0 in range(0, TT, 8):
                    m_t, b_m = misc[0]
                    cnt["misc"] += 1
                    nn = min(8, TT - t0)

                    def tr(e, m_t=m_t, t0=t0, nn=nn):
                        ins = None
                        for c in range(nn):
                            ins = e.transpose(m_t[:16, c, :], bias[:, t0 + c, :], ident[:])
                        return ins
                    P.op("pe", tr, r=[b_bias, b_id], w=[b_m])
                    P.op("dve", lambda e, m_t=m_t, t0=t0, nn=nn: e.tensor_copy(
                        biasT[:, t0 * 128:(t0 + nn) * 128].rearrange("p (c q) -> p c q", q=128), m_t[:16, :nn, :]),
                        r=[b_m], w=[b_biasT])
            LAG = 2
            items = []
            for g in range(NG):
                tiles = plan(g)
                og = cnt["og"] % 2
                cnt["og"] += 1
                for i, (kt, mi, brow) in enumerate(tiles):
                    items.append((g, og, kt, mi, brow, i == 0, i == len(tiles) - 1))

            def finalize(g, og):
                o_t, b_o = oT[og]
                d_t, b_d = den[og]
                q0 = g * 512
                r_t, b_r = rden[og]
                o3, b_o3 = o32[og]
                if sink_h is not None:
                    P.op("dve", lambda e: e.tensor_scalar(r_t[:dv, :], d_t[:dv, :], esink[:dv, sink_h:sink_h + 1],
                                                          None, ALU.add), r=[b_d, b_esink], w=[b_r])
                    P.op("dve", lambda e: e.reciprocal(r_t[:dv, :], r_t[:dv, :]), r=[b_r], w=[b_r])
                    P.op("dve", lambda e: e.tensor_tensor(o3[:dv, :], o_t[:dv, :], r_t[:dv, :], ALU.mult),
                         r=[b_o, b_r], w=[b_o3])
                else:
                    P.op("act", lambda e: e.activation(out=r_t[:dv, :], in_=d_t[:dv, :], func=AF.Ln),
                         r=[b_d], w=[b_r])
                    P.op("act", lambda e: e.activation(out=r_t[:dv, :], in_=r_t[:dv, :], func=AF.Exp, scale=-1.0),
                         r=[b_r], w=[b_r])
                    P.op("dve", lambda e: e.tensor_tensor(o3[:dv, :], o_t[:dv, :], r_t[:dv, :], ALU.mult),
                         r=[b_o, b_r], w=[b_o3])
                P.op("pool", lambda e: e.tensor_tensor(y_t[:dv, q0:q0 + 512], o3[:dv, :], g_t[:dv, q0:q0 + 512],
                                                       ALU.mult), r=[b_o3, b_g], w=[b_yt])

            def pv(item, p_t, b_p):
                g, og, kt, mi, brow, first, last = item
                o_t, b_o = oT[og]
                d_t, b_d = den[og]

                def f(e):
                    e.matmul(o_t[:dv, :], v_t[:, kt, :dv], p_t[:], start=first, stop=last)
                    return e.matmul(d_t[:dv, :], ones[:, :dv], p_t[:], start=first, stop=last)
                P.op("pe", f, r=[b_v, b_p, b_ones], w=[b_o, b_d])
                if last:
                    finalize(g, og)

            inflight = []
            for item in items:
                g, og, kt, mi, brow, first, last = item
                q0 = g * 512
                s_t, b_s = st[cnt["st"] % 3]
                cnt["st"] += 1

                pe_mask = (layer == 1 and mi is not None and M_DIL <= mi < M_DIL + NBIN)

                def qk(e):
                    ins = None
                    nparts = len(ktiles)
                    for pi in range(nparts):
                        k_t, _, K = ktiles[pi]
                        q_t, _, _ = qtiles[pi]
                        ins = e.matmul(s_t[:], k_t[:K, kt * 128:(kt + 1) * 128], q_t[:K, q0:q0 + 512],
                                       start=(pi == 0), stop=(pi == nparts - 1 and brow is None and not pe_mask))
                    if brow is not None:
                        ins = e.matmul(s_t[:], esel[:, brow, :], biasT[:, q0:q0 + 512], start=False, stop=True)
                    if pe_mask:
                        ins = e.matmul(s_t[:], ident[:, :], bmd[:, mi - M_DIL, :], start=False, stop=True)
                    return ins
                rr = [b for (_, b, _) in ktiles] + [b for (_, b, _) in qtiles]
                if brow is not None:
                    rr += [b_esel, b_biasT]
                if pe_mask:
                    rr += [b_id, b_bmd]
                P.op("pe", qk, r=rr, w=[b_s])
                p_t, b_p = pT[cnt["pt"] % NPT]
                cnt["pt"] += 1
                P.op("act", lambda e: e.activation(out=p_t[:], in_=s_t[:], func=AF.Exp, scale=float(scale)),
                     r=[b_s], w=[b_p])
                if mi is not None and not pe_mask:
                    P.op("dve", lambda e: e.tensor_tensor(p_t[:], p_t[:], mask_ap(mi), ALU.mult),
                         r=[b_p, b_masks], w=[b_p])
                inflight.append((item, p_t, b_p))
                if len(inflight) > LAG:
                    pv(*inflight.pop(0))
            while inflight:
                pv(*inflight.pop(0))
            P.dma("pool", y_ap, y_t[:dv, :], r=[b_yt], w=[b_y])

        def rows(name, r0, n):
            return hT[R[name] + r0:R[name] + r0 + n, :]

        def plan_causal(g):
            return [(kt, (M_CAUSAL + kt - 4 * g) if kt >= 4 * g else None, None) for kt in range(4 * g + 4)]

        def plan_moba(g):
            return [(kt, (M_CAUSAL + kt - 4 * g) if kt >= 4 * g else None, (kt // 2) if kt < 4 * g + 2 else None)
                    for kt in range(4 * g + 4)]

        def plan_mem(g):
            return [(0, None, None), (1, None, None)]

        def plan_swa(g):
            return [(4 * g - 1 + j, M_SWA + j, None) for j in range(5) if 4 * g - 1 + j >= 0]

        def plan_dil(g):
            return [(4 * g - 16 + j, M_DIL + j, None) for j in range(20) if 4 * g - 16 + j >= 0]

        gate0 = R["gate"]
        if layer == 0:
            for h in range(8):
                kt_, vt_ = load_kv([(rows("kn", h * 128, 128), 128, b_hT), (rows("kpe", 0, 64), 64, b_hT)],
                                   (rows("va", h * 128, 128), b_hT), 128)
                attn_q([(rows("qn", h * 128, 128), 128, b_hT), (rows("qr", h * 64, 64), 64, b_hT)], kt_, vt_, 128,
                       192 ** -0.5, plan_causal, hT[gate0 + h * 128:gate0 + (h + 1) * 128, :],
                       yT_d[h * 128:(h + 1) * 128, :])
            for h in range(8):
                kt_, vt_ = load_kv([(rows("bk", h * 128, 128), 128, b_hT)], (rows("bv", h * 128, 128), b_hT), 128)
                c = 8 + h
                attn_q([(rows("bq", h * 128, 128), 128, b_hT)], kt_, vt_, 128, 128 ** -0.5, plan_moba,
                       hT[gate0 + c * 128:gate0 + (c + 1) * 128, :], yT_d[c * 128:(c + 1) * 128, :], moba=True)
            ybase = 16
        else:
            for h in range(6):
                kt_, vt_ = load_kv([(rows("dk", h * 128, 128), 128, b_hT)], (rows("dv", h * 128, 128), b_hT), 128)
                c = 8 + h
                attn_q([(rows("dq", h * 128, 128), 128, b_hT)], kt_, vt_, 128, 128 ** -0.5, plan_dil,
                       hT[gate0 + c * 128:gate0 + (c + 1) * 128, :], yT_d[c * 128:(c + 1) * 128, :])
            ybase = 14
        for h in range(4):
            kt_, vt_ = load_kv([(memkv_d[h * 128:(h + 1) * 128, :], 128, b_mem)],
                               (memkv_d[512 + h * 128:512 + (h + 1) * 128, :], b_mem), 128)
            c = ybase + h
            attn_q([(rows("mq", h * 128, 128), 128, b_hT)], kt_, vt_, 128, 128 ** -0.5, plan_mem,
                   hT[gate0 + c * 128:gate0 + (c + 1) * 128, :], yT_d[c * 128:(c + 1) * 128, :])


def phase_swa(P, C, hT, R, yT_d, S, sinks_d):
    TT = S // 128
    b_hT = C["_b_" + hT.tensor.name]
    b_y = C["_b_" + yT_d.tensor.name]
    gate0 = R["gate"]
    scale = 64 ** -0.5
    with Phase(P, "swa") as ph:
        ones, b_ones = ph.sb([128, 128], BF16)
        ident, b_id = ph.sb([128, 128], BF16)
        P.dma("sp", ones[:], C["ones"][:, :], w=[b_ones])
        P.dma("sp", ident[:], C["ident"][:, :], w=[b_id])
        masks, b_masks = ph.sb([128, 2, 512], BF16)
        P.dma("sp", masks[:], C["masks"][M_SWA2:M_SWA2 + 2, :, :].rearrange("m p q -> p m q"), w=[b_masks])
        esink, b_esink = ph.sb([128, 16], F32)
        P.dma("sp", esink[:], sinks_d.partition_broadcast(128), w=[b_esink])
        P.op("act", lambda e: e.activation(out=esink[:], in_=esink[:], func=AF.Exp), r=[], w=[b_esink])
        bm, b_bm = ph.sb([128, 2, 512], BF16)
        P.op("dve", lambda e: e.tensor_scalar(bm[:], masks[:], -1.0, -NEG, ALU.add, ALU.mult),
             r=[b_masks], w=[b_bm])
        sel1, b_sel1 = ph.sb([16, 64], BF16)
        P.dma("sp", sel1[:], C["esel"][:, 0, 0:64], w=[b_sel1])
        srow, b_srow = ph.sb([16, 4, 128], BF16)
        P.op("dve", lambda e: e.memset(srow[:], 0.0), r=[], w=[b_srow])
        q4 = [ph.sb([64, 4, S], BF16) for _ in range(2)]
        g4 = [ph.sb([64, 4, S], BF16) for _ in range(1)]
        y4 = [ph.sb([64, 4, S], BF16) for _ in range(1)]
        kk = [ph.sb([64, S], BF16) for _ in range(2)]
        vT = [ph.sb([64, S], BF16) for _ in range(2)]
        V = [ph.sb([128, TT, 64], BF16) for _ in range(2)]
        pT = [ph.sb([128, 512], BF16) for _ in range(5)]
        rden = [ph.sb([64, 512], F32) for _ in range(2)]
        o32 = [ph.sb([64, 512], F32) for _ in range(2)]
        st = [ph.ps([128, 512], F32) for _ in range(3)]
        oT = [ph.ps([128, 512], F32) for _ in range(2)]
        den = [ph.ps([128, 512], F32) for _ in range(2)]
        misc = ph.ps([128, 8, 128], BF16)
        n_st = 0
        n_pt = 0
        n_og = 0
        it = 0
        for kvh in range(2):
            k_t, b_k = kk[kvh % 2]
            vt, b_vt = vT[kvh % 2]
            v_t, b_v = V[kvh % 2]
            P.dma("sp", k_t[:], hT[R["ck"] + kvh * 64:R["ck"] + (kvh + 1) * 64, :], r=[b_hT], w=[b_k])
            P.dma("sp", vt[:], hT[R["cv"] + kvh * 64:R["cv"] + (kvh + 1) * 64, :], r=[b_hT], w=[b_vt])
            for t0 in range(0, TT, 8):
                m_t, b_m = misc

                def tr(e, m_t=m_t, vt=vt, t0=t0):
                    ins = None
                    for c in range(8):
                        ins = e.transpose(m_t[:, c, :64], vt[:, (t0 + c) * 128:(t0 + c + 1) * 128], ident[:64, :64])
                    return ins
                P.op("pe", tr, r=[b_vt, b_id], w=[b_m])
                P.op("dve", lambda e, m_t=m_t, v_t=v_t, t0=t0: e.tensor_copy(v_t[:, t0:t0 + 8, :], m_t[:, :8, :64]),
                     r=[b_m], w=[b_v])
            for hg in range(2):
                h0 = kvh * 8 + hg * 4
                q_t, b_q = q4[it % 2]
                g_t, b_g = g4[0]
                y_t, b_yt = y4[0]
                it += 1
                P.dma("sp", q_t[:], hT[R["cq"] + h0 * 64:R["cq"] + (h0 + 4) * 64, :].rearrange("(h p) s -> p h s", p=64),
                      r=[b_hT], w=[b_q])
                P.dma("sp", g_t[:], hT[gate0 + h0 * 64:gate0 + (h0 + 4) * 64, :].rearrange("(h p) s -> p h s", p=64),
                      r=[b_hT], w=[b_g])
                P.op("dve", lambda e: e.tensor_copy(srow[0:1, :, :],
                                                    esink[0:1, h0:h0 + 4].unsqueeze(2).to_broadcast([1, 4, 128])),
                     r=[b_esink], w=[b_srow])
                items = []
                for qt in range(TT):
                    og = n_og % 2
                    n_og += 1
                    tiles = [(qt - 1, 0), (qt, 1)] if qt > 0 else [(qt, 1)]
                    for i, (kt, mi) in enumerate(tiles):
                        items.append((qt, og, kt, mi, i == 0, i == len(tiles) - 1))

                def finalize(qt, og):
                    o_t, b_o = oT[og]
                    d_t, b_d = den[og]
                    r_t, b_r = rden[og]
                    o3, b_o3 = o32[og]
                    P.op("act", lambda e: e.activation(out=r_t[:], in_=d_t[:64, :], func=AF.Ln), r=[b_d], w=[b_r])
                    P.op("act", lambda e: e.activation(out=r_t[:], in_=r_t[:], func=AF.Exp, scale=-1.0),
                         r=[b_r], w=[b_r])
                    P.op("dve", lambda e: e.tensor_tensor(o3[:], o_t[:64, :], r_t[:], ALU.mult), r=[b_o, b_r], w=[b_o3])
                    P.op("pool", lambda e: e.tensor_tensor(
                        y_t[:, :, qt * 128:(qt + 1) * 128], o3[:].rearrange("p (h q) -> p h q", h=4),
                        g_t[:, :, qt * 128:(qt + 1) * 128], ALU.mult), r=[b_o3, b_g], w=[b_yt])

                def pv(item, p_t, b_p):
                    qt, og, kt, mi, first, last = item
                    o_t, b_o = oT[og]
                    d_t, b_d = den[og]

                    def f(e):
                        e.matmul(o_t[:64, :], v_t[:, kt, :], p_t[:], start=first, stop=last)
                        ins = e.matmul(d_t[:64, :], ones[:, :64], p_t[:], start=first, stop=False)
                        if last:
                            ins = e.matmul(d_t[:64, :], sel1[:, :], srow[:].rearrange("p h q -> p (h q)"),
                                           start=False, stop=True)
                        return ins
                    P.op("pe", f, r=[b_v, b_p, b_ones, b_sel1, b_srow], w=[b_o, b_d])
                    if last:
                        finalize(qt, og)

                inflight = []
                for item in items:
                    qt, og, kt, mi, first, last = item
                    s_t, b_s = st[n_st % 3]
                    n_st += 1
                    def qk(e):
                        e.matmul(s_t[:], k_t[:, kt * 128:(kt + 1) * 128], q_t[:, :, qt * 128:(qt + 1) * 128],
                                 start=True, stop=False)
                        return e.matmul(s_t[:], ident[:, :], bm[:, mi, :], start=False, stop=True)
                    P.op("pe", qk, r=[b_k, b_q, b_id, b_bm], w=[b_s])
                    p_t, b_p = pT[n_pt % 5]
                    n_pt += 1
                    P.op("act", lambda e: e.activation(out=p_t[:], in_=s_t[:], func=AF.Exp, scale=float(scale)),
                         r=[b_s], w=[b_p])
                    inflight.append((item, p_t, b_p))
                    if len(inflight) > 2:
                        pv(*inflight.pop(0))
                while inflight:
                    pv(*inflight.pop(0))
                P.dma("pool", yT_d[h0 * 64:(h0 + 4) * 64, :].rearrange("(h p) s -> p h s", p=64), y_t[:],
                      r=[b_yt], w=[b_y])


def phase_out(P, C, layer, yT_d, wout_d, NCH, xin_d, xout_d, gname, xnT_d, S, final):
    nc = P.nc
    NT = S // 128
    GT = 4
    b_y = C["_b_" + yT_d.tensor.name]
    b_xin = C["_b_" + xin_d.tensor.name] if ("_b_" + xin_d.tensor.name) in C else None
    with Phase(P, f"out{layer}") as ph:
        wo, b_wo = ph.sb([128, NCH, D], BF16)
        b_wos = [P.buf() for _ in range(NCH)]
        wst = [ph.sb([128, D], F32) for _ in range(2)]
        for c in range(NCH):
            w_t, b_w = wst[c % 2]
            P.dma("sp", w_t[:], wout_d[c * 128:(c + 1) * 128, :], w=[b_w])
            if c % 2:
                P.op("act", lambda e, w_t=w_t, c=c: e.activation(out=wo[:, c, :], in_=w_t[:], func=AF.Copy),
                     r=[b_w], w=[b_wos[c]])
            else:
                P.op("dve", lambda e, w_t=w_t, c=c: e.tensor_copy(wo[:, c, :], w_t[:]), r=[b_w], w=[b_wos[c]])
        ident, b_id = ph.sb([128, 128], BF16)
        P.dma("sp", ident[:], C["ident"][:, :], w=[b_id])
        if final:
            gfin, b_gf = ph.sb([128, D], F32)
            P.dma("sp", gfin[:], C["gfin"].partition_broadcast(128), w=[b_gf])
        else:
            gT, b_g = ph.sb([128, KC], F32)
            P.dma("sp", gT[:], C[gname][:, :], w=[b_g])
        yt = [ph.sb([128, NCH, 128 * GT], BF16) for _ in range(2)]
        xt = [ph.sb([128, D], F32) for _ in range(2)]
        junk = ph.sb([128, D], BF16)
        ss = [ph.sb([128, 1], F32) for _ in range(2)]
        rs = [ph.sb([128, 1], F32) for _ in range(2)]
        acc = [ph.ps([128, 512], F32) for _ in range(4)]
        if not final:
            xs = [ph.sb([128, D], BF16) for _ in range(2)]
            stg = [ph.sb([128, KC, 128 * GT], BF16) for _ in range(2)]
            pst = [ph.ps([128, 8, 128], BF16) for _ in range(2)]
        else:
            ot = [ph.sb([128, D], F32) for _ in range(2)]
        na = 0
        for t in range(NT):
            s = t % 2
            gi = (t // GT) % 2
            y_t, b_yt = yt[gi]
            if t % GT == 0:
                P.dma("sp", y_t[:], yT_d.rearrange("(c p) s -> p c s", p=128)[:, :, t * 128:(t + GT) * 128],
                      r=[b_y], w=[b_yt])
            off = (t % GT) * 128
            x_t, b_x = xt[s]
            P.dma("sp", x_t[:], xin_d[t * 128:(t + 1) * 128, :], r=[b_xin] if b_xin else [], w=[b_x])
            x1_t, b_x1 = x_t, b_x
            for nq in range(4):
                a_t, b_a = acc[na % 4]
                na += 1

                def mm(e, a_t=a_t, y_t=y_t, off=off, nq=nq):
                    ins = None
                    for c in range(NCH):
                        ins = e.matmul(a_t[:], y_t[:, c, off:off + 128], wo[:, c, nq * 512:(nq + 1) * 512],
                                       start=(c == 0), stop=(c == NCH - 1))
                    return ins
                P.op("pe", mm, r=[b_yt] + b_wos, w=[b_a])
                P.op("dve", lambda e, a_t=a_t, x_t=x_t, x1_t=x1_t, nq=nq: e.tensor_tensor(
                    x1_t[:, nq * 512:(nq + 1) * 512], a_t[:], x_t[:, nq * 512:(nq + 1) * 512], ALU.add),
                    r=[b_a, b_x], w=[b_x1])
            P.op("act", lambda e, x1_t=x1_t, s=s: e.activation(out=junk[0][:], in_=x1_t[:], func=AF.Square,
                                                               accum_out=ss[s][0][:]),
                 r=[b_x1], w=[junk[1], ss[s][1]])
            P.op("dve", lambda e, s=s: e.tensor_scalar(rs[s][0][:], ss[s][0][:], 1.0 / D, EPS, ALU.mult, ALU.add),
                 r=[ss[s][1]], w=[rs[s][1]])
            P.op("act", lambda e, s=s: e.activation(out=rs[s][0][:], in_=rs[s][0][:], func=AF.Sqrt),
                 r=[rs[s][1]], w=[rs[s][1]])
            P.op("dve", lambda e, s=s: e.reciprocal(rs[s][0][:], rs[s][0][:]), r=[rs[s][1]], w=[rs[s][1]])
            if final:
                o_t, b_o = ot[s]
                P.op("dve", lambda e, o_t=o_t, x1_t=x1_t, s=s: e.scalar_tensor_tensor(
                    o_t[:], x1_t[:], rs[s][0][:, 0:1], gfin[:], ALU.mult, ALU.mult),
                    r=[b_x1, rs[s][1], b_gf], w=[b_o])
                P.dma("pool", xout_d[t * 128:(t + 1) * 128, :], o_t[:], r=[b_o], w=[C["_b_out"]])
            else:
                P.dma("pool", xout_d[t * 128:(t + 1) * 128, :], x1_t[:], r=[b_x1], w=[C["_b_" + xout_d.tensor.name]])
                P.op("act", lambda e, x1_t=x1_t, s=s: e.activation(out=xs[s][0][:], in_=x1_t[:], func=AF.Copy,
                                                                  scale=rs[s][0][:]),
                     r=[b_x1, rs[s][1]], w=[xs[s][1]])
                sg, b_sg = stg[gi]
                for hh in range(2):
                    pt, b_pt = pst[hh]

                    def tr(e, pt=pt, s=s, hh=hh):
                        ins = None
                        for c in range(8):
                            ins = e.transpose(pt[:, c, :], xs[s][0][:, (hh * 8 + c) * 128:(hh * 8 + c + 1) * 128],
                                              ident[:])
                        return ins
                    P.op("pe", tr, r=[xs[s][1], b_id], w=[b_pt])
                    P.op("dve", lambda e, pt=pt, sg=sg, hh=hh, off=off: e.tensor_tensor(
                        sg[:, hh * 8:(hh + 1) * 8, off:off + 128], pt[:],
                        gT[:, hh * 8:(hh + 1) * 8].unsqueeze(2).to_broadcast([128, 8, 128]), ALU.mult),
                        r=[b_pt, b_g], w=[b_sg])
                if t % GT == GT - 1:
                    t0 = (t // GT) * GT * 128
                    P.dma("pool", xnT_d.rearrange("(c p) s -> p c s", p=128)[:, :, t0:t0 + 128 * GT], sg[:],
                          r=[b_sg], w=[C["_b_" + xnT_d.tensor.name]])


IN_SPECS = None


def build_program(S, stop_after=None, debug_outs=()):
    nc = bass.Bass("TRN2", target_bir_lowering=False)
    P = Prog(nc)
    C = {}

    def din(name, shape, dt=F32):
        C[name] = nc.dram_tensor(name, list(shape), dt, kind="ExternalInput").ap()
        return C[name]

    def scratch(name, shape, dt):
        kind = "ExternalOutput" if name in debug_outs else "Internal"
        t = nc.dram_tensor(name, list(shape), dt, kind=kind).ap()
        C["_b_" + name] = P.buf(name)
        return t
    x = din("x", [S, D])
    mem = din("mem", [NMEM, D])
    din("g0", [128, KC]); din("g1", [128, KC]); din("gm0", [128, KC]); din("gm1", [128, KC])
    din("gq", [128, 4]); din("gkv", [128, 2]); din("gfin", [D])
    w_in0 = din("w_in0", [D, 6976]); w_in1 = din("w_in1", [D, 6400])
    wuq = din("wuq", [512, 1536]); wukv = din("wukv", [256, 2048])
    wm0 = din("wm0", [D, 1024]); wm1 = din("wm1", [D, 1024])
    wo0 = din("wo0", [2560, D]); wo1 = din("wo1", [2304, D])
    sinks = din("sinks", [16])
    din("ident", [128, 128], BF16); din("ones", [128, 128], BF16); din("esel", [16, 16, 128], BF16)
    din("perm", [3, 128, 128], BF16); din("rope", [3, 2, 128, S]); din("masks", [NMASK, 128, 512], BF16)
    din("negm", [128, S // 128, 16]); din("keepm", [128, S // 128, 16])
    out = nc.dram_tensor("out", [S, D], F32, kind="ExternalOutput").ap()
    C["_b_out"] = P.buf("out")

    xnT = scratch("xnT", [D, S], BF16)
    memnT = scratch("memnT", [D, NMEM], BF16)
    memkv0 = scratch("memkv0", [1024, NMEM], BF16)
    memkv1 = scratch("memkv1", [1024, NMEM], BF16)
    lat = scratch("lat", [768, S], F32)
    hT0 = scratch("hT0", [L0_NROWS, S], BF16)
    hT1 = scratch("hT1", [L1_NROWS, S], BF16)
    yT0 = scratch("yT0", [2560, S], BF16)
    yT1 = scratch("yT1", [2304, S], BF16)
    x1 = scratch("x1", [S, D], F32)

    def jobs_for(hT, R, spec):
        jobs = []
        for (name, c0, ncols, kind, tab, dst) in spec:
            for j in range(0, ncols, 128):
                n = min(128, ncols - j)
                if dst is None:
                    d = hT[R[name] + j:R[name] + j + 128, :] if n == 128 else hT[R[name] + j:R[name] + j + n, :]
                else:
                    d = dst[j:j + n, :]
                jobs.append(dict(c0=c0 + j, n=n, kind=kind, tab=tab, dst=d))
        return jobs

    def done():
        P.barrier()
        return nc

    memjobs = lambda dst: [dict(c0=j, n=128, kind="plain", tab=None, dst=dst[j:j + 128, :]) for j in range(0, 1024, 128)]
    phase_norm(P, C, mem, "gm0", memnT, NMEM, "m0")
    phase_inproj(P, C, memnT, wm0, memjobs(memkv0), NMEM, "m0")
    phase_norm(P, C, mem, "gm1", memnT, NMEM, "m1")
    phase_inproj(P, C, memnT, wm1, memjobs(memkv1), NMEM, "m1")
    phase_norm(P, C, x, "g0", xnT, S, "x0")
    if stop_after == "norm0":
        return done()
    spec0 = [("lat", 0, 768, "lat", None, lat), ("kpe", 768, 64, "rope", T_MLA, None),
             ("bq", 832, 1024, "rope", T_MOBA, None), ("bk", 1856, 1024, "rope", T_MOBA, None),
             ("bv", 2880, 1024, "plain", None, None), ("mq", 3904, 512, "plain", None, None),
             ("gate", 4416, 2560, "silu", None, None)]
    phase_inproj(P, C, xnT, w_in0, jobs_for(hT0, L0_ROWS, spec0), S, "l0")
    phase_mla2(P, C, lat, hT0, L0_ROWS, wuq, wukv, S)
    if stop_after == "inproj0":
        return done()
    phase_attn(P, C, 0, hT0, L0_ROWS, memkv0, yT0, S)
    if stop_after == "attn0":
        return done()
    phase_out(P, C, 0, yT0, wo0, 20, x, x1, "g1", xnT, S, final=False)
    if stop_after == "out0":
        return done()
    spec1 = [("cq", 0, 1024, "rope", T_SWA, None), ("ck", 1024, 128, "rope", T_SWA, None),
             ("cv", 1152, 128, "plain", None, None), ("dq", 1280, 768, "rope", T_MOBA, None),
             ("dk", 2048, 768, "rope", T_MOBA, None), ("dv", 2816, 768, "plain", None, None),
             ("mq", 3584, 512, "plain", None, None), ("gate", 4096, 2304, "silu", None, None)]
    phase_inproj(P, C, xnT, w_in1, jobs_for(hT1, L1_ROWS, spec1), S, "l1")
    if stop_after == "inproj1":
        return done()
    phase_swa(P, C, hT1, L1_ROWS, yT1, S, sinks)
    phase_attn(P, C, 1, hT1, L1_ROWS, memkv1, yT1, S, sinks_d=sinks)
    if stop_after == "attn1":
        return done()
    phase_out(P, C, 1, yT1, wo1, 18, x1, out, None, None, S, final=True)
    return done()


def make_in_maps(inputs, S, nb):
    f = lambda a: np.ascontiguousarray(np.asarray(a, dtype=np.float32))
    gl = lambda g: np.ascontiguousarray(f(g).reshape(-1, 128).T)
    negm, keepm = _moba_consts(S)
    esel = np.zeros((16, 16, 128), np.float32)
    for n in range(16):
        esel[n, n, :] = 1.0
    common = {
        "g0": gl(inputs["ev_norm_g"][0]), "g1": gl(inputs["od_norm_g"][0]),
        "gm0": gl(inputs["ev_mem_norm_g"][0]), "gm1": gl(inputs["od_mem_norm_g"][0]),
        "gq": gl(inputs["ev_q_norm_g"][0]), "gkv": gl(inputs["ev_kv_norm_g"][0]),
        "gfin": f(inputs["final_norm_g"]),
        "w_in0": f(inputs["ev_w_in"][0]), "w_in1": f(inputs["od_w_in"][0]),
        "wuq": f(inputs["ev_w_uq"][0]), "wukv": f(inputs["ev_w_ukv"][0]),
        "wm0": f(inputs["ev_w_mem_kv"][0]), "wm1": f(inputs["od_w_mem_kv"][0]),
        "wo0": f(inputs["ev_w_out"][0]), "wo1": f(inputs["od_w_out"][0]),
        "sinks": f(inputs["od_sinks"][0]),
        "ident": np.eye(128, dtype=np.float32).astype(ml_dtypes.bfloat16),
        "ones": np.ones((128, 128), np.float32).astype(ml_dtypes.bfloat16),
        "esel": esel.astype(ml_dtypes.bfloat16),
        "perm": _perms().astype(ml_dtypes.bfloat16), "rope": _rope_tables(S), "masks": _masks(), "negm": negm, "keepm": keepm,
    }
    maps = []
    for b in range(nb):
        m = dict(common)
        m["x"] = f(inputs["x"][b][:S])
        m["mem"] = f(inputs["mem"][b])
        maps.append(m)
    return maps


def kernel(**inputs):
    B, S = inputs["x"].shape[0], inputs["x"].shape[1]
    nc = build_program(S)
    maps = make_in_maps(inputs, S, B)
    res = run_bass_kernel_spmd(nc, maps, core_ids=list(range(B)))
    return np.stack([np.asarray(res.results[b]["out"], dtype=np.float32) for b in range(B)], axis=0)
```

```python
from contextlib import ExitStack

import ml_dtypes
import numpy as np

import concourse.bass as bass
import concourse.mybir as mybir
from concourse.bass_utils import run_bass_kernel_spmd

F32 = mybir.dt.float32
BF16 = mybir.dt.bfloat16
AF = mybir.ActivationFunctionType
ALU = mybir.AluOpType

D = 2048
KC = 16
NMEM = 256
EPS = 1e-6
THETA = 500000.0
NEG = -30000.0

L0_ROWS = {}
_r = 0
for _n, _k in (("kpe", 128), ("bq", 1024), ("bk", 1024), ("bv", 1024), ("mq", 512), ("gate", 2560),
               ("qn", 1024), ("qr", 512), ("kn", 1024), ("va", 1024)):
    L0_ROWS[_n] = _r
    _r += _k
L0_NROWS = _r
L1_ROWS = {}
_r = 0
for _n, _k in (("cq", 1024), ("ck", 128), ("cv", 128), ("dq", 768), ("dk", 768), ("dv", 768), ("mq", 512),
               ("gate", 2304)):
    L1_ROWS[_n] = _r
    _r += _k
L1_NROWS = _r

M_CAUSAL = 0
M_SWA = 4
M_DIL = 9
M_SWA2 = 29
NMASK = 31
T_MLA, T_MOBA, T_SWA = 0, 1, 2


def _rope_tables(S):
    t = np.arange(S, dtype=np.float32)
    tabs = np.zeros((3, 2, 128, S), np.float32)
    tabs[:, 0] = 1.0

    def fill(ti, rot, rows0):
        half = rot // 2
        inv = (1.0 / (np.float32(THETA) ** (np.arange(0, rot, 2, dtype=np.float32) / np.float32(rot)))).astype(np.float32)
        ang = t[None, :] * inv[:, None]
        c, s = np.cos(ang).astype(np.float32), np.sin(ang).astype(np.float32)
        for r0 in rows0:
            tabs[ti, 0, r0:r0 + half] = c
            tabs[ti, 0, r0 + half:r0 + rot] = c
            tabs[ti, 1, r0:r0 + half] = -s
            tabs[ti, 1, r0 + half:r0 + rot] = s

    fill(T_MLA, 64, (0, 64))
    fill(T_MOBA, 32, (0,))
    fill(T_SWA, 16, (0, 64))
    return tabs


def _perms():
    p = np.zeros((3, 128, 128), np.float32)

    def fill(ti, rot, rows0):
        half = rot // 2
        for r0 in rows0:
            for i in range(half):
                p[ti, r0 + i + half, r0 + i] = 1.0
                p[ti, r0 + i, r0 + i + half] = 1.0

    fill(T_MLA, 64, (0, 64))
    fill(T_MOBA, 32, (0,))
    fill(T_SWA, 16, (0, 64))
    return p


def _masks():
    k = np.arange(128)[:, None]
    q = np.arange(512)[None, :]
    m = np.zeros((NMASK, 128, 512), np.float32)
    for j in range(4):
        m[M_CAUSAL + j] = (k + 128 * j <= q)
    for j in range(5):
        d = q - k - 128 * (j - 1)
        m[M_SWA + j] = (d >= 0) & (d <= 127)
    for j in range(20):
        d = q - k - 128 * (j - 16)
        m[M_DIL + j] = (((d >= 0) & (d <= 128)).astype(np.float32)
                        + ((d >= 0) & (d <= 512) & (d % 4 == 0)).astype(np.float32)
                        + ((d >= 0) & (d <= 2048) & (d % 16 == 0)).astype(np.float32))
    q1 = q % 128
    m[M_SWA2 + 0] = (q1 < k)
    m[M_SWA2 + 1] = (k <= q1)
    return m.astype(ml_dtypes.bfloat16)


def _moba_consts(S):
    TT = S // 128
    negm = np.zeros((128, TT, 16), np.float32)
    keepm = np.zeros((128, TT, 16), np.float32)
    for t in range(TT):
        blk = t // 2
        negm[:, t, blk:] = -1e30
        if blk >= 4:
            keepm[:, t, :blk] = 1.0
    return negm, keepm


class Buf:
    __slots__ = ("name", "w", "rs")

    def __init__(self, name):
        self.name = name
        self.w = None
        self.rs = {}


class Prog:
    NDS = 8

    def __init__(self, nc):
        self.nc = nc
        self.es = ExitStack()
        self.eng = {"pe": nc.tensor, "act": nc.scalar, "dve": nc.vector, "pool": nc.gpsimd, "sp": nc.sync}
        self.sem = {k: self.es.enter_context(nc.semaphore("sem_" + k)) for k in self.eng}
        self.cnt = {k: 0 for k in self.eng}
        self.seen = {k: {} for k in self.eng}
        self.dsem = {k: [self.es.enter_context(nc.semaphore(f"dsem_{k}_{i}")) for i in range(self.NDS)]
                     for k in ("sp", "pool")}
        self.dval = {k: [0] * self.NDS for k in self.dsem}
        self.dnext = {k: 0 for k in self.dsem}
        self.nb = 0

    def buf(self, name=None):
        self.nb += 1
        return Buf(name or f"b{self.nb}")

    def _wait(self, e, tok):
        sem, val, key = tok
        if self.seen[e].get(key, 0) >= val:
            return
        self.eng[e].wait_ge(sem, val)
        self.seen[e][key] = val

    def _deps(self, e, r, w):
        toks = []
        for b in r:
            if b.w is not None:
                toks.append(b.w)
        for b in w:
            if b.w is not None:
                toks.append(b.w)
            toks.extend(b.rs.values())
        own = "sem_" + e
        for t in toks:
            if e == "pe" and t[2] == own:
                continue
            self._wait(e, t)

    def _commit(self, tok, r, w):
        for b in w:
            b.w = tok
            b.rs = {}
        for b in r:
            if b in w:
                continue
            b.rs[tok[2]] = tok

    def op(self, e, fn, r=(), w=()):
        self._deps(e, r, w)
        ins = fn(self.eng[e])
        self.cnt[e] += 1
        ins.then_inc(self.sem[e], 1)
        tok = (self.sem[e], self.cnt[e], "sem_" + e)
        self._commit(tok, r, w)

    def dma(self, e, out, in_, r=(), w=()):
        self._deps(e, r, w)
        i = self.dnext[e]
        self.dnext[e] = (i + 1) % self.NDS
        key = f"dsem_{e}_{i}"
        sem = self.dsem[e][i]
        self._wait(e, (sem, self.dval[e][i], key))
        ins = self.eng[e].dma_start(out=out, in_=in_)
        self.dval[e][i] += 16
        ins.then_inc(sem, 16)
        tok = (sem, self.dval[e][i], key)
        self._commit(tok, r, w)

    def barrier(self):
        toks = [(self.sem[k], self.cnt[k], "sem_" + k) for k in self.eng if self.cnt[k] > 0]
        for k in self.dsem:
            for i in range(self.NDS):
                if self.dval[k][i] > 0:
                    toks.append((self.dsem[k][i], self.dval[k][i], f"dsem_{k}_{i}"))
        for e in self.eng:
            for t in toks:
                self._wait(e, t)


class Phase:
    def __init__(self, P, name):
        self.P = P
        self.name = name
        self.es = ExitStack()
        self.n = 0

    def __enter__(self):
        self.P.barrier()
        return self

    def __exit__(self, *a):
        self.P.barrier()
        self.es.close()
        return False

    def sb(self, shape, dt):
        self.n += 1
        t = self.es.enter_context(self.P.nc.sbuf_tensor(f"{self.name}_s{self.n}", list(shape), dt))
        return t, self.P.buf()

    def ps(self, shape, dt):
        self.n += 1
        t = self.es.enter_context(self.P.nc.psum_tensor(f"{self.name}_p{self.n}", list(shape), dt))
        return t, self.P.buf()


def phase_norm(P, C, src_d, gT_name, dst_d, ntok, tag):
    nc = P.nc
    NT = ntok // 128
    GW = min(4, NT)
    with Phase(P, "nrm" + tag) as ph:
        gT, b_g = ph.sb([128, KC], F32)
        ident, b_id = ph.sb([128, 128], BF16)
        P.dma("sp", gT[:], C[gT_name][:, :], w=[b_g])
        P.dma("sp", ident[:], C["ident"][:, :], w=[b_id])
        xt = [ph.sb([128, D], F32) for _ in range(2)]
        junk = ph.sb([128, D], BF16)
        xs = [ph.sb([128, D], BF16) for _ in range(2)]
        ss = [ph.sb([128, 1], F32) for _ in range(2)]
        rs = [ph.sb([128, 1], F32) for _ in range(2)]
        st = [ph.sb([128, KC, 128 * GW], BF16) for _ in range(2)]
        pst = [ph.ps([128, 8, 128], BF16) for _ in range(2)]
        for t in range(NT):
            s = t % 2
            x_t, b_x = xt[s]
            P.dma("sp", x_t[:], src_d[t * 128:(t + 1) * 128, :], w=[b_x])
            P.op("act", lambda e, x_t=x_t, s=s: e.activation(out=junk[0][:], in_=x_t[:], func=AF.Square,
                                                             accum_out=ss[s][0][:]),
                 r=[b_x], w=[junk[1], ss[s][1]])
            P.op("dve", lambda e, s=s: e.tensor_scalar(rs[s][0][:], ss[s][0][:], 1.0 / D, EPS, ALU.mult, ALU.add),
                 r=[ss[s][1]], w=[rs[s][1]])
            P.op("act", lambda e, s=s: e.activation(out=rs[s][0][:], in_=rs[s][0][:], func=AF.Sqrt),
                 r=[rs[s][1]], w=[rs[s][1]])
            P.op("dve", lambda e, s=s: e.reciprocal(rs[s][0][:], rs[s][0][:]), r=[rs[s][1]], w=[rs[s][1]])
            P.op("act", lambda e, x_t=x_t, s=s: e.activation(out=xs[s][0][:], in_=x_t[:], func=AF.Copy,
                                                             scale=rs[s][0][:]),
                 r=[b_x, rs[s][1]], w=[xs[s][1]])
            gi = (t // GW) % 2
            stg, b_st = st[gi]
            off = (t % GW) * 128
            for hh in range(2):
                pt, b_pt = pst[hh]

                def tr(e, pt=pt, s=s, hh=hh):
                    ins = None
                    for c in range(8):
                        ins = e.transpose(pt[:, c, :], xs[s][0][:, (hh * 8 + c) * 128:(hh * 8 + c + 1) * 128],
                                          ident[:])
                    return ins
                P.op("pe", tr, r=[xs[s][1], b_id], w=[b_pt])
                P.op("dve", lambda e, pt=pt, stg=stg, hh=hh, off=off: e.tensor_tensor(
                    stg[:, hh * 8:(hh + 1) * 8, off:off + 128], pt[:],
                    gT[:, hh * 8:(hh + 1) * 8].unsqueeze(2).to_broadcast([128, 8, 128]), ALU.mult),
                    r=[b_pt, b_g], w=[b_st])
            if t % GW == GW - 1:
                t0 = (t // GW) * GW * 128
                P.dma("pool", dst_d.rearrange("(c p) s -> p c s", p=128)[:, :, t0:t0 + 128 * GW], stg[:],
                      r=[b_st], w=[C["_b_" + dst_d.tensor.name]])


def phase_inproj(P, C, xnT_d, w_d, jobs, ntok, tag):
    nc = P.nc
    HALF = min(ntok, 2048)
    NH = ntok // HALF
    GW = min(512, HALF)
    TG = HALF // GW
    b_xsrc = C["_b_" + xnT_d.tensor.name]
    tabs_needed = sorted({j["tab"] for j in jobs if j["kind"] == "rope"})
    with Phase(P, "inp" + tag) as ph:
        xn, b_xn = ph.sb([128, KC, HALF], BF16)
        wf = [ph.sb([128, 4, 512], F32) for _ in range(3)]
        wb = [(ph.sb([128, KC, 512], BF16)[0], [P.buf() for _ in range(4)]) for _ in range(2)]
        units = []
        for job in jobs:
            if units and units[-1][-1]["c0"] + units[-1][-1]["n"] == job["c0"] and \
                    job["c0"] + job["n"] - units[-1][0]["c0"] <= 512:
                units[-1].append(job)
            else:
                units.append([job])
        nwf = 0
        stg = [ph.sb([128, HALF], BF16) for _ in range(2)]
        has_lat = any(j["kind"] == "lat" for j in jobs)
        stg32 = [ph.sb([128, HALF], F32) for _ in range(2)] if has_lat else None
        a32 = [ph.sb([128, GW], BF16) for _ in range(2)]
        t1 = [ph.sb([128, GW], F32) for _ in range(2)]
        t2 = [ph.sb([128, GW], F32) for _ in range(2)]
        tabs = {ti: (ph.sb([128, HALF], F32), ph.sb([128, HALF], F32)) for ti in tabs_needed}
        perm = {ti: ph.sb([128, 128], BF16) for ti in tabs_needed}
        for ti in tabs_needed:
            P.dma("sp", perm[ti][0][:], C["perm"][ti, :, :], w=[perm[ti][1]])
        psA = [ph.ps([128, GW], F32) for _ in range(3)]
        psB = [ph.ps([128, GW], F32) for _ in range(2)]
        na = 0
        nb = 0
        pending = []
        for hf in range(NH):
            h0 = hf * HALF
            P.dma("sp", xn[:], xnT_d.rearrange("(c p) s -> p c s", p=128)[:, :, h0:h0 + HALF], r=[b_xsrc], w=[b_xn])
            for ti in tabs_needed:
                for cs in range(2):
                    P.dma("sp", tabs[ti][cs][0][:], C["rope"][ti, cs, :, h0:h0 + HALF], w=[tabs[ti][cs][1]])
            ji = -1

            def load_unit(ui):
                nonlocal nwf
                unit = units[ui]
                u0 = unit[0]["c0"]
                un = unit[-1]["c0"] + unit[-1]["n"] - u0
                wb_t, b_wbs = wb[ui % 2]
                for q4 in range(4):
                    wf_t, b_wf = wf[nwf % 3]
                    nwf += 1
                    P.dma("sp", wf_t[:, :, :un],
                          w_d.rearrange("(kc p) c -> p kc c", p=128)[:, 4 * q4:4 * q4 + 4, u0:u0 + un], w=[b_wf])
                    if q4 % 2 == 0:
                        P.op("act", lambda e, wf_t=wf_t, wb_t=wb_t, un=un, q4=q4: e.activation(
                            out=wb_t[:, 4 * q4:4 * q4 + 4, :un], in_=wf_t[:, :, :un], func=AF.Copy),
                            r=[b_wf], w=[b_wbs[q4]])
                    else:
                        P.op("dve", lambda e, wf_t=wf_t, wb_t=wb_t, un=un, q4=q4: e.tensor_copy(
                            wb_t[:, 4 * q4:4 * q4 + 4, :un], wf_t[:, :, :un]), r=[b_wf], w=[b_wbs[q4]])
            load_unit(0)
            for ui, unit in enumerate(units):
              u0 = unit[0]["c0"]
              wb_t, b_wbs = wb[ui % 2]
              if ui + 1 < len(units):
                  load_unit(ui + 1)
              for job in unit:
                ji += 1
                s = ji % 2
                n = job["n"]
                c0 = job["c0"]
                wo_ = c0 - u0
                kind = job["kind"]
                if kind == "lat":
                    sg, b_sg = stg32[s]
                else:
                    sg, b_sg = stg[s]
                for tg in range(TG):
                    ps, b_ps = psA[na % 3]
                    na += 1
                    g0 = tg * GW

                    def mm(e, ps=ps, wb_t=wb_t, n=n, g0=g0, wo_=wo_):
                        ins = None
                        for kc in range(KC):
                            ins = e.matmul(ps[:n, :], wb_t[:, kc, wo_:wo_ + n], xn[:, kc, g0:g0 + GW],
                                           start=(kc == 0), stop=(kc == KC - 1))
                        return ins
                    P.op("pe", mm, r=b_wbs + [b_xn], w=[b_ps])
                    while pending:
                        pending.pop(0)()
                    dst = sg[:n, g0:g0 + GW]
                    if kind in ("plain", "lat"):
                        P.op("act", lambda e, dst=dst, ps=ps, n=n: e.activation(out=dst, in_=ps[:n, :], func=AF.Copy),
                             r=[b_ps], w=[b_sg])
                    elif kind == "silu":
                        P.op("act", lambda e, dst=dst, ps=ps, n=n: e.activation(out=dst, in_=ps[:n, :], func=AF.Silu),
                             r=[b_ps], w=[b_sg])
                    else:
                        ti = job["tab"]
                        a_t, b_a = a32[nb % 2]
                        t1_t, b_t1 = t1[nb % 2]
                        t2_t, b_t2 = t2[nb % 2]
                        p2, b_p2 = psB[nb % 2]
                        nb += 1
                        ct, b_ct = tabs[ti][0]
                        sn, b_sn = tabs[ti][1]
                        pm, b_pm = perm[ti]
                        P.op("act", lambda e, a_t=a_t, ps=ps, n=n: e.activation(out=a_t[:n, :], in_=ps[:n, :],
                                                                                func=AF.Copy),
                             r=[b_ps], w=[b_a])
                        def stage_b(p2=p2, b_p2=b_p2, pm=pm, b_pm=b_pm, a_t=a_t, b_a=b_a, t1_t=t1_t, b_t1=b_t1,
                                    t2_t=t2_t, b_t2=b_t2, ct=ct, b_ct=b_ct, sn=sn, b_sn=b_sn, ps=ps, b_ps=b_ps,
                                    dst=dst, b_sg=b_sg, n=n, g0=g0):
                            P.op("pe", lambda e, p2=p2, pm=pm, a_t=a_t, n=n: e.matmul(p2[:n, :], pm[:n, :n], a_t[:n, :],
                                                                                      start=True, stop=True),
                                 r=[b_pm, b_a], w=[b_p2])
                            P.op("dve", lambda e, t1_t=t1_t, a_t=a_t, ct=ct, n=n, g0=g0: e.tensor_tensor(
                                t1_t[:n, :], ps[:n, :], ct[:n, g0:g0 + GW], ALU.mult), r=[b_ps, b_a, b_ct], w=[b_t1])
                            P.op("dve", lambda e, t2_t=t2_t, p2=p2, sn=sn, n=n, g0=g0: e.tensor_tensor(
                                t2_t[:n, :], p2[:n, :], sn[:n, g0:g0 + GW], ALU.mult), r=[b_p2, b_sn], w=[b_t2])
                            P.op("dve", lambda e, dst=dst, t1_t=t1_t, t2_t=t2_t, n=n: e.tensor_tensor(
                                dst, t1_t[:n, :], t2_t[:n, :], ALU.add), r=[b_t1, b_t2], w=[b_sg])
                        pending.append(stage_b)
                while pending:
                    pending.pop(0)()
                dd = job["dst"]
                P.dma("pool", dd[0:n, h0:h0 + HALF], sg[:n, :], r=[b_sg], w=[C["_b_" + dd.tensor.name]])


def phase_mla2(P, C, lat_d, hT, R, wuq_d, wukv_d, S):
    nc = P.nc
    GW = min(512, S)
    TG = S // GW
    b_lat = C["_b_" + lat_d.tensor.name]
    b_hT = C["_b_" + hT.tensor.name]
    with Phase(P, "mla2") as ph:
        wuq, b_wuq = ph.sb([128, 4, 1536], BF16)
        wukv, b_wukv = ph.sb([128, 2, 2048], BF16)
        wst = [ph.sb([128, 2048], F32) for _ in range(2)]
        gq, b_gq = ph.sb([128, 4], F32)
        gkv, b_gkv = ph.sb([128, 2], F32)
        ones, b_ones = ph.sb([128, 128], BF16)
        pm, b_pm = ph.sb([128, 128], BF16)
        P.dma("sp", gq[:], C["gq"][:, :], w=[b_gq])
        P.dma("sp", gkv[:], C["gkv"][:, :], w=[b_gkv])
        P.dma("sp", ones[:], C["ones"][:, :], w=[b_ones])
        P.dma("sp", pm[:], C["perm"][T_MLA, :, :], w=[b_pm])
        k = 0
        for kc in range(4):
            w_t, b_w = wst[k % 2]
            k += 1
            P.dma("sp", w_t[:, :1536], wuq_d[kc * 128:(kc + 1) * 128, :], w=[b_w])
            P.op("pool", lambda e, w_t=w_t, kc=kc: e.tensor_copy(wuq[:, kc, :], w_t[:, :1536]), r=[b_w], w=[b_wuq])
        for kc in range(2):
            w_t, b_w = wst[k % 2]
            k += 1
            P.dma("sp", w_t[:, :], wukv_d[kc * 128:(kc + 1) * 128, :], w=[b_w])
            P.op("pool", lambda e, w_t=w_t, kc=kc: e.tensor_copy(wukv[:, kc, :], w_t[:, :]), r=[b_w], w=[b_wukv])
        latt = [ph.sb([128, 6, GW], F32) for _ in range(2)]
        sq = [ph.sb([128, 6, GW], BF16) for _ in range(2)]
        rq = [ph.sb([128, GW], F32) for _ in range(2)]
        rkv = [ph.sb([128, GW], F32) for _ in range(2)]
        cn = [ph.sb([128, 6, GW], BF16) for _ in range(2)]
        cs = [(ph.sb([128, GW], F32), ph.sb([128, GW], F32)) for _ in range(2)]
        ost = [ph.sb([128, 28, GW], BF16) for _ in range(2)]
        a32 = [ph.sb([128, GW], BF16) for _ in range(2)]
        t1 = [ph.sb([128, GW], F32) for _ in range(2)]
        t2 = [ph.sb([128, GW], F32) for _ in range(2)]
        pss = [ph.ps([128, GW], F32) for _ in range(2)]
        psA = [ph.ps([128, GW], F32) for _ in range(3)]
        psB = [ph.ps([128, GW], F32) for _ in range(2)]
        na = 0
        nb = 0
        for tg in range(TG):
            s = tg % 2
            g0 = tg * GW
            lt, b_lt = latt[s]
            sq_t, b_sq = sq[s]
            cn_t, b_cn = cn[s]
            os_t, b_os = ost[s]
            P.dma("sp", lt[:], lat_d.rearrange("(c p) s -> p c s", p=128)[:, :, g0:g0 + GW], r=[b_lat], w=[b_lt])
            for cc in range(2):
                P.dma("sp", cs[s][cc][0][:], C["rope"][T_MLA, cc, :, g0:g0 + GW], w=[cs[s][cc][1]])
            P.op("act", lambda e, sq_t=sq_t, lt=lt: e.activation(out=sq_t[:], in_=lt[:], func=AF.Square),
                 r=[b_lt], w=[b_sq])
            for (pi, c_lo, c_hi, rr, nfeat, gg, b_gg) in ((0, 0, 4, rq[s], 512, gq, b_gq),
                                                          (1, 4, 6, rkv[s], 256, gkv, b_gkv)):
                p_s, b_p = pss[pi]

                def mm(e, p_s=p_s, c_lo=c_lo, c_hi=c_hi, sq_t=sq_t):
                    ins = None
                    for c in range(c_lo, c_hi):
                        ins = e.matmul(p_s[:], ones[:], sq_t[:, c, :], start=(c == c_lo), stop=(c == c_hi - 1))
                    return ins
                P.op("pe", mm, r=[b_ones, b_sq], w=[b_p])
                r_t, b_r = rr
                P.op("dve", lambda e, r_t=r_t, p_s=p_s, nfeat=nfeat: e.tensor_scalar(
                    r_t[:], p_s[:], 1.0 / nfeat, EPS, ALU.mult, ALU.add), r=[b_p], w=[b_r])
                P.op("act", lambda e, r_t=r_t: e.activation(out=r_t[:], in_=r_t[:], func=AF.Sqrt),
                     r=[b_r], w=[b_r])
                P.op("dve", lambda e, r_t=r_t: e.reciprocal(r_t[:], r_t[:]), r=[b_r], w=[b_r])
                for c in range(c_lo, c_hi):
                    P.op("dve", lambda e, c=c, c_lo=c_lo, lt=lt, cn_t=cn_t, gg=gg, r_t=r_t: e.scalar_tensor_tensor(
                        cn_t[:, c, :], lt[:, c, :], gg[:, c - c_lo:c - c_lo + 1], r_t[:], ALU.mult, ALU.mult),
                        r=[b_lt, b_gg, b_r], w=[b_cn])
            for oc in range(28):
                ps, b_ps = psA[na % 3]
                na += 1
                if oc < 8:
                    groups = [(0, 128, wuq, b_wuq, 0, 4, oc * 192)]
                elif oc < 12:
                    j = oc - 8
                    groups = [(0, 64, wuq, b_wuq, 0, 4, (2 * j) * 192 + 128),
                              (64, 64, wuq, b_wuq, 0, 4, (2 * j + 1) * 192 + 128)]
                elif oc < 20:
                    groups = [(0, 128, wukv, b_wukv, 4, 2, (oc - 12) * 256)]
                else:
                    groups = [(0, 128, wukv, b_wukv, 4, 2, (oc - 20) * 256 + 128)]

                def mm2(e, ps=ps, groups=groups, cn_t=cn_t):
                    ins = None
                    for (p0, m, wt, _b, cbase, nk, col) in groups:
                        for kc in range(nk):
                            ins = e.matmul(ps[p0:p0 + m, :], wt[:, kc, col:col + m], cn_t[:, cbase + kc, :],
                                           start=(kc == 0), stop=(kc == nk - 1))
                    return ins
                P.op("pe", mm2, r=[groups[0][3], b_cn], w=[b_ps])
                dst = os_t[:, oc, :]
                if not (8 <= oc < 12):
                    P.op("act", lambda e, dst=dst, ps=ps: e.activation(out=dst, in_=ps[:], func=AF.Copy),
                         r=[b_ps], w=[b_os])
                else:
                    a_t, b_a = a32[nb % 2]
                    t1_t, b_t1 = t1[nb % 2]
                    t2_t, b_t2 = t2[nb % 2]
                    p2, b_p2 = psB[nb % 2]
                    nb += 1
                    (ct, b_ct), (sn, b_sn) = cs[s]
                    P.op("act", lambda e, a_t=a_t, ps=ps: e.activation(out=a_t[:], in_=ps[:], func=AF.Copy),
                         r=[b_ps], w=[b_a])
                    P.op("pe", lambda e, p2=p2, a_t=a_t: e.matmul(p2[:], pm[:], a_t[:], start=True, stop=True),
                         r=[b_pm, b_a], w=[b_p2])
                    P.op("dve", lambda e, t1_t=t1_t, ps=ps, ct=ct: e.tensor_tensor(t1_t[:], ps[:], ct[:], ALU.mult),
                         r=[b_ps, b_a, b_ct], w=[b_t1])
                    P.op("dve", lambda e, t2_t=t2_t, p2=p2, sn=sn: e.tensor_tensor(t2_t[:], p2[:], sn[:], ALU.mult),
                         r=[b_p2, b_sn], w=[b_t2])
                    P.op("dve", lambda e, dst=dst, t1_t=t1_t, t2_t=t2_t: e.tensor_tensor(dst, t1_t[:], t2_t[:], ALU.add),
                         r=[b_t1, b_t2], w=[b_os])
            for (name, o0, nch) in (("qn", 0, 8), ("qr", 8, 4), ("kn", 12, 8), ("va", 20, 8)):
                r0 = R[name]
                P.dma("pool", hT[r0:r0 + nch * 128, :].rearrange("(c p) s -> p c s", p=128)[:, :, g0:g0 + GW],
                      os_t[:, o0:o0 + nch, :], r=[b_os], w=[b_hT])


def phase_attn(P, C, layer, hT, R, memkv_d, yT_d, S, sinks_d=None):
    nc = P.nc
    TT = S // 128
    NG = S // 512
    b_hT = C["_b_" + hT.tensor.name]
    b_mem = C["_b_" + memkv_d.tensor.name]
    b_y = C["_b_" + yT_d.tensor.name]
    with Phase(P, f"att{layer}") as ph:
        ones, b_ones = ph.sb([128, 128], BF16)
        ident, b_id = ph.sb([128, 128], BF16)
        P.dma("sp", ones[:], C["ones"][:, :], w=[b_ones])
        P.dma("sp", ident[:], C["ident"][:, :], w=[b_id])
        if layer == 0:
            mlist = list(range(M_CAUSAL, M_CAUSAL + 4))
        else:
            mlist = list(range(M_DIL, M_DIL + 20))
        masks, b_masks = ph.sb([128, len(mlist), 512], BF16)
        P.dma("sp", masks[:], C["masks"][mlist[0]:mlist[0] + len(mlist), :, :].rearrange("m p q -> p m q"),
              w=[b_masks])

        def mask_ap(mi):
            return masks[:, mi - mlist[0], :]
        NBIN = 12
        if layer == 1:
            bmd, b_bmd = ph.sb([128, NBIN, 512], BF16)
            P.op("dve", lambda e: e.tensor_scalar(bmd[:], masks[:, 0:NBIN, :], -1.0, -NEG, ALU.add, ALU.mult),
                 r=[b_masks], w=[b_bmd])
        qA = [ph.sb([128, S], BF16) for _ in range(2)]
        qB = [ph.sb([64, S], BF16) for _ in range(2)]
        kA = [ph.sb([128, S], BF16) for _ in range(2)]
        kB = [ph.sb([64, S], BF16) for _ in range(2)]
        vT = [ph.sb([128, S], BF16) for _ in range(2)]
        V = [ph.sb([128, TT, 128], BF16) for _ in range(2)]
        gt = [ph.sb([128, S], BF16) for _ in range(2)]
        yst = [ph.sb([128, S], BF16) for _ in range(2)]
        NPT = 5
        pT = [ph.sb([128, 512], BF16) for _ in range(NPT)]
        rden = [ph.sb([128, 512], F32) for _ in range(2)]
        o32 = [ph.sb([128, 512], F32) for _ in range(2)]
        st = [ph.ps([128, 512], F32) for _ in range(3)]
        oT = [ph.ps([128, 512], F32) for _ in range(2)]
        den = [ph.ps([128, 512], F32) for _ in range(2)]
        misc = [ph.ps([128, 8, 128], BF16) for _ in range(1)]
        if layer == 0:
            esel, b_esel = ph.sb([16, 16, 128], BF16)
            P.dma("sp", esel[:], C["esel"][:, :, :], w=[b_esel])
            negm, b_negm = ph.sb([128, TT, 16], F32)
            keepm, b_keepm = ph.sb([128, TT, 16], F32)
            P.dma("sp", negm[:], C["negm"][:, :, :], w=[b_negm])
            P.dma("sp", keepm[:], C["keepm"][:, :, :], w=[b_keepm])
            kmean, b_kmean = ph.sb([128, 16], F32)
            kmeanb, b_kmeanb = ph.sb([128, 16], BF16)
            P.op("dve", lambda e: e.memset(kmeanb[:], 0.0), r=[], w=[b_kmeanb])
            NBLK = S // 256
            gpad, b_gpad = ph.sb([128, TT, 16], F32)
            m8, b_m8 = ph.sb([128, TT, 8], F32)
            bias, b_bias = ph.sb([128, TT, 16], BF16)
            biasT, b_biasT = ph.sb([16, S], BF16)
        else:
            esink, b_esink = ph.sb([128, 16], F32)
            P.dma("sp", esink[:], sinks_d.partition_broadcast(128), w=[b_esink])
            P.op("act", lambda e: e.activation(out=esink[:], in_=esink[:], func=AF.Exp), r=[], w=[b_esink])
        cnt = {"h": 0, "kv": 0, "pt": 0, "st": 0, "og": 0, "misc": 0}

        def load_kv(kspecs, vspec, dv):
            s = cnt["kv"] % 2
            cnt["kv"] += 1
            ktiles = []
            for i, (kap, K, bsrc) in enumerate(kspecs):
                t, b = (kA, kB)[i][s]
                nk = kap.shape[1]
                P.dma("sp", t[:K, :nk], kap, r=[bsrc], w=[b])
                ktiles.append((t, b, K))
            vt, b_vt = vT[s]
            vap, bsrc = vspec
            nk = vap.shape[1]
            P.dma("sp", vt[:dv, :nk], vap, r=[bsrc], w=[b_vt])
            v_t, b_v = V[s]
            nkt = nk // 128
            for t0 in range(0, nkt, 8):
                m_t, b_m = misc[0]
                cnt["misc"] += 1
                nn = min(8, nkt - t0)

                def tr(e, m_t=m_t, vt=vt, t0=t0, nn=nn):
                    ins = None
                    for c in range(nn):
                        ins = e.transpose(m_t[:, c, :dv], vt[:dv, (t0 + c) * 128:(t0 + c + 1) * 128], ident[:dv, :dv])
                    return ins
                P.op("pe", tr, r=[b_vt, b_id], w=[b_m])
                P.op("dve", lambda e, m_t=m_t, v_t=v_t, t0=t0, nn=nn: e.tensor_copy(v_t[:, t0:t0 + nn, :dv],
                                                                                    m_t[:, :nn, :dv]),
                     r=[b_m], w=[b_v])
            return ktiles, (v_t, b_v)

        def attn_q(qspecs, ktiles, vtile, dv, scale, plan, gate_ap, y_ap, moba=False, sink_h=None):
            s = cnt["h"] % 2
            cnt["h"] += 1
            qtiles = []
            for i, (qap, K, bsrc) in enumerate(qspecs):
                t, b = (qA, qB)[i][s]
                P.dma("sp", t[:K, :], qap, r=[bsrc], w=[b])
                qtiles.append((t, b, K))
            g_t, b_g = gt[s]
            P.dma("sp", g_t[:dv, :], gate_ap, r=[b_hT], w=[b_g])
            y_t, b_yt = yst[s]
            v_t, b_v = vtile
            if moba:
                k_t, b_k, _ = ktiles[0]
                q_t, b_q, _ = qtiles[0]
                P.op("dve", lambda e: e.tensor_reduce(kmean[:, :NBLK], k_t[:, :].rearrange("p (n l) -> p n l", l=256),
                                                      mybir.AxisListType.X, ALU.add), r=[b_k], w=[b_kmean])
                P.op("dve", lambda e: e.tensor_scalar(kmeanb[:, :NBLK], kmean[:, :NBLK], 1.0 / 256.0, None, ALU.mult),
                     r=[b_kmean], w=[b_kmeanb])
                gp, b_gp = st[cnt["st"] % 3]
                cnt["st"] += 1

                def gm(e):
                    ins = None
                    for t in range(TT):
                        ins = e.matmul(gp[:, t * 16:(t + 1) * 16], q_t[:, t * 128:(t + 1) * 128], kmeanb[:],
                                       start=True, stop=True)
                    return ins
                P.op("pe", gm, r=[b_q, b_kmeanb], w=[b_gp])
                P.op("dve", lambda e: e.tensor_tensor(gpad[:], gp[:, :TT * 16].rearrange("p (t n) -> p t n", n=16),
                                                      negm[:], ALU.add), r=[b_gp, b_negm], w=[b_gpad])
                for t in range(TT):
                    P.op("dve", lambda e, t=t: e.max(out=m8[:, t, :], in_=gpad[:, t, :]), r=[b_gpad],
                         w=[b_m8] if t in (0, TT - 1) else [])
                P.op("dve", lambda e: e.tensor_tensor(bias[:], gpad[:], m8[:, :, 2:3].to_broadcast([128, TT, 16]),
                                                      ALU.is_ge), r=[b_gpad, b_m8], w=[b_bias])
                P.op("dve", lambda e: e.tensor_scalar(bias[:], bias[:], -NEG, NEG, ALU.mult, ALU.add),
                     r=[b_bias], w=[b_bias])
                P.op("dve", lambda e: e.tensor_tensor(bias[:], bias[:], keepm[:], ALU.mult),
                     r=[b_bias, b_keepm], w=[b_bias])
                for t0 in range(0, TT, 8):
                    m_t, b_m = misc[0]
                    cnt["misc"] += 1
                    nn = min(8, TT - t0)

                    def tr(e, m_t=m_t, t0=t0, nn=nn):
                        ins = None
                        for c in range(nn):
                            ins = e.transpose(m_t[:16, c, :], bias[:, t0 + c, :], ident[:])
                        return ins
                    P.op("pe", tr, r=[b_bias, b_id], w=[b_m])
                    P.op("dve", lambda e, m_t=m_t, t0=t0, nn=nn: e.tensor_copy(
                        biasT[:, t0 * 128:(t0 + nn) * 128].rearrange("p (c q) -> p c q", q=128), m_t[:16, :nn, :]),
                        r=[b_m], w=[b_biasT])
            LAG = 2
            items = []
            for g in range(NG):
                tiles = plan(g)
                og = cnt["og"] % 2
                cnt["og"] += 1
                for i, (kt, mi, brow) in enumerate(tiles):
                    items.append((g, og, kt, mi, brow, i == 0, i == len(tiles) - 1))

            def finalize(g, og):
                o_t, b_o = oT[og]
                d_t, b_d = den[og]
                q0 = g * 512
                r_t, b_r = rden[og]
                o3, b_o3 = o32[og]
                if sink_h is not None:
                    P.op("dve", lambda e: e.tensor_scalar(r_t[:dv, :], d_t[:dv, :], esink[:dv, sink_h:sink_h + 1],
                                                          None, ALU.add), r=[b_d, b_esink], w=[b_r])
                    P.op("dve", lambda e: e.reciprocal(r_t[:dv, :], r_t[:dv, :]), r=[b_r], w=[b_r])
                    P.op("dve", lambda e: e.tensor_tensor(o3[:dv, :], o_t[:dv, :], r_t[:dv, :], ALU.mult),
                         r=[b_o, b_r], w=[b_o3])
                else:
                    P.op("act", lambda e: e.activation(out=r_t[:dv, :], in_=d_t[:dv, :], func=AF.Ln),
                         r=[b_d], w=[b_r])
                    P.op("act", lambda e: e.activation(out=r_t[:dv, :], in_=r_t[:dv, :], func=AF.Exp, scale=-1.0),
                         r=[b_r], w=[b_r])
                    P.op("dve", lambda e: e.tensor_tensor(o3[:dv, :], o_t[:dv, :], r_t[:dv, :], ALU.mult),
                         r=[b_o, b_r], w=[b_o3])
                P.op("pool", lambda e: e.tensor_tensor(y_t[:dv, q0:q0 + 512], o3[:dv, :], g_t[:dv, q0:q0 + 512],
                                                       ALU.mult), r=[b_o3, b_g], w=[b_yt])

            def pv(item, p_t, b_p):
                g, og, kt, mi, brow, first, last = item
                o_t, b_o = oT[og]
                d_t, b_d = den[og]

                def f(e):
                    e.matmul(o_t[:dv, :], v_t[:, kt, :dv], p_t[:], start=first, stop=last)
                    return e.matmul(d_t[:dv, :], ones[:, :dv], p_t[:], start=first, stop=last)
                P.op("pe", f, r=[b_v, b_p, b_ones], w=[b_o, b_d])
                if last:
                    finalize(g, og)

            inflight = []
            for item in items:
                g, og, kt, mi, brow, first, last = item
                q0 = g * 512
                s_t, b_s = st[cnt["st"] % 3]
                cnt["st"] += 1

                pe_mask = (layer == 1 and mi is not None and M_DIL <= mi < M_DIL + NBIN)

                def qk(e):
                    ins = None
                    nparts = len(ktiles)
                    for pi in range(nparts):
                        k_t, _, K = ktiles[pi]
                        q_t, _, _ = qtiles[pi]
                        ins = e.matmul(s_t[:], k_t[:K, kt * 128:(kt + 1) * 128], q_t[:K, q0:q0 + 512],
                                       start=(pi == 0), stop=(pi == nparts - 1 and brow is None and not pe_mask))
                    if brow is not None:
                        ins = e.matmul(s_t[:], esel[:, brow, :], biasT[:, q0:q0 + 512], start=False, stop=True)
                    if pe_mask:
                        ins = e.matmul(s_t[:], ident[:, :], bmd[:, mi - M_DIL, :], start=False, stop=True)
                    return ins
                rr = [b for (_, b, _) in ktiles] + [b for (_, b, _) in qtiles]
                if brow is not None:
                    rr += [b_esel, b_biasT]
                if pe_mask:
                    rr += [b_id, b_bmd]
                P.op("pe", qk, r=rr, w=[b_s])
                p_t, b_p = pT[cnt["pt"] % NPT]
                cnt["pt"] += 1
                P.op("act", lambda e: e.activation(out=p_t[:], in_=s_t[:], func=AF.Exp, scale=float(scale)),
                     r=[b_s], w=[b_p])
                if mi is not None and not pe_mask:
                    P.op("dve", lambda e: e.tensor_tensor(p_t[:], p_t[:], mask_ap(mi), ALU.mult),
                         r=[b_p, b_masks], w=[b_p])
                inflight.append((item, p_t, b_p))
                if len(inflight) > LAG:
                    pv(*inflight.pop(0))
            while inflight:
                pv(*inflight.pop(0))
            P.dma("pool", y_ap, y_t[:dv, :], r=[b_yt], w=[b_y])

        def rows(name, r0, n):
            return hT[R[name] + r0:R[name] + r0 + n, :]

        def plan_causal(g):
            return [(kt, (M_CAUSAL + kt - 4 * g) if kt >= 4 * g else None, None) for kt in range(4 * g + 4)]

        def plan_moba(g):
            return [(kt, (M_CAUSAL + kt - 4 * g) if kt >= 4 * g else None, (kt // 2) if kt < 4 * g + 2 else None)
                    for kt in range(4 * g + 4)]

        def plan_mem(g):
            return [(0, None, None), (1, None, None)]

        def plan_swa(g):
            return [(4 * g - 1 + j, M_SWA + j, None) for j in range(5) if 4 * g - 1 + j >= 0]

        def plan_dil(g):
            return [(4 * g - 16 + j, M_DIL + j, None) for j in range(20) if 4 * g - 16 + j >= 0]

        gate0 = R["gate"]
        if layer == 0:
            for h in range(8):
                kt_, vt_ = load_kv([(rows("kn", h * 128, 128), 128, b_hT), (rows("kpe", 0, 64), 64, b_hT)],
                                   (rows("va", h * 128, 128), b_hT), 128)
                attn_q([(rows("qn", h * 128, 128), 128, b_hT), (rows("qr", h * 64, 64), 64, b_hT)], kt_, vt_, 128,
                       192 ** -0.5, plan_causal, hT[gate0 + h * 128:gate0 + (h + 1) * 128, :],
                       yT_d[h * 128:(h + 1) * 128, :])
            for h in range(8):
                kt_, vt_ = load_kv([(rows("bk", h * 128, 128), 128, b_hT)], (rows("bv", h * 128, 128), b_hT), 128)
                c = 8 + h
                attn_q([(rows("bq", h * 128, 128), 128, b_hT)], kt_, vt_, 128, 128 ** -0.5, plan_moba,
                       hT[gate0 + c * 128:gate0 + (c + 1) * 128, :], yT_d[c * 128:(c + 1) * 128, :], moba=True)
            ybase = 16
        else:
            for h in range(6):
                kt_, vt_ = load_kv([(rows("dk", h * 128, 128), 128, b_hT)], (rows("dv", h * 128, 128), b_hT), 128)
                c = 8 + h
                attn_q([(rows("dq", h * 128, 128), 128, b_hT)], kt_, vt_, 128, 128 ** -0.5, plan_dil,
                       hT[gate0 + c * 128:gate0 + (c + 1) * 128, :], yT_d[c * 128:(c + 1) * 128, :])
            ybase = 14
        for h in range(4):
            kt_, vt_ = load_kv([(memkv_d[h * 128:(h + 1) * 128, :], 128, b_mem)],
                               (memkv_d[512 + h * 128:512 + (h + 1) * 128, :], b_mem), 128)
            c = ybase + h
            attn_q([(rows("mq", h * 128, 128), 128, b_hT)], kt_, vt_, 128, 128 ** -0.5, plan_mem,
                   hT[gate0 + c * 128:gate0 + (c + 1) * 128, :], yT_d[c * 128:(c + 1) * 128, :])


def phase_swa(P, C, hT, R, yT_d, S, sinks_d):
    TT = S // 128
    b_hT = C["_b_" + hT.tensor.name]
    b_y = C["_b_" + yT_d.tensor.name]
    gate0 = R["gate"]
    scale = 64 ** -0.5
    with Phase(P, "swa") as ph:
        ones, b_ones = ph.sb([128, 128], BF16)
        ident, b_id = ph.sb([128, 128], BF16)
        P.dma("sp", ones[:], C["ones"][:, :], w=[b_ones])
        P.dma("sp", ident[:], C["ident"][:, :], w=[b_id])
        masks, b_masks = ph.sb([128, 2, 512], BF16)
        P.dma("sp", masks[:], C["masks"][M_SWA2:M_SWA2 + 2, :, :].rearrange("m p q -> p m q"), w=[b_masks])
        esink, b_esink = ph.sb([128, 16], F32)
        P.dma("sp", esink[:], sinks_d.partition_broadcast(128), w=[b_esink])
        P.op("act", lambda e: e.activation(out=esink[:], in_=esink[:], func=AF.Exp), r=[], w=[b_esink])
        bm, b_bm = ph.sb([128, 2, 512], BF16)
        P.op("dve", lambda e: e.tensor_scalar(bm[:], masks[:], -1.0, -NEG, ALU.add, ALU.mult),
             r=[b_masks], w=[b_bm])
        sel1, b_sel1 = ph.sb([16, 64], BF16)
        P.dma("sp", sel1[:], C["esel"][:, 0, 0:64], w=[b_sel1])
        srow, b_srow = ph.sb([16, 4, 128], BF16)
        P.op("dve", lambda e: e.memset(srow[:], 0.0), r=[], w=[b_srow])
        q4 = [ph.sb([64, 4, S], BF16) for _ in range(2)]
        g4 = [ph.sb([64, 4, S], BF16) for _ in range(1)]
        y4 = [ph.sb([64, 4, S], BF16) for _ in range(1)]
        kk = [ph.sb([64, S], BF16) for _ in range(2)]
        vT = [ph.sb([64, S], BF16) for _ in range(2)]
        V = [ph.sb([128, TT, 64], BF16) for _ in range(2)]
        pT = [ph.sb([128, 512], BF16) for _ in range(5)]
        rden = [ph.sb([64, 512], F32) for _ in range(2)]
        o32 = [ph.sb([64, 512], F32) for _ in range(2)]
        st = [ph.ps([128, 512], F32) for _ in range(3)]
        oT = [ph.ps([128, 512], F32) for _ in range(2)]
        den = [ph.ps([128, 512], F32) for _ in range(2)]
        misc = ph.ps([128, 8, 128], BF16)
        n_st = 0
        n_pt = 0
        n_og = 0
        it = 0
        for kvh in range(2):
            k_t, b_k = kk[kvh % 2]
            vt, b_vt = vT[kvh % 2]
            v_t, b_v = V[kvh % 2]
            P.dma("sp", k_t[:], hT[R["ck"] + kvh * 64:R["ck"] + (kvh + 1) * 64, :], r=[b_hT], w=[b_k])
            P.dma("sp", vt[:], hT[R["cv"] + kvh * 64:R["cv"] + (kvh + 1) * 64, :], r=[b_hT], w=[b_vt])
            for t0 in range(0, TT, 8):
                m_t, b_m = misc

                def tr(e, m_t=m_t, vt=vt, t0=t0):
                    ins = None
                    for c in range(8):
                        ins = e.transpose(m_t[:, c, :64], vt[:, (t0 + c) * 128:(t0 + c + 1) * 128], ident[:64, :64])
                    return ins
                P.op("pe", tr, r=[b_vt, b_id], w=[b_m])
                P.op("dve", lambda e, m_t=m_t, v_t=v_t, t0=t0: e.tensor_copy(v_t[:, t0:t0 + 8, :], m_t[:, :8, :64]),
                     r=[b_m], w=[b_v])
            for hg in range(2):
                h0 = kvh * 8 + hg * 4
                q_t, b_q = q4[it % 2]
                g_t, b_g = g4[0]
                y_t, b_yt = y4[0]
                it += 1
                P.dma("sp", q_t[:], hT[R["cq"] + h0 * 64:R["cq"] + (h0 + 4) * 64, :].rearrange("(h p) s -> p h s", p=64),
                      r=[b_hT], w=[b_q])
                P.dma("sp", g_t[:], hT[gate0 + h0 * 64:gate0 + (h0 + 4) * 64, :].rearrange("(h p) s -> p h s", p=64),
                      r=[b_hT], w=[b_g])
                P.op("dve", lambda e: e.tensor_copy(srow[0:1, :, :],
                                                    esink[0:1, h0:h0 + 4].unsqueeze(2).to_broadcast([1, 4, 128])),
                     r=[b_esink], w=[b_srow])
                items = []
                for qt in range(TT):
                    og = n_og % 2
                    n_og += 1
                    tiles = [(qt - 1, 0), (qt, 1)] if qt > 0 else [(qt, 1)]
                    for i, (kt, mi) in enumerate(tiles):
                        items.append((qt, og, kt, mi, i == 0, i == len(tiles) - 1))

                def finalize(qt, og):
                    o_t, b_o = oT[og]
                    d_t, b_d = den[og]
                    r_t, b_r = rden[og]
                    o3, b_o3 = o32[og]
                    P.op("act", lambda e: e.activation(out=r_t[:], in_=d_t[:64, :], func=AF.Ln), r=[b_d], w=[b_r])
                    P.op("act", lambda e: e.activation(out=r_t[:], in_=r_t[:], func=AF.Exp, scale=-1.0),
                         r=[b_r], w=[b_r])
                    P.op("dve", lambda e: e.tensor_tensor(o3[:], o_t[:64, :], r_t[:], ALU.mult), r=[b_o, b_r], w=[b_o3])
                    P.op("pool", lambda e: e.tensor_tensor(
                        y_t[:, :, qt * 128:(qt + 1) * 128], o3[:].rearrange("p (h q) -> p h q", h=4),
                        g_t[:, :, qt * 128:(qt + 1) * 128], ALU.mult), r=[b_o3, b_g], w=[b_yt])

                def pv(item, p_t, b_p):
                    qt, og, kt, mi, first, last = item
                    o_t, b_o = oT[og]
                    d_t, b_d = den[og]

                    def f(e):
                        e.matmul(o_t[:64, :], v_t[:, kt, :], p_t[:], start=first, stop=last)
                        ins = e.matmul(d_t[:64, :], ones[:, :64], p_t[:], start=first, stop=False)
                        if last:
                            ins = e.matmul(d_t[:64, :], sel1[:, :], srow[:].rearrange("p h q -> p (h q)"),
                                           start=False, stop=True)
                        return ins
                    P.op("pe", f, r=[b_v, b_p, b_ones, b_sel1, b_srow], w=[b_o, b_d])
                    if last:
                        finalize(qt, og)

                inflight = []
                for item in items:
                    qt, og, kt, mi, first, last = item
                    s_t, b_s = st[n_st % 3]
                    n_st += 1
                    def qk(e):
                        e.matmul(s_t[:], k_t[:, kt * 128:(kt + 1) * 128], q_t[:, :, qt * 128:(qt + 1) * 128],
                                 start=True, stop=False)
                        return e.matmul(s_t[:], ident[:, :], bm[:, mi, :], start=False, stop=True)
                    P.op("pe", qk, r=[b_k, b_q, b_id, b_bm], w=[b_s])
                    p_t, b_p = pT[n_pt % 5]
                    n_pt += 1
                    P.op("act", lambda e: e.activation(out=p_t[:], in_=s_t[:], func=AF.Exp, scale=float(scale)),
                         r=[b_s], w=[b_p])
                    inflight.append((item, p_t, b_p))
                    if len(inflight) > 2:
                        pv(*inflight.pop(0))
                while inflight:
                    pv(*inflight.pop(0))
                P.dma("pool", yT_d[h0 * 64:(h0 + 4) * 64, :].rearrange("(h p) s -> p h s", p=64), y_t[:],
                      r=[b_yt], w=[b_y])


def phase_out(P, C, layer, yT_d, wout_d, NCH, xin_d, xout_d, gname, xnT_d, S, final):
    nc = P.nc
    NT = S // 128
    GT = 4
    b_y = C["_b_" + yT_d.tensor.name]
    b_xin = C["_b_" + xin_d.tensor.name] if ("_b_" + xin_d.tensor.name) in C else None
    with Phase(P, f"out{layer}") as ph:
        wo, b_wo = ph.sb([128, NCH, D], BF16)
        b_wos = [P.buf() for _ in range(NCH)]
        wst = [ph.sb([128, D], F32) for _ in range(2)]
        for c in range(NCH):
            w_t, b_w = wst[c % 2]
            P.dma("sp", w_t[:], wout_d[c * 128:(c + 1) * 128, :], w=[b_w])
            if c % 2:
                P.op("act", lambda e, w_t=w_t, c=c: e.activation(out=wo[:, c, :], in_=w_t[:], func=AF.Copy),
                     r=[b_w], w=[b_wos[c]])
            else:
                P.op("dve", lambda e, w_t=w_t, c=c: e.tensor_copy(wo[:, c, :], w_t[:]), r=[b_w], w=[b_wos[c]])
        ident, b_id = ph.sb([128, 128], BF16)
        P.dma("sp", ident[:], C["ident"][:, :], w=[b_id])
        if final:
            gfin, b_gf = ph.sb([128, D], F32)
            P.dma("sp", gfin[:], C["gfin"].partition_broadcast(128), w=[b_gf])
        else:
            gT, b_g = ph.sb([128, KC], F32)
            P.dma("sp", gT[:], C[gname][:, :], w=[b_g])
        yt = [ph.sb([128, NCH, 128 * GT], BF16) for _ in range(2)]
        xt = [ph.sb([128, D], F32) for _ in range(2)]
        junk = ph.sb([128, D], BF16)
        ss = [ph.sb([128, 1], F32) for _ in range(2)]
        rs = [ph.sb([128, 1], F32) for _ in range(2)]
        acc = [ph.ps([128, 512], F32) for _ in range(4)]
        if not final:
            xs = [ph.sb([128, D], BF16) for _ in range(2)]
            stg = [ph.sb([128, KC, 128 * GT], BF16) for _ in range(2)]
            pst = [ph.ps([128, 8, 128], BF16) for _ in range(2)]
        else:
            ot = [ph.sb([128, D], F32) for _ in range(2)]
        na = 0
        for t in range(NT):
            s = t % 2
            gi = (t // GT) % 2
            y_t, b_yt = yt[gi]
            if t % GT == 0:
                P.dma("sp", y_t[:], yT_d.rearrange("(c p) s -> p c s", p=128)[:, :, t * 128:(t + GT) * 128],
                      r=[b_y], w=[b_yt])
            off = (t % GT) * 128
            x_t, b_x = xt[s]
            P.dma("sp", x_t[:], xin_d[t * 128:(t + 1) * 128, :], r=[b_xin] if b_xin else [], w=[b_x])
            x1_t, b_x1 = x_t, b_x
            for nq in range(4):
                a_t, b_a = acc[na % 4]
                na += 1

                def mm(e, a_t=a_t, y_t=y_t, off=off, nq=nq):
                    ins = None
                    for c in range(NCH):
                        ins = e.matmul(a_t[:], y_t[:, c, off:off + 128], wo[:, c, nq * 512:(nq + 1) * 512],
                                       start=(c == 0), stop=(c == NCH - 1))
                    return ins
                P.op("pe", mm, r=[b_yt] + b_wos, w=[b_a])
                P.op("dve", lambda e, a_t=a_t, x_t=x_t, x1_t=x1_t, nq=nq: e.tensor_tensor(
                    x1_t[:, nq * 512:(nq + 1) * 512], a_t[:], x_t[:, nq * 512:(nq + 1) * 512], ALU.add),
                    r=[b_a, b_x], w=[b_x1])
            P.op("act", lambda e, x1_t=x1_t, s=s: e.activation(out=junk[0][:], in_=x1_t[:], func=AF.Square,
                                                               accum_out=ss[s][0][:]),
                 r=[b_x1], w=[junk[1], ss[s][1]])
            P.op("dve", lambda e, s=s: e.tensor_scalar(rs[s][0][:], ss[s][0][:], 1.0 / D, EPS, ALU.mult, ALU.add),
                 r=[ss[s][1]], w=[rs[s][1]])
            P.op("act", lambda e, s=s: e.activation(out=rs[s][0][:], in_=rs[s][0][:], func=AF.Sqrt),
                 r=[rs[s][1]], w=[rs[s][1]])
            P.op("dve", lambda e, s=s: e.reciprocal(rs[s][0][:], rs[s][0][:]), r=[rs[s][1]], w=[rs[s][1]])
            if final:
                o_t, b_o = ot[s]
                P.op("dve", lambda e, o_t=o_t, x1_t=x1_t, s=s: e.scalar_tensor_tensor(
                    o_t[:], x1_t[:], rs[s][0][:, 0:1], gfin[:], ALU.mult, ALU.mult),
                    r=[b_x1, rs[s][1], b_gf], w=[b_o])
                P.dma("pool", xout_d[t * 128:(t + 1) * 128, :], o_t[:], r=[b_o], w=[C["_b_out"]])
            else:
                P.dma("pool", xout_d[t * 128:(t + 1) * 128, :], x1_t[:], r=[b_x1], w=[C["_b_" + xout_d.tensor.name]])
                P.op("act", lambda e, x1_t=x1_t, s=s: e.activation(out=xs[s][0][:], in_=x1_t[:], func=AF.Copy,
                                                                  scale=rs[s][0][:]),
                     r=[b_x1, rs[s][1]], w=[xs[s][1]])
                sg, b_sg = stg[gi]
                for hh in range(2):
                    pt, b_pt = pst[hh]

                    def tr(e, pt=pt, s=s, hh=hh):
                        ins = None
                        for c in range(8):
                            ins = e.transpose(pt[:, c, :], xs[s][0][:, (hh * 8 + c) * 128:(hh * 8 + c + 1) * 128],
                                              ident[:])
                        return ins
                    P.op("pe", tr, r=[xs[s][1], b_id], w=[b_pt])
                    P.op("dve", lambda e, pt=pt, sg=sg, hh=hh, off=off: e.tensor_tensor(
                        sg[:, hh * 8:(hh + 1) * 8, off:off + 128], pt[:],
                        gT[:, hh * 8:(hh + 1) * 8].unsqueeze(2).to_broadcast([128, 8, 128]), ALU.mult),
                        r=[b_pt, b_g], w=[b_sg])
                if t % GT == GT - 1:
                    t0 = (t // GT) * GT * 128
                    P.dma("pool", xnT_d.rearrange("(c p) s -> p c s", p=128)[:, :, t0:t0 + 128 * GT], sg[:],
                          r=[b_sg], w=[C["_b_" + xnT_d.tensor.name]])


IN_SPECS = None


def build_program(S, stop_after=None, debug_outs=()):
    nc = bass.Bass("TRN2", target_bir_lowering=False)
    P = Prog(nc)
    C = {}

    def din(name, shape, dt=F32):
        C[name] = nc.dram_tensor(name, list(shape), dt, kind="ExternalInput").ap()
        return C[name]

    def scratch(name, shape, dt):
        kind = "ExternalOutput" if name in debug_outs else "Internal"
        t = nc.dram_tensor(name, list(shape), dt, kind=kind).ap()
        C["_b_" + name] = P.buf(name)
        return t
    x = din("x", [S, D])
    mem = din("mem", [NMEM, D])
    din("g0", [128, KC]); din("g1", [128, KC]); din("gm0", [128, KC]); din("gm1", [128, KC])
    din("gq", [128, 4]); din("gkv", [128, 2]); din("gfin", [D])
    w_in0 = din("w_in0", [D, 6976]); w_in1 = din("w_in1", [D, 6400])
    wuq = din("wuq", [512, 1536]); wukv = din("wukv", [256, 2048])
    wm0 = din("wm0", [D, 1024]); wm1 = din("wm1", [D, 1024])
    wo0 = din("wo0", [2560, D]); wo1 = din("wo1", [2304, D])
    sinks = din("sinks", [16])
    din("ident", [128, 128], BF16); din("ones", [128, 128], BF16); din("esel", [16, 16, 128], BF16)
    din("perm", [3, 128, 128], BF16); din("rope", [3, 2, 128, S]); din("masks", [NMASK, 128, 512], BF16)
    din("negm", [128, S // 128, 16]); din("keepm", [128, S // 128, 16])
    out = nc.dram_tensor("out", [S, D], F32, kind="ExternalOutput").ap()
    C["_b_out"] = P.buf("out")

    xnT = scratch("xnT", [D, S], BF16)
    memnT = scratch("memnT", [D, NMEM], BF16)
    memkv0 = scratch("memkv0", [1024, NMEM], BF16)
    memkv1 = scratch("memkv1", [1024, NMEM], BF16)
    lat = scratch("lat", [768, S], F32)
    hT0 = scratch("hT0", [L0_NROWS, S], BF16)
    hT1 = scratch("hT1", [L1_NROWS, S], BF16)
    yT0 = scratch("yT0", [2560, S], BF16)
    yT1 = scratch("yT1", [2304, S], BF16)
    x1 = scratch("x1", [S, D], F32)

    def jobs_for(hT, R, spec):
        jobs = []
        for (name, c0, ncols, kind, tab, dst) in spec:
            for j in range(0, ncols, 128):
                n = min(128, ncols - j)
                if dst is None:
                    d = hT[R[name] + j:R[name] + j + 128, :] if n == 128 else hT[R[name] + j:R[name] + j + n, :]
                else:
                    d = dst[j:j + n, :]
                jobs.append(dict(c0=c0 + j, n=n, kind=kind, tab=tab, dst=d))
        return jobs

    def done():
        P.barrier()
        return nc

    memjobs = lambda dst: [dict(c0=j, n=128, kind="plain", tab=None, dst=dst[j:j + 128, :]) for j in range(0, 1024, 128)]
    phase_norm(P, C, mem, "gm0", memnT, NMEM, "m0")
    phase_inproj(P, C, memnT, wm0, memjobs(memkv0), NMEM, "m0")
    phase_norm(P, C, mem, "gm1", memnT, NMEM, "m1")
    phase_inproj(P, C, memnT, wm1, memjobs(memkv1), NMEM, "m1")
    phase_norm(P, C, x, "g0", xnT, S, "x0")
    if stop_after == "norm0":
        return done()
    spec0 = [("lat", 0, 768, "lat", None, lat), ("kpe", 768, 64, "rope", T_MLA, None),
             ("bq", 832, 1024, "rope", T_MOBA, None), ("bk", 1856, 1024, "rope", T_MOBA, None),
             ("bv", 2880, 1024, "plain", None, None), ("mq", 3904, 512, "plain", None, None),
             ("gate", 4416, 2560, "silu", None, None)]
    phase_inproj(P, C, xnT, w_in0, jobs_for(hT0, L0_ROWS, spec0), S, "l0")
    phase_mla2(P, C, lat, hT0, L0_ROWS, wuq, wukv, S)
    if stop_after == "inproj0":
        return done()
    phase_attn(P, C, 0, hT0, L0_ROWS, memkv0, yT0, S)
    if stop_after == "attn0":
        return done()
    phase_out(P, C, 0, yT0, wo0, 20, x, x1, "g1", xnT, S, final=False)
    if stop_after == "out0":
        return done()
    spec1 = [("cq", 0, 1024, "rope", T_SWA, None), ("ck", 1024, 128, "rope", T_SWA, None),
             ("cv", 1152, 128, "plain", None, None), ("dq", 1280, 768, "rope", T_MOBA, None),
             ("dk", 2048, 768, "rope", T_MOBA, None), ("dv", 2816, 768, "plain", None, None),
             ("mq", 3584, 512, "plain", None, None), ("gate", 4096, 2304, "silu", None, None)]
    phase_inproj(P, C, xnT, w_in1, jobs_for(hT1, L1_ROWS, spec1), S, "l1")
    if stop_after == "inproj1":
        return done()
    phase_swa(P, C, hT1, L1_ROWS, yT1, S, sinks)
    phase_attn(P, C, 1, hT1, L1_ROWS, memkv1, yT1, S, sinks_d=sinks)
    if stop_after == "attn1":
        return done()
    phase_out(P, C, 1, yT1, wo1, 18, x1, out, None, None, S, final=True)
    return done()


def make_in_maps(inputs, S, nb):
    f = lambda a: np.ascontiguousarray(np.asarray(a, dtype=np.float32))
    gl = lambda g: np.ascontiguousarray(f(g).reshape(-1, 128).T)
    negm, keepm = _moba_consts(S)
    esel = np.zeros((16, 16, 128), np.float32)
    for n in range(16):
        esel[n, n, :] = 1.0
    common = {
        "g0": gl(inputs["ev_norm_g"][0]), "g1": gl(inputs["od_norm_g"][0]),
        "gm0": gl(inputs["ev_mem_norm_g"][0]), "gm1": gl(inputs["od_mem_norm_g"][0]),
        "gq": gl(inputs["ev_q_norm_g"][0]), "gkv": gl(inputs["ev_kv_norm_g"][0]),
        "gfin": f(inputs["final_norm_g"]),
        "w_in0": f(inputs["ev_w_in"][0]), "w_in1": f(inputs["od_w_in"][0]),
        "wuq": f(inputs["ev_w_uq"][0]), "wukv": f(inputs["ev_w_ukv"][0]),
        "wm0": f(inputs["ev_w_mem_kv"][0]), "wm1": f(inputs["od_w_mem_kv"][0]),
        "wo0": f(inputs["ev_w_out"][0]), "wo1": f(inputs["od_w_out"][0]),
        "sinks": f(inputs["od_sinks"][0]),
        "ident": np.eye(128, dtype=np.float32).astype(ml_dtypes.bfloat16),
        "ones": np.ones((128, 128), np.float32).astype(ml_dtypes.bfloat16),
        "esel": esel.astype(ml_dtypes.bfloat16),
        "perm": _perms().astype(ml_dtypes.bfloat16), "rope": _rope_tables(S), "masks": _masks(), "negm": negm, "keepm": keepm,
    }
    maps = []
    for b in range(nb):
        m = dict(common)
        m["x"] = f(inputs["x"][b][:S])
        m["mem"] = f(inputs["mem"][b])
        maps.append(m)
    return maps


def kernel(**inputs):
    B, S = inputs["x"].shape[0], inputs["x"].shape[1]
    nc = build_program(S)
    maps = make_in_maps(inputs, S, B)
    res = run_bass_kernel_spmd(nc, maps, core_ids=list(range(B)))
    return np.stack([np.asarray(res.results[b]["out"], dtype=np.float32) for b in range(B)], axis=0)
```
